# Optimizing a Trainium2 kernel written in Bass

```python
import jax, jax.numpy as jnp
from jax import lax
import numpy as np

D_MODEL = 1024
BATCH = 16
SEQ = 4096
DEPTH = 2
DEC_BATCH = 8
DEC_SEQ = 16
PAST_LEN = 2048

CHUNK = 64
Q_BLOCK = 128
N_EVEN = (DEPTH + 1) // 2
N_ODD = DEPTH // 2
FFN_DIM = 2816
MLA_HEADS = 8
MLA_NOPE = 64
MLA_ROPE = 32
MLA_QK = MLA_NOPE + MLA_ROPE
MLA_V = 64
MLA_Q_LORA = 256
MLA_KV_LORA = 128
ROPE_THETA = 10000.0
LRU_WIDTH = 512
LRU_BLOCKS = 8
LRU_BLOCK = LRU_WIDTH // LRU_BLOCKS
LRU_C = 8.0
CONV_WIDTH = 4
EVEN_IN = MLA_Q_LORA + MLA_KV_LORA + MLA_ROPE + 2 * LRU_WIDTH
EVEN_MIX = MLA_HEADS * MLA_V + LRU_WIDTH
FOX_HEADS = 16
FOX_HEAD_DIM = 64
FOX_WIDTH = FOX_HEADS * FOX_HEAD_DIM
ODD_IN = 3 * FOX_WIDTH + FOX_HEADS
MEM_TOKENS = 256
MEM_HEADS = 4
MEM_HEAD_DIM = 128
MEM_WIDTH = MEM_HEADS * MEM_HEAD_DIM
NORM_EPS = 1e-6
NEG_INF = -1e30

kernel_name = 'streaming_mla_rglru_fox_macaron_step'


def _rms_norm(x, g):
    xf = x.astype(jnp.float32)
    y = xf * lax.rsqrt(jnp.mean(xf * xf, axis=-1, keepdims=True) + NORM_EPS)
    return (y * g.astype(jnp.float32)).astype(x.dtype)


def _swiglu(x, w_in, w_out):
    gate, up = jnp.split(x @ w_in, 2, axis=-1)
    return (jax.nn.silu(gate) * up) @ w_out


def _rope(x, pos):
    half = x.shape[-1] // 2
    inv_freq = ROPE_THETA ** (-jnp.arange(half, dtype=jnp.float32) / half)
    ang = pos.astype(jnp.float32)[:, None] * inv_freq[None, :]
    ang = ang.reshape(ang.shape[:1] + (1,) * (x.ndim - 3) + (half,))
    cos, sin = jnp.cos(ang), jnp.sin(ang)
    xf = x.astype(jnp.float32)
    x1, x2 = xf[..., :half], xf[..., half:]
    return jnp.concatenate([x1 * cos - x2 * sin, x2 * cos + x1 * sin], axis=-1).astype(x.dtype)


def _attend_block(q, k, v, mask, bias):
    s = jnp.einsum('bqhd,bkhd->bhqk', q, k).astype(jnp.float32) * (q.shape[-1] ** -0.5)
    if bias is not None:
        s = s + bias
    if mask is not None:
        s = jnp.where(mask, s, NEG_INF)
    p = jax.nn.softmax(s, axis=-1)
    return jnp.einsum('bhqk,bkhd->bqhd', p.astype(v.dtype), v)


def _attention(q, k, v, q_pos, k_pos, chunk_causal, c_q=None, c_k=None):
    def block(qb, qp, cqb):
        if chunk_causal:
            mask = (k_pos[None, :] // CHUNK) <= (qp[:, None] // CHUNK)
        else:
            mask = k_pos[None, :] <= qp[:, None]
        bias = None
        if cqb is not None:
            bias = jnp.swapaxes(cqb, 1, 2)[..., :, None] - jnp.swapaxes(c_k, 1, 2)[..., None, :]
        return _attend_block(qb, k, v, mask, bias)

    B, Tq = q.shape[0], q.shape[1]
    if Tq <= Q_BLOCK:
        return block(q, q_pos, c_q)
    nb = Tq // Q_BLOCK
    to_blocks = lambda a: jnp.swapaxes(a.reshape((B, nb, Q_BLOCK) + a.shape[2:]), 0, 1)
    pos_b = q_pos.reshape(nb, Q_BLOCK)
    if c_q is None:
        out = lax.map(lambda xs: block(xs[0], xs[1], None), (to_blocks(q), pos_b))
    else:
        out = lax.map(lambda xs: block(xs[0], xs[1], xs[2]), (to_blocks(q), pos_b, to_blocks(c_q)))
    return jnp.swapaxes(out, 0, 1).reshape((B, Tq) + out.shape[3:])


def _linear_combine(e1, e2):
    a1, b1 = e1
    a2, b2 = e2
    return a1 * a2, a2 * b1 + b2


def _even_mixer(h, pos, w_in, g_qlat, g_kvlat, w_uq, w_ukv, g_q, g_k, conv_w, conv_b,
                gate_w, gate_b, lam, w_out, past_latent, past_krope, lru_h0, conv_prev):
    B, T, _ = h.shape
    z = h @ w_in
    c_q, c_kv, k_rope, x_rec, x_gate = jnp.split(
        z, [MLA_Q_LORA, MLA_Q_LORA + MLA_KV_LORA, MLA_Q_LORA + MLA_KV_LORA + MLA_ROPE,
            MLA_Q_LORA + MLA_KV_LORA + MLA_ROPE + LRU_WIDTH], axis=-1)
    q = (_rms_norm(c_q, g_qlat) @ w_uq).reshape(B, T, MLA_HEADS, MLA_QK)
    q = _rms_norm(jnp.concatenate([q[..., :MLA_NOPE], _rope(q[..., MLA_NOPE:], pos)], axis=-1), g_q)
    latent_new = _rms_norm(c_kv, g_kvlat)
    krope_new = _rope(k_rope, pos)
    latent = jnp.concatenate([past_latent, latent_new], axis=1)
    krope = jnp.concatenate([past_krope, krope_new], axis=1)
    L = latent.shape[1]
    kv = (latent @ w_ukv).reshape(B, L, MLA_HEADS, MLA_NOPE + MLA_V)
    k = _rms_norm(jnp.concatenate(
        [kv[..., :MLA_NOPE], jnp.broadcast_to(krope[:, :, None, :], (B, L, MLA_HEADS, MLA_ROPE)).astype(kv.dtype)],
        axis=-1), g_k)
    v = kv[..., MLA_NOPE:]
    attn = _attention(q, k, v, pos, jnp.arange(L), True).reshape(B, T, MLA_HEADS * MLA_V)
    u = jnp.concatenate([conv_prev, x_rec], axis=1)
    xc = conv_b + sum(conv_w[j] * u[:, j:j + T] for j in range(CONV_WIDTH))
    gates = jnp.einsum('btnc,ncd->btnd', xc.reshape(B, T, LRU_BLOCKS, LRU_BLOCK), gate_w) + gate_b
    r = jax.nn.sigmoid(gates[..., :LRU_BLOCK].astype(jnp.float32)).reshape(B, T, LRU_WIDTH)
    i = jax.nn.sigmoid(gates[..., LRU_BLOCK:].astype(jnp.float32)).reshape(B, T, LRU_WIDTH)
    log_a = -LRU_C * r * jax.nn.softplus(-lam.astype(jnp.float32))
    a = jnp.exp(log_a)
    b = jnp.sqrt(-jnp.expm1(2.0 * log_a)) * (i * xc.astype(jnp.float32))
    b = b.at[:, 0].add(a[:, 0] * lru_h0.astype(jnp.float32))
    _, hs = lax.associative_scan(_linear_combine, (a, b), axis=1)
    y_rec = jax.nn.gelu(x_gate) * hs.astype(h.dtype)
    out = jnp.concatenate([attn, y_rec], axis=-1) @ w_out
    return out, latent_new, krope_new, hs[:, -1].astype(h.dtype), u[:, -(CONV_WIDTH - 1):]


def _odd_mixer(h, pos, w_in, b_f, g_q, g_k, w_out, past_k, past_v, past_logf):
    B, T, _ = h.shape
    q, k, v, f = jnp.split(h @ w_in, [FOX_WIDTH, 2 * FOX_WIDTH, 3 * FOX_WIDTH], axis=-1)
    q = _rms_norm(q.reshape(B, T, FOX_HEADS, FOX_HEAD_DIM), g_q)
    k_new = _rms_norm(k.reshape(B, T, FOX_HEADS, FOX_HEAD_DIM), g_k)
    v_new = v.reshape(B, T, FOX_HEADS, FOX_HEAD_DIM)
    logf_new = jax.nn.log_sigmoid((f + b_f).astype(jnp.float32))
    P = past_k.shape[1]
    k_all = jnp.concatenate([past_k, k_new], axis=1)
    v_all = jnp.concatenate([past_v, v_new], axis=1)
    logf_all = jnp.concatenate([past_logf.astype(jnp.float32), logf_new], axis=1)
    c = jnp.cumsum(logf_all, axis=1)
    attn = _attention(q, k_all, v_all, pos, jnp.arange(P + T), False, c[:, P:], c)
    out = attn.reshape(B, T, FOX_WIDTH) @ w_out
    return out, k_new, v_new, logf_new


def _mem_kv(mem, g_src, w_kv, g_k):
    B, M, _ = mem.shape
    kv = (_rms_norm(mem, g_src) @ w_kv).reshape(B, M, MEM_HEADS, 2 * MEM_HEAD_DIM)
    return _rms_norm(kv[..., :MEM_HEAD_DIM], g_k), kv[..., MEM_HEAD_DIM:]


def _mem_attend(h, w_q, g_q, mem_k, mem_v, w_o):
    B, T, _ = h.shape
    q = _rms_norm((h @ w_q).reshape(B, T, MEM_HEADS, MEM_HEAD_DIM), g_q)
    return _attend_block(q, mem_k, mem_v, None, None).reshape(B, T, MEM_WIDTH) @ w_o


def _layer(x, li, pos, prm, mem_k, mem_v, state):
    h = x + 0.5 * _swiglu(_rms_norm(x, prm['norm_ffn1'][li]), prm['ffn1_w_in'][li], prm['ffn1_w_out'][li])
    hn = _rms_norm(h, prm['norm_mix'][li])
    j = li // 2
    if li % 2 == 0:
        mix, *new = _even_mixer(hn, pos, prm['ev_w_in'][j], prm['ev_g_qlat'][j], prm['ev_g_kvlat'][j],
                                prm['ev_w_uq'][j], prm['ev_w_ukv'][j], prm['ev_g_q'][j], prm['ev_g_k'][j],
                                prm['ev_conv_w'][j], prm['ev_conv_b'][j], prm['ev_gate_w'][j],
                                prm['ev_gate_b'][j], prm['ev_lambda'][j], prm['ev_w_out'][j], *state)
    else:
        mix, *new = _odd_mixer(hn, pos, prm['od_w_in'][j], prm['od_b_f'][j], prm['od_g_q'][j],
                               prm['od_g_k'][j], prm['od_w_out'][j], *state)
    h = h + mix
    h = h + _mem_attend(_rms_norm(h, prm['norm_mem'][li]), prm['mem_w_q'][li], prm['mem_g_q'][li],
                        mem_k, mem_v, prm['mem_w_o'][li])
    h = h + 0.5 * _swiglu(_rms_norm(h, prm['norm_ffn2'][li]), prm['ffn2_w_in'][li], prm['ffn2_w_out'][li])
    return h, new


def _trunk(x, past_len, prm, mem_kvs, even_states, odd_states):
    pos = past_len + jnp.arange(x.shape[1])
    even_new, odd_new = [], []
    for li in range(DEPTH):
        st = even_states[li // 2] if li % 2 == 0 else odd_states[li // 2]
        x, new = _layer(x, li, pos, prm, mem_kvs[li][0], mem_kvs[li][1], st)
        if li % 2 == 0:
            even_new.append(new)
        else:
            odd_new.append(new)
    return x, even_new, odd_new


def setup_inputs(seed: int = 0) -> dict:
    key = jax.random.key(seed)
    ks = iter(jax.random.split(key, 64))
    f32 = jnp.float32

    def w(shape, fan_in):
        return jax.random.normal(next(ks), shape, f32) * (fan_in ** -0.5)

    def gain(shape):
        return 1.0 + 0.01 * jax.random.normal(next(ks), shape, f32)

    def rnd(shape, s=1.0):
        return s * jax.random.normal(next(ks), shape, f32)

    u = jax.random.uniform(next(ks), (N_EVEN, LRU_WIDTH), f32, 0.9, 0.999)
    a0 = u ** (1.0 / LRU_C)
    lam = jnp.log(a0) - jnp.log1p(-a0)
    return {
        'x_prompt': rnd((BATCH, SEQ, D_MODEL)),
        'x_sample': rnd((DEC_BATCH, DEC_SEQ, D_MODEL)),
        'mem_prompt': rnd((BATCH, MEM_TOKENS, D_MODEL)),
        'cache_mla_latent': rnd((N_EVEN, DEC_BATCH, PAST_LEN, MLA_KV_LORA)),
        'cache_mla_krope': rnd((N_EVEN, DEC_BATCH, PAST_LEN, MLA_ROPE)),
        'state_lru_h': rnd((N_EVEN, DEC_BATCH, LRU_WIDTH), 0.5),
        'state_lru_conv': rnd((N_EVEN, DEC_BATCH, CONV_WIDTH - 1, LRU_WIDTH)),
        'cache_fox_k': rnd((N_ODD, DEC_BATCH, PAST_LEN, FOX_HEADS, FOX_HEAD_DIM)),
        'cache_fox_v': rnd((N_ODD, DEC_BATCH, PAST_LEN, FOX_HEADS, FOX_HEAD_DIM)),
        'cache_fox_logf': jax.nn.log_sigmoid(3.0 + rnd((N_ODD, DEC_BATCH, PAST_LEN, FOX_HEADS), 0.5)),
        'cache_mem_k': rnd((DEPTH, DEC_BATCH, MEM_TOKENS, MEM_HEADS, MEM_HEAD_DIM)),
        'cache_mem_v': rnd((DEPTH, DEC_BATCH, MEM_TOKENS, MEM_HEADS, MEM_HEAD_DIM)),
        'norm_ffn1': gain((DEPTH, D_MODEL)),
        'ffn1_w_in': w((DEPTH, D_MODEL, 2 * FFN_DIM), D_MODEL),
        'ffn1_w_out': w((DEPTH, FFN_DIM, D_MODEL), FFN_DIM),
        'norm_mix': gain((DEPTH, D_MODEL)),
        'norm_mem': gain((DEPTH, D_MODEL)),
        'norm_mem_src': gain((DEPTH, D_MODEL)),
        'mem_w_q': w((DEPTH, D_MODEL, MEM_WIDTH), D_MODEL),
        'mem_w_kv': w((DEPTH, D_MODEL, 2 * MEM_WIDTH), D_MODEL),
        'mem_w_o': w((DEPTH, MEM_WIDTH, D_MODEL), MEM_WIDTH),
        'mem_g_q': gain((DEPTH, MEM_HEAD_DIM)),
        'mem_g_k': gain((DEPTH, MEM_HEAD_DIM)),
        'norm_ffn2': gain((DEPTH, D_MODEL)),
        'ffn2_w_in': w((DEPTH, D_MODEL, 2 * FFN_DIM), D_MODEL),
        'ffn2_w_out': w((DEPTH, FFN_DIM, D_MODEL), FFN_DIM),
        'ev_w_in': w((N_EVEN, D_MODEL, EVEN_IN), D_MODEL),
        'ev_g_qlat': gain((N_EVEN, MLA_Q_LORA)),
        'ev_g_kvlat': gain((N_EVEN, MLA_KV_LORA)),
        'ev_w_uq': w((N_EVEN, MLA_Q_LORA, MLA_HEADS * MLA_QK), MLA_Q_LORA),
        'ev_w_ukv': w((N_EVEN, MLA_KV_LORA, MLA_HEADS * (MLA_NOPE + MLA_V)), MLA_KV_LORA),
        'ev_g_q': gain((N_EVEN, MLA_QK)),
        'ev_g_k': gain((N_EVEN, MLA_QK)),
        'ev_conv_w': w((N_EVEN, CONV_WIDTH, LRU_WIDTH), CONV_WIDTH),
        'ev_conv_b': rnd((N_EVEN, LRU_WIDTH), 0.01),
        'ev_gate_w': w((N_EVEN, LRU_BLOCKS, LRU_BLOCK, 2 * LRU_BLOCK), LRU_BLOCK),
        'ev_gate_b': rnd((N_EVEN, LRU_BLOCKS, 2 * LRU_BLOCK), 0.1),
        'ev_lambda': lam,
        'ev_w_out': w((N_EVEN, EVEN_MIX, D_MODEL), EVEN_MIX),
        'od_w_in': w((N_ODD, D_MODEL, ODD_IN), D_MODEL),
        'od_b_f': 3.0 + rnd((N_ODD, FOX_HEADS), 0.1),
        'od_g_q': gain((N_ODD, FOX_HEAD_DIM)),
        'od_g_k': gain((N_ODD, FOX_HEAD_DIM)),
        'od_w_out': w((N_ODD, FOX_WIDTH, D_MODEL), FOX_WIDTH),
    }


def reference(x_prompt, x_sample, mem_prompt,
              cache_mla_latent, cache_mla_krope, state_lru_h, state_lru_conv,
              cache_fox_k, cache_fox_v, cache_fox_logf, cache_mem_k, cache_mem_v,
              norm_ffn1, ffn1_w_in, ffn1_w_out, norm_mix, norm_mem, norm_mem_src,
              mem_w_q, mem_w_kv, mem_w_o, mem_g_q, mem_g_k, norm_ffn2, ffn2_w_in, ffn2_w_out,
              ev_w_in, ev_g_qlat, ev_g_kvlat, ev_w_uq, ev_w_ukv, ev_g_q, ev_g_k,
              ev_conv_w, ev_conv_b, ev_gate_w, ev_gate_b, ev_lambda, ev_w_out,
              od_w_in, od_b_f, od_g_q, od_g_k, od_w_out):
    prm = dict(norm_ffn1=norm_ffn1, ffn1_w_in=ffn1_w_in, ffn1_w_out=ffn1_w_out, norm_mix=norm_mix,
               norm_mem=norm_mem, mem_w_q=mem_w_q, mem_w_o=mem_w_o, mem_g_q=mem_g_q,
               norm_ffn2=norm_ffn2, ffn2_w_in=ffn2_w_in, ffn2_w_out=ffn2_w_out,
               ev_w_in=ev_w_in, ev_g_qlat=ev_g_qlat, ev_g_kvlat=ev_g_kvlat, ev_w_uq=ev_w_uq,
               ev_w_ukv=ev_w_ukv, ev_g_q=ev_g_q, ev_g_k=ev_g_k, ev_conv_w=ev_conv_w,
               ev_conv_b=ev_conv_b, ev_gate_w=ev_gate_w, ev_gate_b=ev_gate_b, ev_lambda=ev_lambda,
               ev_w_out=ev_w_out, od_w_in=od_w_in, od_b_f=od_b_f, od_g_q=od_g_q, od_g_k=od_g_k,
               od_w_out=od_w_out)

    B, dt = x_prompt.shape[0], x_prompt.dtype
    mem_p = [_mem_kv(mem_prompt, norm_mem_src[li], mem_w_kv[li], mem_g_k[li]) for li in range(DEPTH)]
    ev0 = [(jnp.zeros((B, 0, MLA_KV_LORA), dt), jnp.zeros((B, 0, MLA_ROPE), dt),
            jnp.zeros((B, LRU_WIDTH), dt), jnp.zeros((B, CONV_WIDTH - 1, LRU_WIDTH), dt))
           for _ in range(N_EVEN)]
    od0 = [(jnp.zeros((B, 0, FOX_HEADS, FOX_HEAD_DIM), dt), jnp.zeros((B, 0, FOX_HEADS, FOX_HEAD_DIM), dt),
            jnp.zeros((B, 0, FOX_HEADS), jnp.float32)) for _ in range(N_ODD)]
    y_prompt, ev_p, od_p = _trunk(x_prompt, 0, prm, mem_p, ev0, od0)

    past_len = cache_mla_latent.shape[2]
    mem_s = [(cache_mem_k[li], cache_mem_v[li]) for li in range(DEPTH)]
    ev_s = [(cache_mla_latent[j], cache_mla_krope[j], state_lru_h[j], state_lru_conv[j]) for j in range(N_EVEN)]
    od_s = [(cache_fox_k[j], cache_fox_v[j], cache_fox_logf[j]) for j in range(N_ODD)]
    y_sample, ev_n, od_n = _trunk(x_sample, past_len, prm, mem_s, ev_s, od_s)

    p_mla_latent, p_mla_krope, p_lru_h, p_lru_conv = [jnp.stack([s[f] for s in ev_p]) for f in range(4)]
    p_fox_k, p_fox_v, p_fox_logf = [jnp.stack([s[f] for s in od_p]) for f in range(3)]
    p_mem_k = jnp.stack([kv[0] for kv in mem_p])
    p_mem_v = jnp.stack([kv[1] for kv in mem_p])
    s_mla_latent, s_mla_krope, s_lru_h, s_lru_conv = [jnp.stack([s[f] for s in ev_n]) for f in range(4)]
    s_fox_k, s_fox_v, s_fox_logf = [jnp.stack([s[f] for s in od_n]) for f in range(3)]
    return (y_prompt, y_sample, p_mla_latent, p_mla_krope, p_lru_h, p_lru_conv, p_fox_k, p_fox_v,
            p_fox_logf, p_mem_k, p_mem_v, s_mla_latent, s_mla_krope, s_lru_h, s_lru_conv,
            s_fox_k, s_fox_v, s_fox_logf)
```

```python
import contextlib
import math
import numpy as np
import concourse.bass as bass
import concourse.mybir as mybir
from concourse.bass_utils import run_bass_kernel_spmd

F32 = mybir.dt.float32
BF16 = mybir.dt.bfloat16
AF = mybir.ActivationFunctionType
ALU = mybir.AluOpType

EPOCH = 16384
PAGE = 256
NEG = -30000.0
EPS = 1e-6

D = 1024
FFN = 2816
NH_MLA, NOPE, ROPE, DQK, DV = 8, 64, 32, 96, 64
QL, KVL = 256, 128
LW = 512
NH_FOX, HD_FOX = 16, 64
MEM_T, MEM_H, MEM_HD = 256, 4, 128
EVEN_IN = 1440
ODD_IN = 3088


class Buf:
    __slots__ = ("name", "w", "r", "excl")

    def __init__(self, name, excl=False):
        self.name = name
        self.w = None
        self.r = {}
        self.excl = excl


class View:
    __slots__ = ("ap", "bufs")

    def __init__(self, ap, bufs):
        self.ap = ap
        self.bufs = bufs


def _flat(xs):
    out = []
    for x in xs:
        if x is None:
            continue
        if isinstance(x, Buf):
            out.append(x)
        elif isinstance(x, (list, tuple)):
            out.extend(_flat(x))
        else:
            out.extend(x.bufs)
    return out


class Sched:
    def __init__(self, nc, es, n_dma_sems=14):
        self.nc = nc
        self.es = es
        self.eng = {"pe": nc.tensor, "act": nc.scalar, "dve": nc.vector, "pool": nc.gpsimd, "sp": nc.sync}
        self.compute = ("pe", "act", "dve", "pool")
        self.tick = {e: 0 for e in self.compute}
        self.esems = {e: [] for e in self.compute}
        self.dq = {}
        for q in ("sp", "pool"):
            sems = [es.enter_context(nc.semaphore(f"dq_{q}_{i}")) for i in range(n_dma_sems)]
            self.dq[q] = {"sems": sems, "val": [0] * n_dma_sems, "next": 0}
        self.waited = {e: {} for e in ("pe", "act", "dve", "pool", "sp")}
        self.n_waits = 0
        self.n_ops = 0

    def _sem_for(self, e, tick):
        ep = (tick - 1) // EPOCH
        while len(self.esems[e]) <= ep:
            self.esems[e].append(self.es.enter_context(self.nc.semaphore(f"e_{e}_{len(self.esems[e])}")))
        return self.esems[e][ep], tick - ep * EPOCH

    def _wait(self, cons, tok):
        if tok is None:
            return
        if tok[0] == "c":
            _, e, t = tok
            if e == cons and e == "pe":
                return
            key = ("c", e)
            if self.waited[cons].get(key, 0) >= t:
                return
            self.waited[cons][key] = t
            sem, v = self._sem_for(e, t)
            self.eng[cons].wait_ge(sem, v)
            self.n_waits += 1
        else:
            _, q, i, v = tok
            key = ("d", q, i)
            if self.waited[cons].get(key, 0) >= v:
                return
            self.waited[cons][key] = v
            self.eng[cons].wait_ge(self.dq[q]["sems"][i], v)
            self.n_waits += 1

    def _deps(self, cons, reads, writes):
        for b in reads:
            self._wait(cons, b.w)
        for b in writes:
            self._wait(cons, b.w)
            for tok in b.r.values():
                self._wait(cons, tok)

    def _commit(self, tok, reads, writes):
        key = ("c", tok[1]) if tok[0] == "c" else ("d", tok[1], tok[2])
        for b in reads:
            b.r[key] = tok
        for b in writes:
            b.w = tok
            b.r = {}

    def op(self, e, fn, reads=(), writes=()):
        reads = _flat(reads)
        writes = _flat(writes)
        ex = [b for b in reads if b.excl]
        if ex:
            reads = [b for b in reads if not b.excl]
            writes = writes + [b for b in ex if b not in writes]
        self._deps(e, reads, writes)
        ins = fn()
        self.tick[e] += 1
        t = self.tick[e]
        sem, _ = self._sem_for(e, t)
        ins.then_inc(sem, 1)
        tok = ("c", e, t)
        self._commit(tok, reads, writes)
        self.n_ops += 1
        return tok

    def dma(self, q, out, in_, reads=(), writes=(), **kw):
        reads = _flat(reads)
        writes = _flat(writes)
        d = self.dq[q]
        i = d["next"]
        d["next"] = (i + 1) % len(d["sems"])
        if d["val"][i] > 0:
            self._wait(q, ("d", q, i, d["val"][i]))
        self._deps(q, reads, writes)
        d["val"][i] += 16
        self.eng[q].dma_start(out=out, in_=in_, **kw).then_inc(d["sems"][i], 16)
        tok = ("d", q, i, d["val"][i])
        self._commit(tok, reads, writes)
        self.n_ops += 1
        return tok

    def finish(self):
        for q, d in self.dq.items():
            for i, v in enumerate(d["val"]):
                if v:
                    self._wait("sp", ("d", q, i, v))


class Tile:
    def __init__(self, pool, p0, npg, ap, nch, chunk_bytes, name):
        self.pool, self.p0, self.npg, self.ap = pool, p0, npg, ap
        self.nch, self.chunk_bytes, self.name = nch, chunk_bytes, name
        self.bufs = pool.bufs[p0:p0 + npg]

    def c(self, i, rows=None):
        lo = (i * self.chunk_bytes) // (PAGE * 4)
        hi = ((i + 1) * self.chunk_bytes - 1) // (PAGE * 4)
        ap = self.ap[:, i] if rows is None else self.ap[rows[0]:rows[1], i]
        return View(ap, self.bufs[lo:hi + 1])

    def v(self, ap=None):
        return View(self.ap if ap is None else ap, self.bufs)


class Pool:
    def __init__(self, nc, es, npages):
        self.t = es.enter_context(nc.sbuf_tensor("arena", [128, npages * PAGE], F32))
        self.npages = npages
        self.free = [True] * npages
        self.bufs = [Buf(f"pg{i}") for i in range(npages)]
        self.peak = 0

    def alloc(self, shape, dtype, name=""):
        n = int(np.prod(shape))
        eb = 4 if dtype == F32 else 2
        npg = -(-(n * eb) // (PAGE * 4))
        p0 = None
        run = 0
        for i in range(self.npages):
            run = run + 1 if self.free[i] else 0
            if run == npg:
                p0 = i - npg + 1
                break
        if p0 is None:
            raise RuntimeError(f"SBUF pool exhausted allocating {name} {shape} ({npg} pages); free={sum(self.free)}")
        for i in range(p0, p0 + npg):
            self.free[i] = False
        self.peak = max(self.peak, p0 + npg)
        ap32 = self.t[:, p0 * PAGE:(p0 + npg) * PAGE]
        ap = ap32 if dtype == F32 else ap32.bitcast(BF16)
        ap = ap[:, 0:n]
        if len(shape) == 2:
            ap = ap.rearrange("p (c t) -> p c t", c=shape[0])
            nch, cb = shape[0], shape[1] * eb
        elif len(shape) == 3:
            ap = ap.rearrange("p (c a t) -> p c a t", c=shape[0], a=shape[1])
            nch, cb = shape[0], shape[1] * shape[2] * eb
        else:
            nch, cb = 1, n * eb
        return Tile(self, p0, npg, ap, nch, cb, name)

    def release(self, *tiles):
        for t in tiles:
            for i in range(t.p0, t.p0 + t.npg):
                assert not self.free[i]
                self.free[i] = True


class Pipe:
    def __init__(self, gens, width=3):
        self.pending = list(gens)
        self.active = []
        self.width = width

    def step(self):
        if self.pending and len(self.active) < self.width:
            self.active.append(self.pending.pop(0))
        for g in list(self.active):
            try:
                next(g)
            except StopIteration:
                self.active.remove(g)
        return bool(self.pending or self.active)

    def drain(self):
        while self.step():
            pass


class PS:
    def __init__(self, t, name):
        self.t = t
        self.buf = Buf(name, excl=True)
        self.bufs = [self.buf]
        self.held = False


class _Stop(Exception):
    pass


class MK:
    def stage(self, name):
        if self.cfg.get('STOP') == name:
            raise _Stop()

    def __init__(self, cfg):
        self.cfg = cfg
        self.SEQ, self.NSEQ, self.TT, self.DEC, self.PAST = cfg["SEQ"], cfg["NSEQ"], cfg["TT"], cfg["DEC"], cfg["PAST"]

    def declare(self):
        nc = self.nc
        c = self
        SEQ, NSEQ, DEC, PAST = c.SEQ, c.NSEQ, c.DEC, c.PAST
        I = lambda n, s: nc.dram_tensor(n, list(s), F32, kind="ExternalInput").ap()
        O = lambda n, s: nc.dram_tensor(n, list(s), F32, kind="ExternalOutput").ap()
        self.inp = {}
        for n, s in self.input_shapes().items():
            self.inp[n] = I(n, s)
        self.out = {}
        for n, s in self.output_shapes().items():
            self.out[n] = O(n, s)
        LP = SEQ
        LS = PAST + DEC
        self.scr = []
        for s in range(NSEQ + 1):
            L = LP if s < NSEQ else LS
            nkt = -(-L // 128)
            d = {
                "mk": nc.dram_tensor(f"scr_mk{s}", [NH_MLA, DQK, L], BF16, kind="Internal").ap(),
                "mv": nc.dram_tensor(f"scr_mv{s}", [nkt, 128, NH_MLA * 65], BF16, kind="Internal").ap(),
                "fk": nc.dram_tensor(f"scr_fk{s}", [8, 128, L], BF16, kind="Internal").ap(),
                "fv": nc.dram_tensor(f"scr_fv{s}", [nkt, 128, NH_FOX * 65], BF16, kind="Internal").ap(),
                "bufs": {},
            }
            self.scr.append(d)

    def input_shapes(self):
        c = self
        sh = {
            "xp": (c.NSEQ, c.SEQ, D), "xs": (c.DEC, D), "memp": (c.NSEQ, MEM_T, D),
            "c_lat": (c.PAST, KVL), "c_kr": (c.PAST, ROPE), "s_h": (LW,), "s_conv": (3, LW),
            "c_fk": (c.PAST, 1024), "c_fv": (c.PAST, 1024), "c_fl": (c.PAST, NH_FOX),
            "c_mk": (2, MEM_T, 512), "c_mv": (2, MEM_T, 512),
            "norm_ffn1": (2, D), "ffn1_w_in": (2, D, 2 * FFN), "ffn1_w_out": (2, FFN, D),
            "norm_mix": (2, D), "norm_mem": (2, D), "norm_mem_src": (2, D),
            "mem_w_q": (2, D, 512), "mem_w_kv": (2, D, 1024), "mem_w_o": (2, 512, D),
            "mem_g_q": (2, 128), "mem_g_k": (2, 128), "norm_ffn2": (2, D),
            "ffn2_w_in": (2, D, 2 * FFN), "ffn2_w_out": (2, FFN, D),
            "ev_w_in": (1, D, EVEN_IN), "ev_g_qlat": (1, QL), "ev_g_kvlat": (1, KVL),
            "ev_w_uq": (1, QL, NH_MLA * DQK), "ev_w_ukv": (1, KVL, 1024), "ev_g_q": (1, DQK), "ev_g_k": (1, DQK),
            "ev_conv_w": (1, 4, LW), "ev_conv_b": (1, LW), "ev_gate_w": (1, 8, 64, 128), "ev_gate_b": (1, 8, 128),
            "ev_lambda": (1, LW), "ev_w_out": (1, 1024, D),
            "od_w_in": (1, D, ODD_IN), "od_b_f": (1, NH_FOX), "od_g_q": (1, 64), "od_g_k": (1, 64),
            "od_w_out": (1, 1024, D),
            "cmat": (8, 128, 128), "cmask": (2, 128, 512), "ctix": (128, 512), "crope": (128, 2),
        }
        return sh

    def output_shapes(self):
        c = self
        return {
            "o_yp": (c.NSEQ, c.SEQ, D), "o_ys": (c.DEC, D),
            "o_plat": (c.NSEQ, c.SEQ, KVL), "o_pkr": (c.NSEQ, c.SEQ, ROPE), "o_ph": (c.NSEQ, LW),
            "o_pconv": (c.NSEQ, 3, LW), "o_pfk": (c.NSEQ, c.SEQ, 1024), "o_pfv": (c.NSEQ, c.SEQ, 1024),
            "o_pfl": (c.NSEQ, c.SEQ, NH_FOX), "o_pmk": (2, c.NSEQ, MEM_T, 512), "o_pmv": (2, c.NSEQ, MEM_T, 512),
            "o_slat": (c.DEC, KVL), "o_skr": (c.DEC, ROPE), "o_sh": (LW,), "o_sconv": (3, LW),
            "o_sfk": (c.DEC, 1024), "o_sfv": (c.DEC, 1024), "o_sfl": (c.DEC, NH_FOX),
        }

    def next_ps(self, hold=False):
        n = len(self.psg)
        for _ in range(n):
            i = self.ps_rr
            self.ps_rr = (i + 1) % n
            if not self.psg[i].held:
                self.psg[i].held = hold
                return self.psg[i]
        raise RuntimeError("all PSUM banks held")

    def ps_release(self, *pss):
        for p in pss:
            p.held = False

    def mm(self, ps, out_ap, terms, reads):
        nc = self.nc

        def f():
            n = len(terms)
            ins = None
            for i, (l, r) in enumerate(terms):
                ins = nc.tensor.matmul(out_ap, l, r, start=(i == 0), stop=(i == n - 1))
            return ins
        return self.S.op("pe", f, reads=reads, writes=[ps])

    def mm1(self, ps, out_ap, l, r, start, stop, reads):
        nc = self.nc
        return self.S.op("pe", lambda: nc.tensor.matmul(out_ap, l, r, start=start, stop=stop), reads=reads, writes=[ps])

    def tr(self, ps, out_ap, in_ap, k, reads):
        nc = self.nc
        b = in_ap.base_partition()
        return self.S.op("pe", lambda: nc.tensor.transpose(out_ap, in_ap, self.ident.ap[b:b + k, b:b + k]),
                         reads=list(reads) + [self.ident], writes=[ps])

    def act(self, out, in_, func, reads, writes, bias=None, scale=None):
        nc = self.nc
        kw = {}
        if bias is not None:
            kw["bias"] = bias
        if scale is not None:
            kw["scale"] = scale
        return self.S.op("act", lambda: nc.scalar.activation(out=out, in_=in_, func=func, **kw), reads=reads, writes=writes)

    def copy(self, eng, out, in_, reads, writes):
        nc = self.nc
        if eng == "act":
            return self.S.op("act", lambda: nc.scalar.copy(out=out, in_=in_), reads=reads, writes=writes)
        return self.S.op("dve", lambda: nc.vector.tensor_copy(out=out, in_=in_), reads=reads, writes=writes)

    def alt(self):
        self._alt ^= 1
        return "act" if self._alt else "dve"

    def ts(self, out, in0, s1, s2, op0, op1, reads, writes):
        nc = self.nc
        if op1 is None:
            return self.S.op("dve", lambda: nc.vector.tensor_scalar(out=out, in0=in0, scalar1=s1, scalar2=None, op0=op0), reads=reads, writes=writes)
        return self.S.op("dve", lambda: nc.vector.tensor_scalar(out=out, in0=in0, scalar1=s1, scalar2=s2, op0=op0, op1=op1), reads=reads, writes=writes)

    def stt(self, out, in0, scalar, in1, op0, op1, reads, writes):
        nc = self.nc
        return self.S.op("dve", lambda: nc.vector.scalar_tensor_tensor(out=out, in0=in0, scalar=scalar, in1=in1, op0=op0, op1=op1), reads=reads, writes=writes)

    def tt(self, out, in0, in1, op, reads, writes):
        nc = self.nc
        return self.S.op("dve", lambda: nc.vector.tensor_tensor(out=out, in0=in0, in1=in1, op=op), reads=reads, writes=writes)

    def wload(self, parts):
        i = self.w_rr
        self.w_rr = (self.w_rr + 1) % len(self.wslots)
        slot = self.wslots[i]
        for dst_fn, src in parts:
            self.S.dma("pool", dst_fn(slot.ap), src, writes=[slot])
        return slot

    def xstat_chunk(self, x, oc, ntok):
        P = self.P
        key = id(x)
        if oc == 0:
            self.xstat[key] = dict(ps=self.next_ps(hold=True), pend=None)
        d = self.xstat[key]
        sq = P.alloc([ntok], BF16, "sq")
        self.act(sq.ap, x.ap[:, oc, :], AF.Square, reads=[x.c(oc)], writes=[sq])
        if d["pend"] is not None:
            self._xstat_mm(d, ntok)
        d["pend"] = (oc, sq)

    def _xstat_mm(self, d, ntok):
        oc, sq = d["pend"]
        ps = d["ps"]
        self.mm1(ps, ps.t[:, 0:ntok], self.cmb.ap[:, 0, :], sq.ap, oc == 0, oc == 7, reads=[sq, self.cm])
        self.P.release(sq)
        d["pend"] = None

    def rms_x(self, x, gname, li, outs, ntok):
        g0 = self.gcol[gname][li]
        d = self.xstat.pop(id(x), None)
        pre = None
        if d is not None:
            if d["pend"] is not None:
                self._xstat_mm(d, ntok)
            pre = d["ps"]
        for _ in self.rms_g([x.c(k) for k in range(8)], 128, self.cmb.ap[:, 0, :], [self.gv.ap[:, g0 + k:g0 + k + 1] for k in range(8)],
                            outs, ntok, pre=pre):
            pass

    def rms_g(self, srcs, rows, ones_ap, gcols, outs, ntok, outs2=None, custom=None, pre=None):
        P, S = self.P, self.S
        if pre is not None:
            ps = pre
        else:
            n = len(srcs)
            if n == 1:
                sq = P.alloc([ntok], BF16, "sq")
                self.act(sq.ap[0:rows, :], srcs[0].ap, AF.Square, reads=[srcs[0]], writes=[sq])
                yield
                ps = self.next_ps(hold=True)
                self.mm1(ps, ps.t[0:rows, 0:ntok], ones_ap, sq.ap[0:rows, :], True, True, reads=[sq, self.cm])
                P.release(sq)
            else:
                ps = self.next_ps(hold=True)
                for c, v in enumerate(srcs):
                    sq = P.alloc([ntok], BF16, "sq")
                    self.act(sq.ap[0:rows, :], v.ap, AF.Square, reads=[v], writes=[sq])
                    self.mm1(ps, ps.t[0:rows, 0:ntok], ones_ap, sq.ap[0:rows, :], c == 0, c == n - 1, reads=[sq, self.cm])
                    P.release(sq)
            yield
        rt = P.alloc([ntok], F32, "rt")
        self.act(rt.ap[0:rows, :], ps.t[0:rows, 0:ntok], AF.Ln, reads=[ps, self.epsc], writes=[rt], bias=self.epsc.ap[0:rows, 0:1])
        self.act(rt.ap[0:rows, :], rt.ap[0:rows, :], AF.Exp, reads=[rt], writes=[rt], scale=-0.5)
        self.ps_release(ps)
        yield
        if custom is not None:
            custom(rt)
            P.release(rt)
            return
        for c, v in enumerate(srcs):
            self.stt(outs[c].ap, v.ap, gcols[c], rt.ap[0:rows, :], ALU.mult, ALU.mult, reads=[v, rt, self.gv], writes=[outs[c]])
            if outs2 is not None:
                self.stt(outs2[c].ap, v.ap, gcols[c], rt.ap[0:rows, :], ALU.mult, ALU.mult, reads=[v, rt, self.gv], writes=[outs2[c]])
        P.release(rt)

    def rms(self, *a, **k):
        for _ in self.rms_g(*a, **k):
            pass

    def store_tokmajor(self, chunks, rows_list, ntok, dst_fn):
        P = self.P
        F = sum(rows_list)
        nsub = -(-ntok // 128)
        for s in range(nsub):
            n = min(128, ntok - 128 * s)
            stg = P.alloc([F], F32, "ostg")
            off = 0
            ps = None
            psoff = 0
            pend = []
            for v, rows in zip(chunks, rows_list):
                if ps is None or psoff + rows > 512:
                    if ps is not None:
                        pend.append((ps, off - psoff, psoff))
                    ps = self.next_ps()
                    psoff = 0
                self.tr(ps, ps.t[0:n, psoff:psoff + rows], v.ap[:, 128 * s:128 * s + n], rows, reads=[v])
                psoff += rows
                off += rows
            pend.append((ps, off - psoff, psoff))
            for (pp, o0, w) in pend:
                self.copy(self.alt(), stg.ap[0:n, o0:o0 + w], pp.t[0:n, 0:w], reads=[pp], writes=[stg])
            self.S.dma("sp", dst_fn(s, n), stg.ap[0:n, 0:F], reads=[stg])
            P.release(stg)

    def ffn(self, blocks, li, which, mid=None):
        P = self.P
        w_in = self.inp[f"ffn{which}_w_in"][li]
        w_out = self.inp[f"ffn{which}_w_out"][li]
        g0 = self.gcol[f"norm_ffn{which}"][li]
        hns, hs = [], []
        for x, ntok in blocks:
            hn = P.alloc([8, ntok], BF16, "hn")
            self.rms_x(x, f"norm_ffn{which}", li, [hn.c(k) for k in range(8)], ntok)
            hns.append(hn)
            hs.append(P.alloc([22, ntok], BF16, "h"))
        w_in_v = w_in.rearrange("(k p) c -> p k c", p=128)
        for jb in range(11):
            if mid is not None and jb == 3:
                mid()
            slot = self.wload([
                (lambda a: a[:, 0:4096].rearrange("p (k c) -> p k c", k=8)[:, :, 0:256], w_in_v[:, :, 256 * jb:256 * jb + 256]),
                (lambda a: a[:, 0:4096].rearrange("p (k c) -> p k c", k=8)[:, :, 256:512], w_in_v[:, :, FFN + 256 * jb:FFN + 256 * jb + 256]),
            ])
            sv = slot.ap[:, 0:4096].rearrange("p (k c) -> p k c", k=8)
            for (x, ntok), hn, h in zip(blocks, hns, hs):
                for jj in range(2):
                    j = 2 * jb + jj
                    psg = self.next_ps()
                    psu = self.next_ps()
                    self.mm(psg, psg.t[:, 0:ntok], [(sv[:, k, jj * 128:(jj + 1) * 128], hn.ap[:, k, :]) for k in range(8)], reads=[slot, hn])
                    self.mm(psu, psu.t[:, 0:ntok], [(sv[:, k, 256 + jj * 128:256 + (jj + 1) * 128], hn.ap[:, k, :]) for k in range(8)], reads=[slot, hn])
                    sg = P.alloc([ntok], F32, "sg")
                    self.act(sg.ap, psg.t[:, 0:ntok], AF.Silu, reads=[psg], writes=[sg])
                    self.tt(h.ap[:, j, :], sg.ap, psu.t[:, 0:ntok], ALU.mult, reads=[sg, psu], writes=[h.c(j)])
                    P.release(sg)
        w_out_v = w_out.rearrange("(j p) c -> p j c", p=128)
        for oc in range(8):
            slot = self.wload([(lambda a: a[:, 0:22 * 128].rearrange("p (j c) -> p j c", j=22), w_out_v[:, :, 128 * oc:128 * oc + 128])])
            sv = slot.ap[:, 0:22 * 128].rearrange("p (j c) -> p j c", j=22)
            for (x, ntok), h in zip(blocks, hs):
                ps = self.next_ps()
                self.mm(ps, ps.t[:, 0:ntok], [(sv[:, j, :], h.ap[:, j, :]) for j in range(22)], reads=[slot, h])
                self.stt(x.ap[:, oc, :], ps.t[:, 0:ntok], 0.5, x.ap[:, oc, :], ALU.mult, ALU.add, reads=[ps, x.c(oc)], writes=[x.c(oc)])
                if not (li == 1 and which == 2):
                    self.xstat_chunk(x, oc, ntok)
        P.release(*hns, *hs)

    def attention(self, seq, kind, qv, nh, dk, scale, nkeys, ntok, diag, bias_fn, out_fn, bg=None, skip=None):
        P, S, nc = self.P, self.S, self.nc
        sc = self.scr[seq]
        nkt = -(-nkeys // 128)
        kK = "mk" if kind == "mla" else "fk"
        kV = "mv" if kind == "mla" else "fv"
        HG = 2
        pend = None
        upk = 1 if kind == "mla" else 2
        nku, nvg = nh // upk, nh // HG
        kq, vq = {}, {}

        def load_k(u):
            t = P.alloc([nkeys], BF16, "kbuf")
            if kind == "mla":
                S.dma("sp", t.ap[0:DQK, :], sc[kK][u, :, 0:nkeys], reads=self.scr_bufs(seq, kK, nkeys), writes=[t])
            else:
                S.dma("sp", t.ap, sc[kK][u, :, 0:nkeys], reads=self.scr_bufs(seq, kK, nkeys), writes=[t])
            return t

        def load_v(gi):
            t = P.alloc([nkt, HG * 65], BF16, "vbuf")
            src = sc[kV][0:nkt, :, gi * HG * 65:(gi + 1) * HG * 65].rearrange("k p c -> p k c")
            S.dma("sp", t.ap, src, reads=self.scr_bufs(seq, kV, nkeys), writes=[t])
            return t
        kq[0] = load_k(0)
        vq[0] = load_v(0)
        bgc = [0]
        bg_every = max(2, (nh * nkt) // 24)
        for h in range(nh):
            u, gi = h // upk, h // HG
            if h % upk == 0:
                if u - 1 in kq:
                    P.release(kq.pop(u - 1))
                if u + 1 < nku:
                    kq[u + 1] = load_k(u + 1)
            if h % HG == 0:
                if gi - 1 in vq:
                    P.release(vq.pop(gi - 1))
                if gi + 1 < nvg:
                    vq[gi + 1] = load_v(gi + 1)
            kbuf, vbuf = kq[u], vq[gi]
            kb0 = 0
            q = qv(h)
            po = self.pso[h % 2]
            pts = {}
            c0_of = {kt: (skip.get(kt, 0) if skip else 0) for kt in range(nkt)}
            full = [kt for kt in range(nkt) if c0_of[kt] == 0]
            part = [kt for kt in range(nkt) if c0_of[kt] > 0]
            if len(full) >= 2:
                order = [full[0]] + part + full[1:]
            else:
                order = full + part
            first, last = order[0], order[-1]
            closing = c0_of[last] > 0
            if closing:
                last = None

            def emit_s(kt):
                ksz = min(128, nkeys - 128 * kt)
                c0 = c0_of[kt]
                ps = self.next_ps()
                terms = [(kbuf.ap[kb0:kb0 + dk, 128 * kt:128 * kt + ksz], q.ap[:, c0:ntok])]
                rd = [kbuf, q]
                if kt in diag:
                    terms.append((self.identb.ap[0:ksz, 0:ksz], diag[kt][:, 0:ntok - c0]))
                    rd += [self.identb, self.maskt]
                self.mm(ps, ps.t[0:ksz, c0:ntok], terms, reads=rd)
                pt = P.alloc([ntok], BF16, "pt")
                b = bias_fn(kt, h, ksz) if bias_fn is not None else None
                self.act(pt.ap[0:ksz, c0:ntok], ps.t[0:ksz, c0:ntok], AF.Exp, reads=[ps] + ([self.nbt] if b is not None else []), writes=[pt], bias=b, scale=scale)
                pts[kt] = (pt, ksz, c0)

            def emit_pv(kt):
                pt, ksz, c0 = pts.pop(kt)
                hv = vbuf.ap[0:ksz, kt, (h % HG) * 65:(h % HG) * 65 + 65]
                self.mm1(po, po.t[0:65, c0:ntok], hv, pt.ap[0:ksz, c0:ntok], kt == first, kt == last, reads=[vbuf, pt])
                P.release(pt)
            NB = 3
            batches = [order[i:i + NB] for i in range(0, len(order), NB)]
            for kt in batches[0]:
                emit_s(kt)
            for bi, bt in enumerate(batches):
                if bi + 1 < len(batches):
                    for kt in batches[bi + 1]:
                        emit_s(kt)
                for kt in bt:
                    emit_pv(kt)
                if closing and bi == len(batches) - 1:
                    self.mm1(po, po.t[0:65, 0:ntok], self.zerob.ap[:, 0:65], self.maskt.ap[:, 0, 0:ntok], False, True, reads=[self.zerob, self.maskt])
                if bg is not None:
                    bgc[0] += len(bt)
                    while bgc[0] >= bg_every:
                        bgc[0] -= bg_every
                        bg.step()
            osb = P.alloc([ntok], F32, "osb")
            self.copy("dve", osb.ap[0:65, :], po.t[0:65, 0:ntok], reads=[po], writes=[osb])
            S.op("dve", lambda: nc.vector.reciprocal(out=osb.ap[64:65, :], in_=osb.ap[64:65, :]), reads=[osb], writes=[osb])
            if pend is not None:
                pend()

            def fin(osb=osb, h=h):
                pb = self.next_ps()
                self.mm(pb, pb.t[0:64, 0:ntok], [(self.onesf.ap[64:65, 0:64], osb.ap[64:65, :])], reads=[osb, self.onesf])
                dst, dview = out_fn(h)
                self.tt(dst, osb.ap[0:64, :], pb.t[0:64, 0:ntok], ALU.mult, reads=[osb, pb], writes=[dview])
                P.release(osb)
            pend = fin
        if bg is not None:
            bg.drain()
        pend()
        for t in list(kq.values()) + list(vq.values()):
            P.release(t)

    def scr_bufs(self, seq, kind, nkeys):
        d = self.scr[seq]["bufs"]
        return [b for (k, lo), b in d.items() if k == kind and lo < nkeys]

    def scr_buf(self, seq, kind, lo):
        d = self.scr[seq]["bufs"]
        if (kind, lo) not in d:
            d[(kind, lo)] = Buf(f"scr{seq}{kind}{lo}")
        return d[(kind, lo)]

    def rope_tables(self, pos0, ntok):
        P = self.P
        R = slice(64, 96)
        I32 = mybir.dt.int32
        ang = P.alloc([ntok], F32, "ang")
        cosT = P.alloc([ntok], F32, "cosT")
        ssin = P.alloc([ntok], F32, "ssin")
        ki = P.alloc([ntok], F32, "ki")
        kf = P.alloc([ntok], F32, "kf")
        HI = 6.28125
        LO = 2.0 * math.pi - HI
        self.ts(ang.ap[R, :], self.tix.ap[R, 0:ntok], float(pos0), None, ALU.add, None, reads=[self.tix], writes=[ang])
        self.ts(ang.ap[R, :], ang.ap[R, :], self.ropec.ap[R, 0:1], None, ALU.mult, None, reads=[ang, self.ropec], writes=[ang])
        for shift, dst in ((0.0, ssin), (0.5 * math.pi, cosT)):
            if shift != 0.0:
                self.ts(ang.ap[R, :], ang.ap[R, :], shift, None, ALU.add, None, reads=[ang], writes=[ang])
            self.ts(ki.ap.bitcast(I32)[R, :], ang.ap[R, :], 1.0 / (2.0 * math.pi), None, ALU.mult, None, reads=[ang], writes=[ki])
            self.copy("dve", kf.ap[R, :], ki.ap.bitcast(I32)[R, :], reads=[ki], writes=[kf])
            self.stt(dst.ap[R, :], kf.ap[R, :], -HI, ang.ap[R, :], ALU.mult, ALU.add, reads=[kf, ang], writes=[dst])
            self.stt(dst.ap[R, :], kf.ap[R, :], -LO, dst.ap[R, :], ALU.mult, ALU.add, reads=[kf, dst], writes=[dst])
            self.ts(dst.ap[R, :], dst.ap[R, :], -math.pi, math.pi, ALU.max, ALU.min, reads=[dst], writes=[dst])
            self.act(dst.ap[R, :], dst.ap[R, :], AF.Sin, reads=[dst], writes=[dst])
        self.ts(ssin.ap[R, :], ssin.ap[R, :], self.ropec.ap[R, 1:2], None, ALU.mult, None, reads=[ssin, self.ropec], writes=[ssin])
        P.release(ang, ki, kf)
        return cosT, ssin

    def mla_kv_expand(self, seq, latb, kr, pos_rel, ntok):
        P, S, nc = self.P, self.S, self.nc
        sc = self.scr[seq]
        R = slice(64, 96)
        kst = P.alloc([NH_MLA, ntok], BF16, "kst")
        def kgen(h):
            ps = self.next_ps(hold=True)
            self.mm(ps, ps.t[0:64, 0:ntok], [(self.wukv.ap[:, 128 * h:128 * h + 64], latb.ap)], reads=[self.wukv, latb])
            yield
            kf = P.alloc([ntok], F32, "kf")
            self.copy("act", kf.ap[0:64, :], ps.t[0:64, 0:ntok], reads=[ps], writes=[kf])
            self.copy("dve", kf.ap[R, :], kr.ap[R, :], reads=[kr], writes=[kf])
            self.ps_release(ps)
            yield
            yield from self.rms_g([View(kf.ap[0:96, :], kf.bufs)], 96, self.cmb.ap[0:96, 3, 0:96],
                                  [self.gv.ap[0:96, self.gcol["ev_g_k"]:self.gcol["ev_g_k"] + 1]],
                                  [View(kst.ap[0:96, h, :], kst.c(h).bufs)], ntok)
            P.release(kf)
        Pipe([kgen(h) for h in range(NH_MLA)], width=3).drain()
        bk = self.scr_buf(seq, "mk", pos_rel)
        S.dma("sp", sc["mk"][:, :, pos_rel:pos_rel + ntok].rearrange("h r t -> r h t"), kst.ap[0:96, :, :], reads=[kst], writes=[bk])
        P.release(kst)
        nsub = -(-ntok // 128)
        vst = P.alloc([nsub, NH_MLA * 65], BF16, "vst")
        S.op("dve", lambda: nc.vector.memset(vst.ap, 1.0), writes=[vst])
        wv = self.wukv.ap.rearrange("p (h c) -> p h c", h=NH_MLA)[:, :, 64:128]
        for s in range(nsub):
            n = min(128, ntok - 128 * s)
            ps = self.next_ps()
            self.mm(ps, ps.t[0:n, 0:512], [(latb.ap[:, 128 * s:128 * s + n], wv)], reads=[self.wukv, latb])
            self.copy(self.alt(), vst.ap[0:n, s, :].rearrange("p (h c) -> p h c", h=NH_MLA)[:, :, 0:64],
                      ps.t[0:n, 0:512].rearrange("p (h c) -> p h c", h=NH_MLA), reads=[ps], writes=[vst])
        bv = self.scr_buf(seq, "mv", pos_rel)
        kt0 = pos_rel // 128
        S.dma("sp", sc["mv"][kt0:kt0 + nsub].rearrange("k p c -> p k c"), vst.ap, reads=[vst], writes=[bv])
        P.release(vst)

    def even_mixer(self, x, seq, st, pos0, pos_rel, ntok, outs):
        P, S, nc = self.P, self.S, self.nc
        g = self.gcol
        w_in = self.inp["ev_w_in"][0].rearrange("(k p) c -> p k c", p=128)
        R = slice(64, 96)
        hn = P.alloc([8, ntok], BF16, "hn")
        self.rms_x(x, "norm_mix", 0, [hn.c(k) for k in range(8)], ntok)
        hk = lambda k: hn.ap[:, k, :]
        v8 = lambda a, w: a[:, 0:8 * w].rearrange("p (k c) -> p k c", k=8)
        slotA = self.wload([
            (lambda a: v8(a, 448)[:, :, 0:416], w_in[:, :, 0:416]),
            (lambda a: v8(a, 448)[:, :, 416:432], w_in[:, :, 400:416]),
            (lambda a: v8(a, 448)[:, :, 432:448], w_in[:, :, 384:400]),
        ])
        sA = v8(slotA.ap, 448)
        cosT, ssin = self.rope_tables(pos0, ntok)
        cqn = P.alloc([2, ntok], BF16, "cqn")
        lat = P.alloc([ntok], F32, "lat")
        latb = P.alloc([ntok], BF16, "latb")
        kr = P.alloc([ntok], F32, "kr")

        def lat_chain():
            ps = self.next_ps(hold=True)
            self.mm(ps, ps.t[:, 0:ntok], [(sA[:, k, 256:384], hk(k)) for k in range(8)], reads=[slotA, hn])
            yield
            yield from self.rms_g([View(ps.t[:, 0:ntok], ps.bufs)], 128, self.cmb.ap[:, 2, :], [self.gv.ap[:, g["ev_g_kvlat"]:g["ev_g_kvlat"] + 1]],
                                  [lat.v()], ntok, outs2=[latb.v()])
            self.ps_release(ps)

        def kr_chain():
            ps = self.next_ps()
            self.mm(ps, ps.t[0:96, 0:ntok], [(sA[:, k, 320:416], hk(k)) for k in range(8)], reads=[slotA, hn])
            ps2 = self.next_ps()
            self.mm(ps2, ps2.t[0:96, 0:ntok], [(sA[:, k, 352:448], hk(k)) for k in range(8)], reads=[slotA, hn])
            self.tt(kr.ap[R, :], ps.t[R, 0:ntok], cosT.ap[R, :], ALU.mult, reads=[ps, cosT], writes=[kr])
            tmp = P.alloc([ntok], F32, "tmp")
            self.tt(tmp.ap[R, :], ps2.t[R, 0:ntok], ssin.ap[R, :], ALU.mult, reads=[ps2, ssin], writes=[tmp])
            self.tt(kr.ap[R, :], kr.ap[R, :], tmp.ap[R, :], ALU.add, reads=[kr, tmp], writes=[kr])
            P.release(tmp)
            yield

        def cq_chain():
            cq = P.alloc([2, ntok], F32, "cq")
            for c in range(2):
                ps = self.next_ps()
                self.mm(ps, ps.t[:, 0:ntok], [(sA[:, k, 128 * c:128 * c + 128], hk(k)) for k in range(8)], reads=[slotA, hn])
                self.copy(self.alt(), cq.ap[:, c, :], ps.t[:, 0:ntok], reads=[ps], writes=[cq.c(c)])
            yield
            yield from self.rms_g([cq.c(0), cq.c(1)], 128, self.cmb.ap[:, 1, :], [self.gv.ap[:, g["ev_g_qlat"] + c:g["ev_g_qlat"] + c + 1] for c in range(2)],
                                  [cqn.c(0), cqn.c(1)], ntok)
            P.release(cq)
        Pipe([lat_chain(), kr_chain(), cq_chain()], width=3).drain()
        self.mla_kv_expand(seq, latb, kr, pos_rel, ntok)
        self.store_tokmajor([lat.v()], [128], ntok, outs["lat"])
        self.store_tokmajor([View(kr.ap[R, :], kr.bufs)], [32], ntok, outs["kr"])
        P.release(lat, latb, kr)
        qn = P.alloc([NH_MLA, ntok], BF16, "qn")

        def qgen(h):
            ps = self.next_ps(hold=True)
            self.mm(ps, ps.t[0:96, 0:ntok], [(self.wuq.ap[:, k, 96 * h:96 * h + 96], cqn.ap[:, k, :]) for k in range(2)], reads=[self.wuq, cqn])
            ps2 = self.next_ps(hold=True)
            self.mm(ps2, ps2.t[0:96, 0:ntok], [(self.wuqs.ap[:, k, 96 * h:96 * h + 96], cqn.ap[:, k, :]) for k in range(2)], reads=[self.wuqs, cqn])
            yield
            qf = P.alloc([ntok], F32, "qf")
            self.copy("act", qf.ap[0:64, :], ps.t[0:64, 0:ntok], reads=[ps], writes=[qf])
            self.tt(qf.ap[R, :], ps.t[R, 0:ntok], cosT.ap[R, :], ALU.mult, reads=[ps, cosT], writes=[qf])
            tmp = P.alloc([ntok], F32, "tmp")
            self.tt(tmp.ap[R, :], ps2.t[R, 0:ntok], ssin.ap[R, :], ALU.mult, reads=[ps2, ssin], writes=[tmp])
            self.tt(qf.ap[R, :], qf.ap[R, :], tmp.ap[R, :], ALU.add, reads=[qf, tmp], writes=[qf])
            P.release(tmp)
            self.ps_release(ps, ps2)
            yield
            yield from self.rms_g([View(qf.ap[0:96, :], qf.bufs)], 96, self.cmb.ap[0:96, 3, 0:96], [self.gv.ap[0:96, g["ev_g_q"]:g["ev_g_q"] + 1]],
                                  [View(qn.ap[0:96, h, :], qn.c(h).bufs)], ntok)
            P.release(qf)
        mixin = P.alloc([8, ntok], BF16, "mixin")
        rg = self.rglru(hn, w_in, st, ntok, mixin, outs)
        Pipe([qgen(i) for i in range(NH_MLA)], width=2).drain()
        P.release(cqn, cosT, ssin)
        bg = Pipe(rg, width=2)
        nkeys = pos_rel + ntok
        diag = {}
        if st["kind"] == "prompt":
            for i in range(self.TT // 128):
                diag[pos_rel // 128 + i] = self.maskt.ap[:, 1, :]
        self.attention(seq, "mla", lambda h: View(qn.ap[0:96, h, :], qn.c(h).bufs), NH_MLA, DQK, DQK ** -0.5, nkeys, ntok, diag, None,
                       lambda h: (mixin.ap[64 * (h % 2):64 * (h % 2) + 64, h // 2, :], mixin.c(h // 2)), bg=bg,
                       skip=({pos_rel // 128 + i: 128 * i for i in range(self.TT // 128)} if st["kind"] == "prompt" else None))
        P.release(qn, hn)
        w_out = self.inp["ev_w_out"][0].rearrange("(j p) c -> p j c", p=128)
        for ob in range(2):
            slot = self.wload([(lambda a: v8(a, 512), w_out[:, :, 512 * ob:512 * ob + 512])])
            sv = v8(slot.ap, 512)
            for oo in range(4):
                oc = 4 * ob + oo
                ps = self.next_ps()
                self.mm(ps, ps.t[:, 0:ntok], [(sv[:, j, 128 * oo:128 * oo + 128], mixin.ap[:, j, :]) for j in range(8)], reads=[slot, mixin])
                self.tt(x.ap[:, oc, :], ps.t[:, 0:ntok], x.ap[:, oc, :], ALU.add, reads=[ps, x.c(oc)], writes=[x.c(oc)])
                self.xstat_chunk(x, oc, ntok)
        P.release(mixin)

    def rglru(self, hn, w_in, st, ntok, mixin, outs):
        P, S, nc = self.P, self.S, self.nc
        g = self.gcol
        v8 = lambda a, w: a[:, 0:8 * w].rearrange("p (k c) -> p k c", k=8)
        slotB = self.wload([(lambda a: v8(a, 512), w_in[:, :, 416:928])])
        slotC = self.wload([(lambda a: v8(a, 512), w_in[:, :, 928:1440])])
        sB, sC = v8(slotB.ap, 512), v8(slotC.ap, 512)
        U = st["U"]
        hc = st["hcarry"]
        if ntok < 3:
            raise NotImplementedError

        def chain(c):
            ps = self.next_ps(hold=True)
            self.mm(ps, ps.t[:, 0:ntok], [(sB[:, k, 128 * c:128 * c + 128], hn.ap[:, k, :]) for k in range(8)], reads=[slotB, hn])
            psg = self.next_ps(hold=True)
            self.mm(psg, psg.t[:, 0:ntok], [(sC[:, k, 128 * c:128 * c + 128], hn.ap[:, k, :]) for k in range(8)], reads=[slotC, hn])
            yield
            self.copy("act", U.ap[:, c, 3:3 + ntok], ps.t[:, 0:ntok], reads=[ps], writes=[U.c(c)])
            xg = P.alloc([ntok], F32, "xg")
            self.copy("act", xg.ap, psg.t[:, 0:ntok], reads=[psg], writes=[xg])
            self.ps_release(ps, psg)
            yield
            xc = P.alloc([ntok], F32, "xc")
            cw = g["conv_w"]
            self.ts(xc.ap, U.ap[:, c, 0:ntok], self.gv.ap[:, cw + c:cw + c + 1], self.gv.ap[:, g["conv_b"] + c:g["conv_b"] + c + 1], ALU.mult, ALU.add,
                    reads=[U.c(c), self.gv], writes=[xc])
            for j in range(1, 4):
                self.stt(xc.ap, U.ap[:, c, j:j + ntok], self.gv.ap[:, cw + 4 * j + c:cw + 4 * j + c + 1], xc.ap, ALU.mult, ALU.add,
                         reads=[U.c(c), xc, self.gv], writes=[xc])
            self.copy("dve", U.ap[:, c, 0:3], U.ap[:, c, ntok:ntok + 3], reads=[U.c(c)], writes=[U.c(c)])
            u = P.alloc([ntok], F32, "u")
            self.tt(u.ap, xg.ap, xg.ap, ALU.mult, reads=[xg], writes=[u])
            self.ts(u.ap, u.ap, 0.044715, 1.0, ALU.mult, ALU.add, reads=[u], writes=[u])
            self.tt(u.ap, u.ap, xg.ap, ALU.mult, reads=[u, xg], writes=[u])
            yield
            xcb = P.alloc([ntok], BF16, "xcb")
            self.copy("act", xcb.ap, xc.ap, reads=[xc], writes=[xcb])
            self.act(u.ap, u.ap, AF.Sigmoid, reads=[u], writes=[u], scale=1.5957691216057308)
            yield
            psr = self.next_ps(hold=True)
            self.mm(psr, psr.t[:, 0:ntok], [(self.gw.ap[:, c, :], xcb.ap)], reads=[self.gw, xcb])
            psi = self.next_ps(hold=True)
            self.mm(psi, psi.t[:, 0:ntok], [(self.gw.ap[:, 4 + c, :], xcb.ap)], reads=[self.gw, xcb])
            P.release(xcb)
            self.tt(u.ap, u.ap, xg.ap, ALU.mult, reads=[u, xg], writes=[u])
            P.release(xg)
            yield
            a = P.alloc([ntok], F32, "a")
            b = P.alloc([ntok], F32, "b")
            self.act(a.ap, psr.t[:, 0:ntok], AF.Sigmoid, reads=[psr, self.gv], writes=[a], bias=self.gv.ap[:, g["gb_r"] + c:g["gb_r"] + c + 1])
            self.act(a.ap, a.ap, AF.Exp, reads=[a, self.gv], writes=[a], scale=self.gv.ap[:, g["nsp8"] + c:g["nsp8"] + c + 1])
            self.act(b.ap, psi.t[:, 0:ntok], AF.Sigmoid, reads=[psi, self.gv], writes=[b], bias=self.gv.ap[:, g["gb_i"] + c:g["gb_i"] + c + 1])
            self.ps_release(psr, psi)
            yield
            self.tt(b.ap, b.ap, xc.ap, ALU.mult, reads=[b, xc], writes=[b])
            t1 = P.alloc([ntok], F32, "t1")
            self.tt(t1.ap, a.ap, a.ap, ALU.mult, reads=[a], writes=[t1])
            self.ts(t1.ap, t1.ap, -1.0, 1.0, ALU.mult, ALU.add, reads=[t1], writes=[t1])
            P.release(xc)
            yield
            self.act(t1.ap, t1.ap, AF.Sqrt, reads=[t1], writes=[t1])
            yield
            self.tt(b.ap, b.ap, t1.ap, ALU.mult, reads=[b, t1], writes=[b])
            P.release(t1)
            hs = P.alloc([ntok], F32, "hs")
            S.op("dve", lambda: nc.vector.tensor_tensor_scan(out=hs.ap, data0=a.ap, data1=b.ap, initial=hc.ap[:, c:c + 1], op0=ALU.mult, op1=ALU.add),
                 reads=[a, b, hc], writes=[hs])
            self.copy("dve", hc.ap[:, c:c + 1], hs.ap[:, ntok - 1:ntok], reads=[hs], writes=[hc])
            P.release(a, b)
            self.tt(mixin.ap[:, 4 + c, :], u.ap, hs.ap, ALU.mult, reads=[u, hs], writes=[mixin.c(4 + c)])
            P.release(u, hs)
        return [chain(c) for c in range(4)]

    def mem_attend(self, x, li, mst, ntok):
        P, S, nc = self.P, self.S, self.nc
        g = self.gcol
        v8 = lambda a, w: a[:, 0:8 * w].rearrange("p (k c) -> p k c", k=8)
        hn = P.alloc([8, ntok], BF16, "hn")
        self.rms_x(x, "norm_mem", li, [hn.c(k) for k in range(8)], ntok)
        wq = self.inp["mem_w_q"][li].rearrange("(k p) c -> p k c", p=128)
        slot = self.wload([(lambda a: v8(a, 512), wq)])
        sv = v8(slot.ap, 512)
        att = P.alloc([4, ntok], BF16, "att")
        memK, memV = mst["K"], mst["V"]

        def hgen(h):
            ps = self.next_ps(hold=True)
            self.mm(ps, ps.t[:, 0:ntok], [(sv[:, k, 128 * h:128 * h + 128], hn.ap[:, k, :]) for k in range(8)], reads=[slot, hn])
            yield
            qn = P.alloc([ntok], BF16, "mqn")
            yield from self.rms_g([View(ps.t[:, 0:ntok], ps.bufs)], 128, self.cmb.ap[:, 2, :], [self.gv.ap[:, g["mem_g_q"][li]:g["mem_g_q"][li] + 1]], [qn.v()], ntok)
            self.ps_release(ps)
            yield
            pts = []
            for mt in range(2):
                pss = self.next_ps()
                self.mm(pss, pss.t[:, 0:ntok], [(memK.ap[:, li * 4 + h, 128 * mt:128 * mt + 128], qn.ap)], reads=[memK, qn])
                pt = P.alloc([ntok], BF16, "pt")
                self.act(pt.ap, pss.t[:, 0:ntok], AF.Exp, reads=[pss], writes=[pt], scale=MEM_HD ** -0.5)
                pts.append(pt)
            P.release(qn)
            yield
            po = self.next_ps(hold=True)
            pd = self.next_ps(hold=True)
            for mt in range(2):
                self.mm1(po, po.t[:, 0:ntok], memV.ap[:, li * 2 + mt, 128 * h:128 * h + 128], pts[mt].ap, mt == 0, mt == 1, reads=[memV, pts[mt]])
            for mt in range(2):
                self.mm1(pd, pd.t[:, 0:ntok], self.cmb.ap[:, 5, :], pts[mt].ap, mt == 0, mt == 1, reads=[self.cm, pts[mt]])
            P.release(*pts)
            yield
            rc = P.alloc([ntok], F32, "rc")
            self.act(rc.ap, pd.t[:, 0:ntok], AF.Ln, reads=[pd], writes=[rc])
            self.act(rc.ap, rc.ap, AF.Exp, reads=[rc], writes=[rc], scale=-1.0)
            yield
            self.tt(att.ap[:, h, :], po.t[:, 0:ntok], rc.ap, ALU.mult, reads=[po, rc], writes=[att.c(h)])
            P.release(rc)
            self.ps_release(po, pd)
        Pipe([hgen(h) for h in range(MEM_H)], width=2).drain()
        P.release(hn)
        wo = self.inp["mem_w_o"][li].rearrange("(j p) c -> p j c", p=128)
        v4 = lambda a: a[:, 0:4096].rearrange("p (j c) -> p j c", j=4)
        slot = self.wload([(lambda a: v4(a), wo)])
        sv = v4(slot.ap)
        for oc in range(8):
            ps = self.next_ps()
            self.mm(ps, ps.t[:, 0:ntok], [(sv[:, j, 128 * oc:128 * oc + 128], att.ap[:, j, :]) for j in range(4)], reads=[slot, att])
            self.tt(x.ap[:, oc, :], ps.t[:, 0:ntok], x.ap[:, oc, :], ALU.add, reads=[ps, x.c(oc)], writes=[x.c(oc)])
            self.xstat_chunk(x, oc, ntok)
        P.release(att)

    def odd_mixer(self, x, seq, st, pos0, pos_rel, ntok, outs):
        P, S, nc = self.P, self.S, self.nc
        g = self.gcol
        sc = self.scr[seq]
        v8 = lambda a, w: a[:, 0:8 * w].rearrange("p (k c) -> p k c", k=8)
        w_in = self.inp["od_w_in"][0].rearrange("(k p) c -> p k c", p=128)
        hn = P.alloc([8, ntok], BF16, "hn")
        self.rms_x(x, "norm_mix", 1, [hn.c(k) for k in range(8)], ntok)
        nsub = -(-ntok // 128)
        bv = self.scr_buf(seq, "fv", pos_rel)
        kt0 = pos_rel // 128
        vst = P.alloc([nsub, NH_FOX * 65], BF16, "fvst")
        S.op("dve", lambda: nc.vector.memset(vst.ap, 1.0), writes=[vst])
        slots = [self.wload([(lambda a: v8(a, 512), w_in[:, :, 2048 + 512 * blk:2048 + 512 * blk + 512])]) for blk in range(2)]
        for s in range(nsub):
            n = min(128, ntok - 128 * s)
            stg = P.alloc([1024], F32, "vstg")
            for blk in range(2):
                sv = v8(slots[blk].ap, 512)
                ps = self.next_ps()
                self.mm(ps, ps.t[0:n, 0:512], [(hn.ap[:, k, 128 * s:128 * s + n], sv[:, k, :]) for k in range(8)], reads=[slots[blk], hn])
                self.copy("act", stg.ap[0:n, 512 * blk:512 * blk + 512], ps.t[0:n, 0:512], reads=[ps], writes=[stg])
                self.copy("dve", vst.ap[0:n, s, :].rearrange("p (h c) -> p h c", h=NH_FOX)[:, 8 * blk:8 * blk + 8, 0:64],
                          ps.t[0:n, 0:512].rearrange("p (h c) -> p h c", h=8), reads=[ps], writes=[vst])
            S.dma("sp", outs["fv"](s, n), stg.ap[0:n, :], reads=[stg])
            P.release(stg)
        S.dma("sp", sc["fv"][kt0:kt0 + nsub].rearrange("k p c -> p k c"), vst.ap, reads=[vst], writes=[bv])
        P.release(vst)
        qn = P.alloc([NH_FOX, ntok], BF16, "fqz")
        S.op("dve", lambda: nc.vector.memset(qn.ap, 0.0), writes=[qn])
        knf = P.alloc([8, ntok], F32, "knf")
        kst = P.alloc([8, ntok], BF16, "fkst")

        def cgen(part, c, slot):
            sv = v8(slot.ap, 512)
            cc = c % 4
            ps = self.next_ps(hold=True)
            self.mm(ps, ps.t[:, 0:ntok], [(sv[:, k, 128 * cc:128 * cc + 128], hn.ap[:, k, :]) for k in range(8)], reads=[slot, hn])
            yield
            if part == 0:
                def qout(rt, c=c, ps=ps):
                    for hh in range(2):
                        rr = slice(64 * hh, 64 * hh + 64)
                        self.stt(qn.ap[rr, 2 * c + hh, :], ps.t[rr, 0:ntok], self.gv.ap[rr, g["od_g_q"]:g["od_g_q"] + 1], rt.ap[rr, :], ALU.mult, ALU.mult,
                                 reads=[ps, rt, self.gv], writes=[qn.c(2 * c + hh)])
                yield from self.rms_g([View(ps.t[:, 0:ntok], ps.bufs)], 128, self.cmb.ap[:, 4, :], None, None, ntok, custom=qout)
            else:
                yield from self.rms_g([View(ps.t[:, 0:ntok], ps.bufs)], 128, self.cmb.ap[:, 4, :], [self.gv.ap[:, g["od_g_k"]:g["od_g_k"] + 1]], [knf.c(c)], ntok,
                                      outs2=[kst.c(c)])
            self.ps_release(ps)
        kslots = [self.wload([(lambda a: v8(a, 512), w_in[:, :, 1024 + 512 * i:1024 + 512 * i + 512])]) for i in range(2)]
        Pipe([cgen(1, c, kslots[c // 4]) for c in range(8)], width=2).drain()
        bk = self.scr_buf(seq, "fk", pos_rel)
        S.dma("sp", sc["fk"][:, :, pos_rel:pos_rel + ntok].rearrange("c r t -> r c t"), kst.ap, reads=[kst], writes=[bk])
        P.release(kst)
        qslots = [self.wload([(lambda a: v8(a, 512), w_in[:, :, 512 * i:512 * i + 512])]) for i in range(2)]
        Pipe([cgen(0, c, qslots[c // 4]) for c in range(8)], width=2).drain()
        self.store_tokmajor([knf.c(c) for c in range(8)], [128] * 8, ntok, outs["fk"])
        P.release(knf)
        slot = self.wload([(lambda a: v8(a, 16), w_in[:, :, 3072:3088])])
        sv = v8(slot.ap, 16)
        ps = self.next_ps()
        self.mm(ps, ps.t[0:16, 0:ntok], [(sv[:, k, :], hn.ap[:, k, :]) for k in range(8)], reads=[slot, hn])
        P.release(hn)
        lf = P.alloc([ntok], F32, "lf")
        self.act(lf.ap[0:16, :], ps.t[0:16, 0:ntok], AF.Exp, reads=[ps, self.gv], writes=[lf], bias=self.gv.ap[0:16, g["nbf"]:g["nbf"] + 1], scale=-1.0)
        self.act(lf.ap[0:16, :], lf.ap[0:16, :], AF.Ln, reads=[lf, self.onec], writes=[lf], bias=self.onec.ap[0:16, 0:1])
        self.ts(lf.ap[0:16, :], lf.ap[0:16, :], -1.0, None, ALU.mult, None, reads=[lf], writes=[lf])
        self.store_tokmajor([View(lf.ap[0:16, :], lf.bufs)], [16], ntok, outs["fl"])
        self.fox_cumsum(st, lf, pos_rel, ntok)
        P.release(lf)
        nkeys = pos_rel + ntok
        nkt = -(-nkeys // 128)
        cK = st["cK"]
        nb = self.nbt
        ref_kt = pos_rel // 128 + (2 if ntok >= 384 else 0)
        pr = self.next_ps()
        self.mm(pr, pr.t[:, 0:16], [(self.sel0.ap, cK.ap[:, ref_kt, :])], reads=[self.sel0, cK])
        cref = P.alloc([16], F32, "cref")
        self.copy("dve", cref.ap, pr.t[:, 0:16], reads=[pr], writes=[cref])
        for kt in range(nkt):
            self.tt(nb.ap[:, kt, :], cref.ap, cK.ap[:, kt, :], ALU.subtract, reads=[cref, cK], writes=[nb])
        P.release(cref)
        diag = {}
        if st["kind"] == "prompt":
            for i in range(self.TT // 128):
                diag[pos_rel // 128 + i] = self.maskt.ap[:, 0, :]
        else:
            diag[pos_rel // 128] = self.maskt.ap[0:ntok, 0, :]
        attn = P.alloc([8, ntok], BF16, "fattn")
        skip = {pos_rel // 128 + i: 128 * i for i in range(self.TT // 128)} if st["kind"] == "prompt" else None
        self.attention(seq, "fox", lambda h: View(qn.ap[:, h, :], qn.c(h).bufs), NH_FOX, 128, HD_FOX ** -0.5,
                       nkeys, ntok, diag, lambda kt, h, ksz: nb.ap[0:ksz, kt, h:h + 1],
                       lambda h: (attn.ap[64 * (h % 2):64 * (h % 2) + 64, h // 2, :], attn.c(h // 2)), skip=skip)
        P.release(qn)
        w_out = self.inp["od_w_out"][0].rearrange("(j p) c -> p j c", p=128)
        for ob in range(2):
            slot = self.wload([(lambda a: v8(a, 512), w_out[:, :, 512 * ob:512 * ob + 512])])
            sv = v8(slot.ap, 512)
            for oo in range(4):
                oc = 4 * ob + oo
                ps = self.next_ps()
                self.mm(ps, ps.t[:, 0:ntok], [(sv[:, j, 128 * oo:128 * oo + 128], attn.ap[:, j, :]) for j in range(8)], reads=[slot, attn])
                self.tt(x.ap[:, oc, :], ps.t[:, 0:ntok], x.ap[:, oc, :], ALU.add, reads=[ps, x.c(oc)], writes=[x.c(oc)])
                self.xstat_chunk(x, oc, ntok)
        P.release(attn)

    def fox_cumsum(self, st, lf, pos_rel, ntok):
        P, S, nc = self.P, self.S, self.nc
        cc = st["ccarry"]
        cT = P.alloc([ntok], F32, "cT")
        S.op("dve", lambda: nc.vector.tensor_tensor_scan(out=cT.ap[0:16, :], data0=self.onesf.ap[0:16, 0:ntok], data1=lf.ap[0:16, :],
                                                         initial=cc.ap[0:16, 0:1], op0=ALU.mult, op1=ALU.add),
             reads=[lf, cc, self.onesf], writes=[cT])
        self.copy("dve", cc.ap[0:16, 0:1], cT.ap[0:16, ntok - 1:ntok], reads=[cT], writes=[cc])
        cK = st["cK"]
        nsub = -(-ntok // 128)
        for s in range(nsub):
            n = min(128, ntok - 128 * s)
            ps = self.next_ps()
            self.tr(ps, ps.t[0:n, 0:16], cT.ap[0:16, 128 * s:128 * s + n], 16, reads=[cT])
            self.copy(self.alt(), cK.ap[0:n, pos_rel // 128 + s, :], ps.t[0:n, 0:16], reads=[ps], writes=[cK])
        P.release(cT)

    def load_featmajor(self, src_fn, ntok, F, dst_fn, q="sp"):
        P = self.P
        nsub = -(-ntok // 128)
        nch = -(-F // 128)
        for s in range(nsub):
            n = min(128, ntok - 128 * s)
            stg = P.alloc([F], F32, "istg")
            self.S.dma(q, stg.ap[0:n, 0:F], src_fn(s, n), writes=[stg])
            for c in range(nch):
                rows = min(128, F - 128 * c)
                ps = self.next_ps()
                self.tr(ps, ps.t[0:rows, 0:n], stg.ap[0:n, 128 * c:128 * c + rows], n, reads=[stg])
                dst, dv = dst_fn(c, rows, s, n)
                self.copy(self.alt(), dst, ps.t[0:rows, 0:n], reads=[ps], writes=[dv])
            P.release(stg)

    def mem_kv_prompt(self, si, mst):
        P, S, nc = self.P, self.S, self.nc
        g = self.gcol
        v8 = lambda a, w: a[:, 0:8 * w].rearrange("p (k c) -> p k c", k=8)
        memT = P.alloc([8, MEM_T], F32, "memT")
        self.load_featmajor(lambda s, n: self.inp["memp"][si, 128 * s:128 * s + n, :], MEM_T, D,
                            lambda c, rows, s, n: (memT.ap[:, c, 128 * s:128 * s + n], memT.c(c)))
        self.stage("memT")
        for li in range(2):
            memn = P.alloc([8, MEM_T], BF16, "memn")
            self.rms([memT.c(k) for k in range(8)], 128, self.cmb.ap[:, 0, :], [self.gv.ap[:, g["norm_mem_src"][li] + k:g["norm_mem_src"][li] + k + 1] for k in range(8)],
                     [memn.c(k) for k in range(8)], MEM_T)
            wkv = self.inp["mem_w_kv"][li].rearrange("(k p) c -> p k c", p=128)
            slots = [self.wload([(lambda a: v8(a, 512), wkv[:, :, 512 * b:512 * b + 512])]) for b in range(2)]
            self.stage("memn")
            kf = P.alloc([4, MEM_T], F32, "mkf")
            for h in range(MEM_H):
                sv = v8(slots[h // 2].ap, 512)
                ps = self.next_ps()
                self.mm(ps, ps.t[:, 0:MEM_T], [(sv[:, k, 256 * (h % 2):256 * (h % 2) + 128], memn.ap[:, k, :]) for k in range(8)], reads=[slots[h // 2], memn])
                self.rms([View(ps.t[:, 0:MEM_T], ps.bufs)], 128, self.cmb.ap[:, 2, :], [self.gv.ap[:, g["mem_g_k"][li]:g["mem_g_k"][li] + 1]],
                         [kf.c(h)], MEM_T, outs2=[View(mst["K"].ap[:, li * 4 + h, :], mst["K"].bufs)])
            self.store_tokmajor([kf.c(h) for h in range(4)], [128] * 4, MEM_T, lambda s, n: self.out["o_pmk"][li, si, 128 * s:128 * s + n, :])
            P.release(kf)
            self.stage("memk")
            for s in range(2):
                ps = self.next_ps()
                for hh in range(2):
                    for k in range(8):
                        sv = v8(slots[hh].ap, 512)
                        rhs = sv[:, k, :].rearrange("p (h c) -> p h c", h=2)[:, :, 128:256]
                        self.mm1(ps, ps.t[:, 256 * hh:256 * hh + 256], memn.ap[:, k, 128 * s:128 * s + 128], rhs,
                                 k == 0, k == 7, reads=[slots[hh], memn])
                stg = P.alloc([512], F32, "mvstg")
                self.copy("act", stg.ap, ps.t[:, 0:512], reads=[ps], writes=[stg])
                self.copy("dve", mst["V"].ap[:, li * 2 + s, :], ps.t[:, 0:512], reads=[ps], writes=[mst["V"]])
                self.stage(f"memv{li}{s}a")
                S.dma("sp", self.out["o_pmv"][li, si, 128 * s:128 * s + 128, :], stg.ap, reads=[stg])
                P.release(stg)
                self.stage(f"memv{li}{s}")
            P.release(memn)
        P.release(memT)

    def mem_kv_sample(self, mst):
        P, S = self.P, self.S
        for li in range(2):
            self.load_featmajor(lambda s, n: self.inp["c_mk"][li, 128 * s:128 * s + n, :], MEM_T, 512,
                                lambda c, rows, s, n: (mst["K"].ap[:, li * 4 + c, 128 * s:128 * s + n], mst["K"]))
            S.dma("pool", mst["V"].ap[:, li * 2:li * 2 + 2, :], self.inp["c_mv"][li].rearrange("(s p) c -> p s c", p=128), writes=[mst["V"]])

    def issue_xload(self, x_src_fn, ntok):
        nsub = -(-ntok // 128)
        stg = self.P.alloc([nsub, D], F32, "xstg")
        for s in range(nsub):
            n = min(128, ntok - 128 * s)
            self.S.dma("sp", stg.ap[0:n, s, :], x_src_fn(s, n), writes=[stg.c(s)])
        return stg

    def run_tiles(self, ctxs, prefetch=None, deferred=None, defer_store=False):
        P = self.P
        for c in ctxs:
            ntok, xstg = c["ntok"], c["xstg"]
            x = P.alloc([8, ntok], F32, "x")
            nsub = -(-ntok // 128)
            for ch in range(8):
                ps = self.next_ps()
                for s in range(nsub):
                    n = min(128, ntok - 128 * s)
                    self.tr(ps, ps.t[:, 128 * s:128 * s + n], xstg.ap[0:n, s, 128 * ch:128 * ch + 128], n, reads=[xstg.c(s)])
                self.copy(self.alt(), x.ap[:, ch, :], ps.t[:, 0:ntok], reads=[ps], writes=[x.c(ch)])
                self.xstat_chunk(x, ch, ntok)
            P.release(xstg)
            c["x"] = x
        blocks = [(c["x"], c["ntok"]) for c in ctxs]
        a = lambda c: (c["x"], c["seq"], c["st"], c["pos0"], c["pos_rel"], c["ntok"], c["outs"])
        self.stage("xload")
        self.ffn(blocks, 0, 1, mid=deferred)
        self.stage("ffn1")
        for c in ctxs:
            self.even_mixer(*a(c))
        self.stage("even")
        for c in ctxs:
            if not c["mst"]:
                c["mst"].update({"K": P.alloc([8, MEM_T], BF16, "memK"), "V": P.alloc([4, 512], BF16, "memV")})
                self.mem_kv_sample(c["mst"])
            self.mem_attend(c["x"], 0, c["mst"], c["ntok"])
        self.stage("mem0")
        self.ffn(blocks, 0, 2)
        self.stage("l0")
        self.ffn(blocks, 1, 1)
        for c in ctxs:
            self.odd_mixer(*a(c))
        self.stage("odd")
        for c in ctxs:
            self.mem_attend(c["x"], 1, c["mst"], c["ntok"])
        nxt = prefetch() if prefetch is not None else None
        self.ffn(blocks, 1, 2)

        def store():
            for c in ctxs:
                self.store_tokmajor([c["x"].c(k) for k in range(8)], [128] * 8, c["ntok"], c["outs"]["y"])
                P.release(c["x"])
        if defer_store:
            return nxt, store
        store()
        return nxt, None

    def setup(self):
        nc, S, P = self.nc, self.S, self.P
        inp = self.inp
        self._alt = 0
        self.xstat = {}
        self.ps_rr = 0
        self.w_rr = 0
        self.cmf = P.alloc([8, 128], F32, "cmf")
        S.dma("sp", self.cmf.ap, inp["cmat"].rearrange("m p c -> p m c"), writes=[self.cmf])
        self.cmb = P.alloc([8, 128], BF16, "cmb")
        self.cm = self.cmb
        S.dma("pool", self.cmb.ap, inp["cmat"].rearrange("m p c -> p m c"), writes=[self.cmb])
        self.ident = View(self.cmf.ap[:, 6, :], self.cmf.bufs)
        self.identb = View(self.cmb.ap[:, 6, :], self.cmb.bufs)
        self.sel0 = View(self.cmf.ap[:, 7, :], self.cmf.bufs)
        self.maskt = P.alloc([2, 512], BF16, "maskt")
        S.dma("pool", self.maskt.ap, inp["cmask"].rearrange("m p c -> p m c"), writes=[self.maskt])
        self.zerob = P.alloc([128], BF16, "zerob")
        S.op("dve", lambda: nc.vector.memset(self.zerob.ap, 0.0), writes=[self.zerob])
        self.tix = P.alloc([512], F32, "tix")
        S.dma("sp", self.tix.ap, inp["ctix"], writes=[self.tix])
        self.onesf = P.alloc([512], F32, "onesf")
        S.op("dve", lambda: nc.vector.memset(self.onesf.ap, 1.0), writes=[self.onesf])
        self.ropec = P.alloc([2], F32, "ropec")
        S.dma("sp", self.ropec.ap, inp["crope"], writes=[self.ropec])
        self.epsc = P.alloc([1], F32, "epsc")
        S.op("dve", lambda: nc.vector.memset(self.epsc.ap, EPS), writes=[self.epsc])
        self.npic = P.alloc([1], F32, "npic")
        S.op("dve", lambda: nc.vector.memset(self.npic.ap, -math.pi), writes=[self.npic])
        self.onec = P.alloc([1], F32, "onec")
        S.op("dve", lambda: nc.vector.memset(self.onec.ap, 1.0), writes=[self.onec])
        cols = {}
        ncol = 0

        def take(n):
            nonlocal ncol
            c0 = ncol
            ncol += n
            return c0
        for nm in ("norm_ffn1", "norm_mix", "norm_mem", "norm_mem_src", "norm_ffn2"):
            cols[nm] = [take(8), take(8)]
        cols["ev_g_qlat"] = take(2)
        cols["ev_g_kvlat"] = take(1)
        cols["mem_g_q"] = [take(1), take(1)]
        cols["mem_g_k"] = [take(1), take(1)]
        cols["ev_g_q"] = take(1)
        cols["ev_g_k"] = take(1)
        cols["od_g_q"] = take(1)
        cols["od_g_k"] = take(1)
        cols["conv_w"] = take(16)
        cols["conv_b"] = take(4)
        cols["gb_r"] = take(4)
        cols["gb_i"] = take(4)
        cols["nsp8"] = take(4)
        cols["nbf"] = take(1)
        self.gcol = cols
        self.gv = P.alloc([ncol], F32, "gv")
        gv = self.gv
        S.op("dve", lambda: nc.vector.memset(gv.ap, 0.0), writes=[gv])
        nq = {"allow_slow_non_contiguous": True}

        def col_load(c0, src_1d, n):
            S.dma("sp", gv.ap[:, c0:c0 + n], src_1d.rearrange("(c p) -> p c", p=128), writes=[gv], **nq)

        def rows_load(r0, r1, c0, src_1d):
            S.dma("sp", gv.ap[r0:r1, c0:c0 + 1], src_1d.rearrange("(p o) -> p o", o=1), writes=[gv], **nq)
        for nm in ("norm_ffn1", "norm_mix", "norm_mem", "norm_mem_src", "norm_ffn2"):
            for li in range(2):
                col_load(cols[nm][li], inp[nm][li], 8)
        col_load(cols["ev_g_qlat"], inp["ev_g_qlat"][0], 2)
        col_load(cols["ev_g_kvlat"], inp["ev_g_kvlat"][0], 1)
        for li in range(2):
            col_load(cols["mem_g_q"][li], inp["mem_g_q"][li], 1)
            col_load(cols["mem_g_k"][li], inp["mem_g_k"][li], 1)
        rows_load(0, 96, cols["ev_g_q"], inp["ev_g_q"][0])
        rows_load(0, 96, cols["ev_g_k"], inp["ev_g_k"][0])
        for half in range(2):
            rows_load(64 * half, 64 * half + 64, cols["od_g_q"], inp["od_g_q"][0])
            rows_load(64 * half, 64 * half + 64, cols["od_g_k"], inp["od_g_k"][0])
        for j in range(4):
            col_load(cols["conv_w"] + 4 * j, inp["ev_conv_w"][0, j], 4)
        col_load(cols["conv_b"], inp["ev_conv_b"][0], 4)
        for c in range(4):
            for half in range(2):
                rows_load(64 * half, 64 * half + 64, cols["gb_r"] + c, inp["ev_gate_b"][0, 2 * c + half, 0:64])
                rows_load(64 * half, 64 * half + 64, cols["gb_i"] + c, inp["ev_gate_b"][0, 2 * c + half, 64:128])
        col_load(cols["nsp8"], inp["ev_lambda"][0], 4)
        rows_load(0, 16, cols["nbf"], inp["od_b_f"][0])
        c0 = cols["nsp8"]
        self.act(gv.ap[:, c0:c0 + 4], gv.ap[:, c0:c0 + 4], AF.Exp, reads=[gv], writes=[gv], scale=-1.0)
        self.act(gv.ap[:, c0:c0 + 4], gv.ap[:, c0:c0 + 4], AF.Ln, reads=[gv], writes=[gv], bias=self.onec.ap[:, 0:1])
        self.ts(gv.ap[:, c0:c0 + 4], gv.ap[:, c0:c0 + 4], -8.0, None, ALU.mult, None, reads=[gv], writes=[gv])
        c0 = cols["nbf"]
        self.ts(gv.ap[0:16, c0:c0 + 1], gv.ap[0:16, c0:c0 + 1], -1.0, None, ALU.mult, None, reads=[gv], writes=[gv])
        self.wuq = P.alloc([2, 768], BF16, "wuq")
        self.wuqs = P.alloc([2, 768], BF16, "wuqs")
        wuq_src = inp["ev_w_uq"][0].rearrange("(k p) c -> p k c", p=128)
        S.dma("pool", self.wuq.ap, wuq_src, writes=[self.wuq])
        src4 = inp["ev_w_uq"][0].rearrange("(k p) (h d) -> p k h d", p=128, d=DQK)
        dst4 = self.wuqs.ap.rearrange("p k (h d) -> p k h d", d=DQK)
        for k in range(2):
            S.dma("pool", dst4[:, k, :, 0:64], src4[:, k, :, 0:64], writes=[self.wuqs])
            S.dma("pool", dst4[:, k, :, 64:80], src4[:, k, :, 80:96], writes=[self.wuqs])
            S.dma("pool", dst4[:, k, :, 80:96], src4[:, k, :, 64:80], writes=[self.wuqs])
        self.wukv = P.alloc([1024], BF16, "wukv")
        S.dma("pool", self.wukv.ap, inp["ev_w_ukv"][0], writes=[self.wukv])
        self.gw = P.alloc([8, 128], BF16, "gw")
        S.op("dve", lambda: nc.vector.memset(self.gw.ap, 0.0), writes=[self.gw])
        for c in range(4):
            for half in range(2):
                n = 2 * c + half
                S.dma("pool", self.gw.ap[64 * half:64 * half + 64, c, 64 * half:64 * half + 64], inp["ev_gate_w"][0, n, :, 0:64], writes=[self.gw])
                S.dma("pool", self.gw.ap[64 * half:64 * half + 64, 4 + c, 64 * half:64 * half + 64], inp["ev_gate_w"][0, n, :, 64:128], writes=[self.gw])
        self.wslots = [P.alloc([4096], BF16, f"wslot{i}") for i in range(self.cfg.get("NWSLOT", 6))]
        nktmax = -(-max(self.SEQ, self.PAST + self.DEC) // 128)
        self.nbt = P.alloc([nktmax, 16], F32, "nb")

    def new_seq_state(self, kind, with_mem=True):
        nc, S, P = self.nc, self.S, self.P
        nktmax = -(-max(self.SEQ, self.PAST + self.DEC) // 128)
        st = {"kind": kind}
        st["U"] = P.alloc([4, (self.TT if kind == "prompt" else self.DEC) + 3], F32, "U")
        st["hcarry"] = P.alloc([4], F32, "hcarry")
        st["ccarry"] = P.alloc([1], F32, "ccarry")
        st["cK"] = P.alloc([nktmax, 16], F32, "cK")
        S.op("dve", lambda: nc.vector.memset(st["U"].ap, 0.0), writes=[st["U"]])
        S.op("dve", lambda: nc.vector.memset(st["hcarry"].ap, 0.0), writes=[st["hcarry"]])
        S.op("dve", lambda: nc.vector.memset(st["ccarry"].ap, 0.0), writes=[st["ccarry"]])
        S.op("dve", lambda: nc.vector.memset(st["cK"].ap, 0.0), writes=[st["cK"]])
        mst = {"K": P.alloc([8, MEM_T], BF16, "memK"), "V": P.alloc([4, 512], BF16, "memV")} if with_mem else {}
        return st, mst

    def free_seq_state(self, st, mst):
        self.P.release(st["U"], st["hcarry"], st["ccarry"], st["cK"], mst["K"], mst["V"])

    def build(self):
        self.nc = nc = bass.Bass("TRN2", target_bir_lowering=False)
        self.declare()
        nq = {"allow_slow_non_contiguous": True}
        with contextlib.ExitStack() as es:
            self.S = S = Sched(nc, es)
            self.P = P = Pool(nc, es, self.cfg.get("NPAGES", 206))
            pst = [es.enter_context(nc.psum_tensor(f"ps{i}", [128, 512], F32)) for i in range(8)]
            self.psg = [PS(pst[i], f"ps{i}") for i in range(6)]
            self.pso = [PS(pst[6], "pso0"), PS(pst[7], "pso1")]
            try:
                self.setup()
                self.stage("setup")
                self.body()
            except _Stop:
                pass
            S.finish()
            self.stats = dict(ops=S.n_ops, waits=S.n_waits, ticks=dict(S.tick), peak_pages=P.peak)
        return nc

    def body(self):
        if True:
            nc, S, P = self.nc, self.S, self.P
            nq = {"allow_slow_non_contiguous": True}
            TT = self.TT
            out = self.out
            for si in range(self.NSEQ):
                st, mst = self.new_seq_state("prompt")
                self.mem_kv_prompt(si, mst)
                self.stage("memkv")
                ntile = self.SEQ // TT
                xsrc = lambda p0: (lambda s, n: self.inp["xp"][si, p0 + 128 * s:p0 + 128 * s + n, :])
                xstg = self.issue_xload(xsrc(0), TT)
                dstore = None
                for ti in range(ntile):
                    p0 = ti * TT
                    outs = {
                        "y": lambda s, n, p0=p0: out["o_yp"][si, p0 + 128 * s:p0 + 128 * s + n, :],
                        "lat": lambda s, n, p0=p0: out["o_plat"][si, p0 + 128 * s:p0 + 128 * s + n, :],
                        "kr": lambda s, n, p0=p0: out["o_pkr"][si, p0 + 128 * s:p0 + 128 * s + n, :],
                        "fk": lambda s, n, p0=p0: out["o_pfk"][si, p0 + 128 * s:p0 + 128 * s + n, :],
                        "fv": lambda s, n, p0=p0: out["o_pfv"][si, p0 + 128 * s:p0 + 128 * s + n, :],
                        "fl": lambda s, n, p0=p0: out["o_pfl"][si, p0 + 128 * s:p0 + 128 * s + n, :],
                    }
                    pf = (lambda p1=p0 + TT: self.issue_xload(xsrc(p1), TT)) if ti + 1 < ntile else None
                    ctxs = [dict(seq=si, st=st, mst=mst, xstg=xstg, pos0=p0, pos_rel=p0, ntok=TT, outs=outs)]
                    last = (si == self.NSEQ - 1 and ti == ntile - 1 and self.DEC > 0)
                    if last:
                        sctx = self.sample_prepare()
                        ctxs.append(sctx)
                    xstg, dstore = self.run_tiles(ctxs, prefetch=pf, deferred=(dstore if ti > 0 else None), defer_store=(ti + 1 < ntile))
                    if last:
                        self.sample_finish(sctx)
                S.dma("sp", out["o_ph"][si].rearrange("(c p) -> p c", p=128), st["hcarry"].ap, reads=[st["hcarry"]], **nq)
                for c in range(4):
                    S.dma("sp", out["o_pconv"][si][:, 128 * c:128 * c + 128].rearrange("j p -> p j"), st["U"].ap[:, c, 0:3], reads=[st["U"]], **nq)
                self.free_seq_state(st, mst)

    def sample_prepare(self):
        nc, S, P = self.nc, self.S, self.P
        inp, out = self.inp, self.out
        nq = {"allow_slow_non_contiguous": True}
        seq = self.NSEQ
        PAST, DEC = self.PAST, self.DEC
        st, mst = self.new_seq_state("sample", with_mem=False)
        S.dma("sp", st["hcarry"].ap, inp["s_h"].rearrange("(c p) -> p c", p=128), writes=[st["hcarry"]], **nq)
        for c in range(4):
            S.dma("sp", st["U"].ap[:, c, 0:3], inp["s_conv"][:, 128 * c:128 * c + 128].rearrange("j p -> p j"), writes=[st["U"]], **nq)
        sc = self.scr[seq]
        CH = 512
        for p0 in range(0, PAST, CH):
            n = min(CH, PAST - p0)
            latb = P.alloc([n], BF16, "platb")
            kr = P.alloc([n], F32, "pkr")
            self.load_featmajor(lambda s, m: inp["c_lat"][p0 + 128 * s:p0 + 128 * s + m, :], n, KVL,
                                lambda c, rows, s, m: (latb.ap[:, 128 * s:128 * s + m], latb.v()))
            self.load_featmajor(lambda s, m: inp["c_kr"][p0 + 128 * s:p0 + 128 * s + m, :], n, ROPE,
                                lambda c, rows, s, m: (kr.ap[64:96, 128 * s:128 * s + m], kr.v()))
            self.mla_kv_expand(seq, latb, kr, p0, n)
            P.release(latb, kr)
            kst = P.alloc([8, n], BF16, "pfk")
            self.load_featmajor(lambda s, m: inp["c_fk"][p0 + 128 * s:p0 + 128 * s + m, :], n, 1024,
                                lambda c, rows, s, m: (kst.ap[:, c, 128 * s:128 * s + m], kst.c(c)))
            S.dma("sp", sc["fk"][:, :, p0:p0 + n].rearrange("c r t -> r c t"), kst.ap, reads=[kst], writes=[self.scr_buf(seq, "fk", p0)])
            P.release(kst)
            nsub = n // 128
            vst = P.alloc([nsub, NH_FOX * 65], BF16, "pfv")
            S.op("dve", lambda: nc.vector.memset(vst.ap, 1.0), writes=[vst])
            for s in range(nsub):
                stg = P.alloc([1024], F32, "pvstg")
                S.dma("sp", stg.ap, inp["c_fv"][p0 + 128 * s:p0 + 128 * s + 128, :], writes=[stg])
                self.copy(self.alt(), vst.ap[:, s, :].rearrange("p (h c) -> p h c", h=NH_FOX)[:, :, 0:64], stg.ap.rearrange("p (h c) -> p h c", h=NH_FOX),
                          reads=[stg], writes=[vst])
                P.release(stg)
            S.dma("sp", sc["fv"][p0 // 128:p0 // 128 + nsub].rearrange("k p c -> p k c"), vst.ap, reads=[vst], writes=[self.scr_buf(seq, "fv", p0)])
            P.release(vst)
            lf = P.alloc([n], F32, "plf")
            self.load_featmajor(lambda s, m: inp["c_fl"][p0 + 128 * s:p0 + 128 * s + m, :], n, NH_FOX,
                                lambda c, rows, s, m: (lf.ap[0:16, 128 * s:128 * s + m], lf.v()))
            self.fox_cumsum(st, lf, p0, n)
            P.release(lf)
        outs = {
            "y": lambda s, n: out["o_ys"][0:n, :],
            "lat": lambda s, n: out["o_slat"][0:n, :],
            "kr": lambda s, n: out["o_skr"][0:n, :],
            "fk": lambda s, n: out["o_sfk"][0:n, :],
            "fv": lambda s, n: out["o_sfv"][0:n, :],
            "fl": lambda s, n: out["o_sfl"][0:n, :],
        }
        xstg = self.issue_xload(lambda s, n: inp["xs"][0:n, :], DEC)
        return dict(seq=seq, st=st, mst=mst, xstg=xstg, pos0=PAST, pos_rel=PAST, ntok=DEC, outs=outs)

    def sample_finish(self, ctx):
        nc, S, P = self.nc, self.S, self.P
        out = self.out
        nq = {"allow_slow_non_contiguous": True}
        st, mst = ctx["st"], ctx["mst"]
        S.dma("sp", out["o_sh"].rearrange("(c p) -> p c", p=128), st["hcarry"].ap, reads=[st["hcarry"]], **nq)
        for c in range(4):
            S.dma("sp", out["o_sconv"][:, 128 * c:128 * c + 128].rearrange("j p -> p j"), st["U"].ap[:, c, 0:3], reads=[st["U"]], **nq)
        self.free_seq_state(st, mst)


def make_consts(TT):
    cm = np.zeros((8, 128, 128), np.float32)
    cm[0] = 1.0 / 1024
    cm[1] = 1.0 / 256
    cm[2] = 1.0 / 128
    cm[3, :96, :96] = 1.0 / 96
    cm[4, :64, :64] = 1.0 / 64
    cm[4, 64:, 64:] = 1.0 / 64
    cm[5] = 1.0
    cm[6] = np.eye(128, dtype=np.float32)
    cm[7, 0, :] = 1.0
    mask = np.zeros((2, 128, 512), np.float32)
    kk = np.arange(128)[:, None]
    qq = np.arange(512)[None, :]
    mask[0] = np.where(kk <= qq, 0.0, NEG)
    mask[1] = np.where(kk // 64 <= qq // 64, 0.0, NEG)
    tix = np.broadcast_to(np.arange(512, dtype=np.float32)[None, :], (128, 512)).copy()
    rope = np.zeros((128, 2), np.float32)
    half = ROPE // 2
    invf = (10000.0 ** (-np.arange(half, dtype=np.float32) / half)).astype(np.float32)
    rope[64:80, 0] = invf
    rope[80:96, 0] = invf
    rope[64:80, 1] = -1.0
    rope[80:96, 1] = 1.0
    return {"cmat": cm, "cmask": mask, "ctix": tix, "crope": rope}


FULL_CFG = dict(SEQ=4096, NSEQ=2, TT=512, DEC=16, PAST=2048)
_W_NAMES = ["norm_ffn1", "ffn1_w_in", "ffn1_w_out", "norm_mix", "norm_mem", "norm_mem_src", "mem_w_q", "mem_w_kv", "mem_w_o",
            "mem_g_q", "mem_g_k", "norm_ffn2", "ffn2_w_in", "ffn2_w_out", "ev_w_in", "ev_g_qlat", "ev_g_kvlat", "ev_w_uq", "ev_w_ukv",
            "ev_g_q", "ev_g_k", "ev_conv_w", "ev_conv_b", "ev_gate_w", "ev_gate_b", "ev_lambda", "ev_w_out", "od_w_in", "od_b_f",
            "od_g_q", "od_g_k", "od_w_out"]


def run(cfg, inputs, n_cores=8):
    f = lambda a: np.ascontiguousarray(np.asarray(a, dtype=np.float32))
    mk = MK(cfg)
    nc = mk.build()
    NSEQ, DEC = cfg["NSEQ"], cfg["DEC"]
    consts = make_consts(cfg["TT"])
    shared = {n: f(inputs[n]) for n in _W_NAMES}
    shared.update(consts)
    in_maps = []
    for i in range(n_cores):
        m = dict(shared)
        m["xp"] = f(inputs["x_prompt"][NSEQ * i:NSEQ * (i + 1)])
        m["memp"] = f(inputs["mem_prompt"][NSEQ * i:NSEQ * (i + 1)])
        m["xs"] = f(inputs["x_sample"][i])
        m["c_lat"] = f(inputs["cache_mla_latent"][0, i])
        m["c_kr"] = f(inputs["cache_mla_krope"][0, i])
        m["s_h"] = f(inputs["state_lru_h"][0, i])
        m["s_conv"] = f(inputs["state_lru_conv"][0, i])
        m["c_fk"] = f(np.asarray(inputs["cache_fox_k"])[0, i].reshape(-1, 1024))
        m["c_fv"] = f(np.asarray(inputs["cache_fox_v"])[0, i].reshape(-1, 1024))
        m["c_fl"] = f(inputs["cache_fox_logf"][0, i])
        m["c_mk"] = f(np.asarray(inputs["cache_mem_k"])[:, i].reshape(2, MEM_T, 512))
        m["c_mv"] = f(np.asarray(inputs["cache_mem_v"])[:, i].reshape(2, MEM_T, 512))
        in_maps.append(m)
    res = run_bass_kernel_spmd(nc, in_maps, core_ids=list(range(n_cores)))
    R = res.results
    cat = lambda k: np.concatenate([r[k] for r in R], axis=0)
    stk = lambda k: np.stack([r[k] for r in R], axis=0)
    B = NSEQ * n_cores
    SEQ = cfg["SEQ"]
    outs = (
        cat("o_yp"),
        stk("o_ys"),
        cat("o_plat")[None],
        cat("o_pkr")[None],
        cat("o_ph")[None],
        cat("o_pconv")[None],
        cat("o_pfk").reshape(1, B, SEQ, NH_FOX, HD_FOX),
        cat("o_pfv").reshape(1, B, SEQ, NH_FOX, HD_FOX),
        cat("o_pfl")[None],
        np.concatenate([r["o_pmk"] for r in R], axis=1).reshape(2, B, MEM_T, MEM_H, MEM_HD),
        np.concatenate([r["o_pmv"] for r in R], axis=1).reshape(2, B, MEM_T, MEM_H, MEM_HD),
        stk("o_slat")[None],
        stk("o_skr")[None],
        stk("o_sh")[None],
        stk("o_sconv")[None],
        stk("o_sfk").reshape(1, n_cores, DEC, NH_FOX, HD_FOX),
        stk("o_sfv").reshape(1, n_cores, DEC, NH_FOX, HD_FOX),
        stk("o_sfl")[None],
    )
    return tuple(np.ascontiguousarray(o, dtype=np.float32) for o in outs), mk


def kernel(**inputs):
    outs, _ = run(FULL_CFG, inputs)
    return outs
```

```python
import contextlib
import math
import numpy as np
import concourse.bass as bass
import concourse.mybir as mybir
from concourse.bass_utils import run_bass_kernel_spmd

F32 = mybir.dt.float32
BF16 = mybir.dt.bfloat16
AF = mybir.ActivationFunctionType
ALU = mybir.AluOpType

EPOCH = 16384
PAGE = 256
NEG = -30000.0
EPS = 1e-6

D = 1024
FFN = 2816
NH_MLA, NOPE, ROPE, DQK, DV = 8, 64, 32, 96, 64
QL, KVL = 256, 128
LW = 512
NH_FOX, HD_FOX = 16, 64
MEM_T, MEM_H, MEM_HD = 256, 4, 128
EVEN_IN = 1440
ODD_IN = 3088


class Buf:
    __slots__ = ("name", "w", "r", "excl")

    def __init__(self, name, excl=False):
        self.name = name
        self.w = None
        self.r = {}
        self.excl = excl


class View:
    __slots__ = ("ap", "bufs")

    def __init__(self, ap, bufs):
        self.ap = ap
        self.bufs = bufs


def _flat(xs):
    out = []
    for x in xs:
        if x is None:
            continue
        if isinstance(x, Buf):
            out.append(x)
        elif isinstance(x, (list, tuple)):
            out.extend(_flat(x))
        else:
            out.extend(x.bufs)
    return out


class Sched:
    def __init__(self, nc, es, n_dma_sems=14):
        self.nc = nc
        self.es = es
        self.eng = {"pe": nc.tensor, "act": nc.scalar, "dve": nc.vector, "pool": nc.gpsimd, "sp": nc.sync}
        self.compute = ("pe", "act", "dve", "pool")
        self.tick = {e: 0 for e in self.compute}
        self.esems = {e: [] for e in self.compute}
        self.dq = {}
        for q in ("sp", "pool"):
            sems = [es.enter_context(nc.semaphore(f"dq_{q}_{i}")) for i in range(n_dma_sems)]
            self.dq[q] = {"sems": sems, "val": [0] * n_dma_sems, "next": 0}
        self.waited = {e: {} for e in ("pe", "act", "dve", "pool", "sp")}
        self.n_waits = 0
        self.n_ops = 0

    def _sem_for(self, e, tick):
        ep = (tick - 1) // EPOCH
        while len(self.esems[e]) <= ep:
            self.esems[e].append(self.es.enter_context(self.nc.semaphore(f"e_{e}_{len(self.esems[e])}")))
        return self.esems[e][ep], tick - ep * EPOCH

    def _wait(self, cons, tok):
        if tok is None:
            return
        if tok[0] == "c":
            _, e, t = tok
            if e == cons and e == "pe":
                return
            key = ("c", e)
            if self.waited[cons].get(key, 0) >= t:
                return
            self.waited[cons][key] = t
            sem, v = self._sem_for(e, t)
            self.eng[cons].wait_ge(sem, v)
            self.n_waits += 1
        else:
            _, q, i, v = tok
            key = ("d", q, i)
            if self.waited[cons].get(key, 0) >= v:
                return
            self.waited[cons][key] = v
            self.eng[cons].wait_ge(self.dq[q]["sems"][i], v)
            self.n_waits += 1

    def _deps(self, cons, reads, writes):
        for b in reads:
            self._wait(cons, b.w)
        for b in writes:
            self._wait(cons, b.w)
            for tok in b.r.values():
                self._wait(cons, tok)

    def _commit(self, tok, reads, writes):
        key = ("c", tok[1]) if tok[0] == "c" else ("d", tok[1], tok[2])
        for b in reads:
            b.r[key] = tok
        for b in writes:
            b.w = tok
            b.r = {}

    def op(self, e, fn, reads=(), writes=()):
        reads = _flat(reads)
        writes = _flat(writes)
        ex = [b for b in reads if b.excl]
        if ex:
            reads = [b for b in reads if not b.excl]
            writes = writes + [b for b in ex if b not in writes]
        self._deps(e, reads, writes)
        ins = fn()
        self.tick[e] += 1
        t = self.tick[e]
        sem, _ = self._sem_for(e, t)
        ins.then_inc(sem, 1)
        tok = ("c", e, t)
        self._commit(tok, reads, writes)
        self.n_ops += 1
        return tok

    def dma(self, q, out, in_, reads=(), writes=(), **kw):
        reads = _flat(reads)
        writes = _flat(writes)
        d = self.dq[q]
        i = d["next"]
        d["next"] = (i + 1) % len(d["sems"])
        if d["val"][i] > 0:
            self._wait(q, ("d", q, i, d["val"][i]))
        self._deps(q, reads, writes)
        d["val"][i] += 16
        self.eng[q].dma_start(out=out, in_=in_, **kw).then_inc(d["sems"][i], 16)
        tok = ("d", q, i, d["val"][i])
        self._commit(tok, reads, writes)
        self.n_ops += 1
        return tok

    def finish(self):
        for q, d in self.dq.items():
            for i, v in enumerate(d["val"]):
                if v:
                    self._wait("sp", ("d", q, i, v))


class Tile:
    def __init__(self, pool, p0, npg, ap, nch, chunk_bytes, name):
        self.pool, self.p0, self.npg, self.ap = pool, p0, npg, ap
        self.nch, self.chunk_bytes, self.name = nch, chunk_bytes, name
        self.bufs = pool.bufs[p0:p0 + npg]

    def c(self, i, rows=None):
        lo = (i * self.chunk_bytes) // (PAGE * 4)
        hi = ((i + 1) * self.chunk_bytes - 1) // (PAGE * 4)
        ap = self.ap[:, i] if rows is None else self.ap[rows[0]:rows[1], i]
        return View(ap, self.bufs[lo:hi + 1])

    def v(self, ap=None):
        return View(self.ap if ap is None else ap, self.bufs)


class Pool:
    def __init__(self, nc, es, npages):
        self.t = es.enter_context(nc.sbuf_tensor("arena", [128, npages * PAGE], F32))
        self.npages = npages
        self.free = [True] * npages
        self.bufs = [Buf(f"pg{i}") for i in range(npages)]
        self.peak = 0

    def alloc(self, shape, dtype, name=""):
        n = int(np.prod(shape))
        eb = 4 if dtype == F32 else 2
        npg = -(-(n * eb) // (PAGE * 4))
        p0 = None
        run = 0
        for i in range(self.npages):
            run = run + 1 if self.free[i] else 0
            if run == npg:
                p0 = i - npg + 1
                break
        if p0 is None:
            raise RuntimeError(f"SBUF pool exhausted allocating {name} {shape} ({npg} pages); free={sum(self.free)}")
        for i in range(p0, p0 + npg):
            self.free[i] = False
        self.peak = max(self.peak, p0 + npg)
        ap32 = self.t[:, p0 * PAGE:(p0 + npg) * PAGE]
        ap = ap32 if dtype == F32 else ap32.bitcast(BF16)
        ap = ap[:, 0:n]
        if len(shape) == 2:
            ap = ap.rearrange("p (c t) -> p c t", c=shape[0])
            nch, cb = shape[0], shape[1] * eb
        elif len(shape) == 3:
            ap = ap.rearrange("p (c a t) -> p c a t", c=shape[0], a=shape[1])
            nch, cb = shape[0], shape[1] * shape[2] * eb
        else:
            nch, cb = 1, n * eb
        return Tile(self, p0, npg, ap, nch, cb, name)

    def release(self, *tiles):
        for t in tiles:
            for i in range(t.p0, t.p0 + t.npg):
                assert not self.free[i]
                self.free[i] = True


class Pipe:
    def __init__(self, gens, width=3):
        self.pending = list(gens)
        self.active = []
        self.width = width

    def step(self):
        if self.pending and len(self.active) < self.width:
            self.active.append(self.pending.pop(0))
        for g in list(self.active):
            try:
                next(g)
            except StopIteration:
                self.active.remove(g)
        return bool(self.pending or self.active)

    def drain(self):
        while self.step():
            pass


class PS:
    def __init__(self, t, name):
        self.t = t
        self.buf = Buf(name, excl=True)
        self.bufs = [self.buf]
        self.held = False


class _Stop(Exception):
    pass


class MK:
    def stage(self, name):
        if self.cfg.get('STOP') == name:
            raise _Stop()

    def __init__(self, cfg):
        self.cfg = cfg
        self.SEQ, self.NSEQ, self.TT, self.DEC, self.PAST = cfg["SEQ"], cfg["NSEQ"], cfg["TT"], cfg["DEC"], cfg["PAST"]

    def declare(self):
        nc = self.nc
        c = self
        SEQ, NSEQ, DEC, PAST = c.SEQ, c.NSEQ, c.DEC, c.PAST
        I = lambda n, s: nc.dram_tensor(n, list(s), F32, kind="ExternalInput").ap()
        O = lambda n, s: nc.dram_tensor(n, list(s), F32, kind="ExternalOutput").ap()
        self.inp = {}
        for n, s in self.input_shapes().items():
            self.inp[n] = I(n, s)
        self.out = {}
        for n, s in self.output_shapes().items():
            self.out[n] = O(n, s)
        LP = SEQ
        LS = PAST + DEC
        self.scr = []
        for s in range(NSEQ + 1):
            L = LP if s < NSEQ else LS
            nkt = -(-L // 128)
            d = {
                "mk": nc.dram_tensor(f"scr_mk{s}", [NH_MLA, DQK, L], BF16, kind="Internal").ap(),
                "mv": nc.dram_tensor(f"scr_mv{s}", [nkt, 128, NH_MLA * 65], BF16, kind="Internal").ap(),
                "fk": nc.dram_tensor(f"scr_fk{s}", [8, 128, L], BF16, kind="Internal").ap(),
                "fv": nc.dram_tensor(f"scr_fv{s}", [nkt, 128, NH_FOX * 65], BF16, kind="Internal").ap(),
                "bufs": {},
            }
            self.scr.append(d)

    def input_shapes(self):
        c = self
        sh = {
            "xp": (c.NSEQ, c.SEQ, D), "xs": (c.DEC, D), "memp": (c.NSEQ, MEM_T, D),
            "c_lat": (c.PAST, KVL), "c_kr": (c.PAST, ROPE), "s_h": (LW,), "s_conv": (3, LW),
            "c_fk": (c.PAST, 1024), "c_fv": (c.PAST, 1024), "c_fl": (c.PAST, NH_FOX),
            "c_mk": (2, MEM_T, 512), "c_mv": (2, MEM_T, 512),
            "norm_ffn1": (2, D), "ffn1_w_in": (2, D, 2 * FFN), "ffn1_w_out": (2, FFN, D),
            "norm_mix": (2, D), "norm_mem": (2, D), "norm_mem_src": (2, D),
            "mem_w_q": (2, D, 512), "mem_w_kv": (2, D, 1024), "mem_w_o": (2, 512, D),
            "mem_g_q": (2, 128), "mem_g_k": (2, 128), "norm_ffn2": (2, D),
            "ffn2_w_in": (2, D, 2 * FFN), "ffn2_w_out": (2, FFN, D),
            "ev_w_in": (1, D, EVEN_IN), "ev_g_qlat": (1, QL), "ev_g_kvlat": (1, KVL),
            "ev_w_uq": (1, QL, NH_MLA * DQK), "ev_w_ukv": (1, KVL, 1024), "ev_g_q": (1, DQK), "ev_g_k": (1, DQK),
            "ev_conv_w": (1, 4, LW), "ev_conv_b": (1, LW), "ev_gate_w": (1, 8, 64, 128), "ev_gate_b": (1, 8, 128),
            "ev_lambda": (1, LW), "ev_w_out": (1, 1024, D),
            "od_w_in": (1, D, ODD_IN), "od_b_f": (1, NH_FOX), "od_g_q": (1, 64), "od_g_k": (1, 64),
            "od_w_out": (1, 1024, D),
            "cmat": (8, 128, 128), "cmask": (2, 128, 512), "ctix": (128, 512), "crope": (128, 2),
        }
        return sh

    def output_shapes(self):
        c = self
        return {
            "o_yp": (c.NSEQ, c.SEQ, D), "o_ys": (c.DEC, D),
            "o_plat": (c.NSEQ, c.SEQ, KVL), "o_pkr": (c.NSEQ, c.SEQ, ROPE), "o_ph": (c.NSEQ, LW),
            "o_pconv": (c.NSEQ, 3, LW), "o_pfk": (c.NSEQ, c.SEQ, 1024), "o_pfv": (c.NSEQ, c.SEQ, 1024),
            "o_pfl": (c.NSEQ, c.SEQ, NH_FOX), "o_pmk": (2, c.NSEQ, MEM_T, 512), "o_pmv": (2, c.NSEQ, MEM_T, 512),
            "o_slat": (c.DEC, KVL), "o_skr": (c.DEC, ROPE), "o_sh": (LW,), "o_sconv": (3, LW),
            "o_sfk": (c.DEC, 1024), "o_sfv": (c.DEC, 1024), "o_sfl": (c.DEC, NH_FOX),
        }

    def next_ps(self, hold=False):
        n = len(self.psg)
        for _ in range(n):
            i = self.ps_rr
            self.ps_rr = (i + 1) % n
            if not self.psg[i].held:
                self.psg[i].held = hold
                return self.psg[i]
        raise RuntimeError("all PSUM banks held")

    def ps_release(self, *pss):
        for p in pss:
            p.held = False

    def mm(self, ps, out_ap, terms, reads):
        nc = self.nc

        def f():
            n = len(terms)
            ins = None
            for i, (l, r) in enumerate(terms):
                ins = nc.tensor.matmul(out_ap, l, r, start=(i == 0), stop=(i == n - 1))
            return ins
        return self.S.op("pe", f, reads=reads, writes=[ps])

    def mm1(self, ps, out_ap, l, r, start, stop, reads):
        nc = self.nc
        return self.S.op("pe", lambda: nc.tensor.matmul(out_ap, l, r, start=start, stop=stop), reads=reads, writes=[ps])

    def tr(self, ps, out_ap, in_ap, k, reads):
        nc = self.nc
        b = in_ap.base_partition()
        return self.S.op("pe", lambda: nc.tensor.transpose(out_ap, in_ap, self.ident.ap[b:b + k, b:b + k]),
                         reads=list(reads) + [self.ident], writes=[ps])

    def act(self, out, in_, func, reads, writes, bias=None, scale=None):
        nc = self.nc
        kw = {}
        if bias is not None:
            kw["bias"] = bias
        if scale is not None:
            kw["scale"] = scale
        return self.S.op("act", lambda: nc.scalar.activation(out=out, in_=in_, func=func, **kw), reads=reads, writes=writes)

    def copy(self, eng, out, in_, reads, writes):
        nc = self.nc
        if eng == "act":
            return self.S.op("act", lambda: nc.scalar.copy(out=out, in_=in_), reads=reads, writes=writes)
        return self.S.op("dve", lambda: nc.vector.tensor_copy(out=out, in_=in_), reads=reads, writes=writes)

    def alt(self):
        self._alt ^= 1
        return "act" if self._alt else "dve"

    def ts(self, out, in0, s1, s2, op0, op1, reads, writes):
        nc = self.nc
        if op1 is None:
            return self.S.op("dve", lambda: nc.vector.tensor_scalar(out=out, in0=in0, scalar1=s1, scalar2=None, op0=op0), reads=reads, writes=writes)
        return self.S.op("dve", lambda: nc.vector.tensor_scalar(out=out, in0=in0, scalar1=s1, scalar2=s2, op0=op0, op1=op1), reads=reads, writes=writes)

    def stt(self, out, in0, scalar, in1, op0, op1, reads, writes):
        nc = self.nc
        return self.S.op("dve", lambda: nc.vector.scalar_tensor_tensor(out=out, in0=in0, scalar=scalar, in1=in1, op0=op0, op1=op1), reads=reads, writes=writes)

    def tt(self, out, in0, in1, op, reads, writes):
        nc = self.nc
        return self.S.op("dve", lambda: nc.vector.tensor_tensor(out=out, in0=in0, in1=in1, op=op), reads=reads, writes=writes)

    def wload(self, parts):
        i = self.w_rr
        self.w_rr = (self.w_rr + 1) % len(self.wslots)
        slot = self.wslots[i]
        for dst_fn, src in parts:
            self.S.dma("pool", dst_fn(slot.ap), src, writes=[slot])
        return slot

    def xstat_chunk(self, x, oc, ntok):
        P = self.P
        key = id(x)
        if oc == 0:
            self.xstat[key] = dict(ps=self.next_ps(hold=True), pend=None)
        d = self.xstat[key]
        sq = P.alloc([ntok], BF16, "sq")
        self.act(sq.ap, x.ap[:, oc, :], AF.Square, reads=[x.c(oc)], writes=[sq])
        if d["pend"] is not None:
            self._xstat_mm(d, ntok)
        d["pend"] = (oc, sq)

    def _xstat_mm(self, d, ntok):
        oc, sq = d["pend"]
        ps = d["ps"]
        self.mm1(ps, ps.t[:, 0:ntok], self.cmb.ap[:, 0, :], sq.ap, oc == 0, oc == 7, reads=[sq, self.cm])
        self.P.release(sq)
        d["pend"] = None

    def rms_x(self, x, gname, li, outs, ntok):
        g0 = self.gcol[gname][li]
        d = self.xstat.pop(id(x), None)
        pre = None
        if d is not None:
            if d["pend"] is not None:
                self._xstat_mm(d, ntok)
            pre = d["ps"]
        for _ in self.rms_g([x.c(k) for k in range(8)], 128, self.cmb.ap[:, 0, :], [self.gv.ap[:, g0 + k:g0 + k + 1] for k in range(8)],
                            outs, ntok, pre=pre):
            pass

    def rms_g(self, srcs, rows, ones_ap, gcols, outs, ntok, outs2=None, custom=None, pre=None):
        P, S = self.P, self.S
        if pre is not None:
            ps = pre
        else:
            n = len(srcs)
            if n == 1:
                sq = P.alloc([ntok], BF16, "sq")
                self.act(sq.ap[0:rows, :], srcs[0].ap, AF.Square, reads=[srcs[0]], writes=[sq])
                yield
                ps = self.next_ps(hold=True)
                self.mm1(ps, ps.t[0:rows, 0:ntok], ones_ap, sq.ap[0:rows, :], True, True, reads=[sq, self.cm])
                P.release(sq)
            else:
                ps = self.next_ps(hold=True)
                for c, v in enumerate(srcs):
                    sq = P.alloc([ntok], BF16, "sq")
                    self.act(sq.ap[0:rows, :], v.ap, AF.Square, reads=[v], writes=[sq])
                    self.mm1(ps, ps.t[0:rows, 0:ntok], ones_ap, sq.ap[0:rows, :], c == 0, c == n - 1, reads=[sq, self.cm])
                    P.release(sq)
            yield
        rt = P.alloc([ntok], F32, "rt")
        self.act(rt.ap[0:rows, :], ps.t[0:rows, 0:ntok], AF.Ln, reads=[ps, self.epsc], writes=[rt], bias=self.epsc.ap[0:rows, 0:1])
        self.act(rt.ap[0:rows, :], rt.ap[0:rows, :], AF.Exp, reads=[rt], writes=[rt], scale=-0.5)
        self.ps_release(ps)
        yield
        if custom is not None:
            custom(rt)
            P.release(rt)
            return
        for c, v in enumerate(srcs):
            self.stt(outs[c].ap, v.ap, gcols[c], rt.ap[0:rows, :], ALU.mult, ALU.mult, reads=[v, rt, self.gv], writes=[outs[c]])
            if outs2 is not None:
                self.stt(outs2[c].ap, v.ap, gcols[c], rt.ap[0:rows, :], ALU.mult, ALU.mult, reads=[v, rt, self.gv], writes=[outs2[c]])
        P.release(rt)

    def rms(self, *a, **k):
        for _ in self.rms_g(*a, **k):
            pass

    def store_tokmajor(self, chunks, rows_list, ntok, dst_fn):
        P = self.P
        F = sum(rows_list)
        nsub = -(-ntok // 128)
        for s in range(nsub):
            n = min(128, ntok - 128 * s)
            stg = P.alloc([F], F32, "ostg")
            off = 0
            ps = None
            psoff = 0
            pend = []
            for v, rows in zip(chunks, rows_list):
                if ps is None or psoff + rows > 512:
                    if ps is not None:
                        pend.append((ps, off - psoff, psoff))
                    ps = self.next_ps()
                    psoff = 0
                self.tr(ps, ps.t[0:n, psoff:psoff + rows], v.ap[:, 128 * s:128 * s + n], rows, reads=[v])
                psoff += rows
                off += rows
            pend.append((ps, off - psoff, psoff))
            for (pp, o0, w) in pend:
                self.copy(self.alt(), stg.ap[0:n, o0:o0 + w], pp.t[0:n, 0:w], reads=[pp], writes=[stg])
            self.S.dma("sp", dst_fn(s, n), stg.ap[0:n, 0:F], reads=[stg])
            P.release(stg)

    def ffn(self, blocks, li, which, mid=None):
        P = self.P
        w_in = self.inp[f"ffn{which}_w_in"][li]
        w_out = self.inp[f"ffn{which}_w_out"][li]
        g0 = self.gcol[f"norm_ffn{which}"][li]
        hns, hs = [], []
        for x, ntok in blocks:
            hn = P.alloc([8, ntok], BF16, "hn")
            self.rms_x(x, f"norm_ffn{which}", li, [hn.c(k) for k in range(8)], ntok)
            hns.append(hn)
            hs.append(P.alloc([22, ntok], BF16, "h"))
        w_in_v = w_in.rearrange("(k p) c -> p k c", p=128)
        for jb in range(11):
            if mid is not None and jb == 3:
                mid()
            slot = self.wload([
                (lambda a: a[:, 0:4096].rearrange("p (k c) -> p k c", k=8)[:, :, 0:256], w_in_v[:, :, 256 * jb:256 * jb + 256]),
                (lambda a: a[:, 0:4096].rearrange("p (k c) -> p k c", k=8)[:, :, 256:512], w_in_v[:, :, FFN + 256 * jb:FFN + 256 * jb + 256]),
            ])
            sv = slot.ap[:, 0:4096].rearrange("p (k c) -> p k c", k=8)
            for (x, ntok), hn, h in zip(blocks, hns, hs):
                for jj in range(2):
                    j = 2 * jb + jj
                    psg = self.next_ps()
                    psu = self.next_ps()
                    self.mm(psg, psg.t[:, 0:ntok], [(sv[:, k, jj * 128:(jj + 1) * 128], hn.ap[:, k, :]) for k in range(8)], reads=[slot, hn])
                    self.mm(psu, psu.t[:, 0:ntok], [(sv[:, k, 256 + jj * 128:256 + (jj + 1) * 128], hn.ap[:, k, :]) for k in range(8)], reads=[slot, hn])
                    sg = P.alloc([ntok], F32, "sg")
                    self.act(sg.ap, psg.t[:, 0:ntok], AF.Silu, reads=[psg], writes=[sg])
                    self.tt(h.ap[:, j, :], sg.ap, psu.t[:, 0:ntok], ALU.mult, reads=[sg, psu], writes=[h.c(j)])
                    P.release(sg)
        w_out_v = w_out.rearrange("(j p) c -> p j c", p=128)
        for oc in range(8):
            slot = self.wload([(lambda a: a[:, 0:22 * 128].rearrange("p (j c) -> p j c", j=22), w_out_v[:, :, 128 * oc:128 * oc + 128])])
            sv = slot.ap[:, 0:22 * 128].rearrange("p (j c) -> p j c", j=22)
            for (x, ntok), h in zip(blocks, hs):
                ps = self.next_ps()
                self.mm(ps, ps.t[:, 0:ntok], [(sv[:, j, :], h.ap[:, j, :]) for j in range(22)], reads=[slot, h])
                self.stt(x.ap[:, oc, :], ps.t[:, 0:ntok], 0.5, x.ap[:, oc, :], ALU.mult, ALU.add, reads=[ps, x.c(oc)], writes=[x.c(oc)])
                if not (li == 1 and which == 2):
                    self.xstat_chunk(x, oc, ntok)
        P.release(*hns, *hs)

    def attention(self, seq, kind, qv, nh, dk, scale, nkeys, ntok, diag, bias_fn, out_fn, bg=None, skip=None):
        P, S, nc = self.P, self.S, self.nc
        sc = self.scr[seq]
        nkt = -(-nkeys // 128)
        kK = "mk" if kind == "mla" else "fk"
        kV = "mv" if kind == "mla" else "fv"
        HG = 2
        pend = None
        upk = 1 if kind == "mla" else 2
        nku, nvg = nh // upk, nh // HG
        kq, vq = {}, {}

        def load_k(u):
            t = P.alloc([nkeys], BF16, "kbuf")
            if kind == "mla":
                S.dma("sp", t.ap[0:DQK, :], sc[kK][u, :, 0:nkeys], reads=self.scr_bufs(seq, kK, nkeys), writes=[t])
            else:
                S.dma("sp", t.ap, sc[kK][u, :, 0:nkeys], reads=self.scr_bufs(seq, kK, nkeys), writes=[t])
            return t

        def load_v(gi):
            t = P.alloc([nkt, HG * 65], BF16, "vbuf")
            src = sc[kV][0:nkt, :, gi * HG * 65:(gi + 1) * HG * 65].rearrange("k p c -> p k c")
            S.dma("sp", t.ap, src, reads=self.scr_bufs(seq, kV, nkeys), writes=[t])
            return t
        kq[0] = load_k(0)
        vq[0] = load_v(0)
        bgc = [0]
        bg_every = max(2, (nh * nkt) // 24)
        for h in range(nh):
            u, gi = h // upk, h // HG
            if h % upk == 0:
                if u - 1 in kq:
                    P.release(kq.pop(u - 1))
                if u + 1 < nku:
                    kq[u + 1] = load_k(u + 1)
            if h % HG == 0:
                if gi - 1 in vq:
                    P.release(vq.pop(gi - 1))
                if gi + 1 < nvg:
                    vq[gi + 1] = load_v(gi + 1)
            kbuf, vbuf = kq[u], vq[gi]
            kb0 = 0
            q = qv(h)
            po = self.pso[h % 2]
            pts = {}
            c0_of = {kt: (skip.get(kt, 0) if skip else 0) for kt in range(nkt)}
            full = [kt for kt in range(nkt) if c0_of[kt] == 0]
            part = [kt for kt in range(nkt) if c0_of[kt] > 0]
            if len(full) >= 2:
                order = [full[0]] + part + full[1:]
            else:
                order = full + part
            first, last = order[0], order[-1]
            closing = c0_of[last] > 0
            if closing:
                last = None

            def emit_s(kt):
                ksz = min(128, nkeys - 128 * kt)
                c0 = c0_of[kt]
                ps = self.next_ps()
                terms = [(kbuf.ap[kb0:kb0 + dk, 128 * kt:128 * kt + ksz], q.ap[:, c0:ntok])]
                rd = [kbuf, q]
                if kt in diag:
                    terms.append((self.identb.ap[0:ksz, 0:ksz], diag[kt][:, 0:ntok - c0]))
                    rd += [self.identb, self.maskt]
                self.mm(ps, ps.t[0:ksz, c0:ntok], terms, reads=rd)
                pt = P.alloc([ntok], BF16, "pt")
                b = bias_fn(kt, h, ksz) if bias_fn is not None else None
                self.act(pt.ap[0:ksz, c0:ntok], ps.t[0:ksz, c0:ntok], AF.Exp, reads=[ps] + ([self.nbt] if b is not None else []), writes=[pt], bias=b, scale=scale)
                pts[kt] = (pt, ksz, c0)

            def emit_pv(kt):
                pt, ksz, c0 = pts.pop(kt)
                hv = vbuf.ap[0:ksz, kt, (h % HG) * 65:(h % HG) * 65 + 65]
                self.mm1(po, po.t[0:65, c0:ntok], hv, pt.ap[0:ksz, c0:ntok], kt == first, kt == last, reads=[vbuf, pt])
                P.release(pt)
            NB = 3
            batches = [order[i:i + NB] for i in range(0, len(order), NB)]
            for kt in batches[0]:
                emit_s(kt)
            for bi, bt in enumerate(batches):
                if bi + 1 < len(batches):
                    for kt in batches[bi + 1]:
                        emit_s(kt)
                for kt in bt:
                    emit_pv(kt)
                if closing and bi == len(batches) - 1:
                    self.mm1(po, po.t[0:65, 0:ntok], self.zerob.ap[:, 0:65], self.maskt.ap[:, 0, 0:ntok], False, True, reads=[self.zerob, self.maskt])
                if bg is not None:
                    bgc[0] += len(bt)
                    while bgc[0] >= bg_every:
                        bgc[0] -= bg_every
                        bg.step()
            osb = P.alloc([ntok], F32, "osb")
            self.copy("dve", osb.ap[0:65, :], po.t[0:65, 0:ntok], reads=[po], writes=[osb])
            S.op("dve", lambda: nc.vector.reciprocal(out=osb.ap[64:65, :], in_=osb.ap[64:65, :]), reads=[osb], writes=[osb])
            if pend is not None:
                pend()

            def fin(osb=osb, h=h):
                pb = self.next_ps()
                self.mm(pb, pb.t[0:64, 0:ntok], [(self.onesf.ap[64:65, 0:64], osb.ap[64:65, :])], reads=[osb, self.onesf])
                dst, dview = out_fn(h)
                self.tt(dst, osb.ap[0:64, :], pb.t[0:64, 0:ntok], ALU.mult, reads=[osb, pb], writes=[dview])
                P.release(osb)
            pend = fin
        if bg is not None:
            bg.drain()
        pend()
        for t in list(kq.values()) + list(vq.values()):
            P.release(t)

    def scr_bufs(self, seq, kind, nkeys):
        d = self.scr[seq]["bufs"]
        return [b for (k, lo), b in d.items() if k == kind and lo < nkeys]

    def scr_buf(self, seq, kind, lo):
        d = self.scr[seq]["bufs"]
        if (kind, lo) not in d:
            d[(kind, lo)] = Buf(f"scr{seq}{kind}{lo}")
        return d[(kind, lo)]

    def rope_tables(self, pos0, ntok):
        P = self.P
        R = slice(64, 96)
        I32 = mybir.dt.int32
        ang = P.alloc([ntok], F32, "ang")
        cosT = P.alloc([ntok], F32, "cosT")
        ssin = P.alloc([ntok], F32, "ssin")
        ki = P.alloc([ntok], F32, "ki")
        kf = P.alloc([ntok], F32, "kf")
        HI = 6.28125
        LO = 2.0 * math.pi - HI
        self.ts(ang.ap[R, :], self.tix.ap[R, 0:ntok], float(pos0), None, ALU.add, None, reads=[self.tix], writes=[ang])
        self.ts(ang.ap[R, :], ang.ap[R, :], self.ropec.ap[R, 0:1], None, ALU.mult, None, reads=[ang, self.ropec], writes=[ang])
        for shift, dst in ((0.0, ssin), (0.5 * math.pi, cosT)):
            if shift != 0.0:
                self.ts(ang.ap[R, :], ang.ap[R, :], shift, None, ALU.add, None, reads=[ang], writes=[ang])
            self.ts(ki.ap.bitcast(I32)[R, :], ang.ap[R, :], 1.0 / (2.0 * math.pi), None, ALU.mult, None, reads=[ang], writes=[ki])
            self.copy("dve", kf.ap[R, :], ki.ap.bitcast(I32)[R, :], reads=[ki], writes=[kf])
            self.stt(dst.ap[R, :], kf.ap[R, :], -HI, ang.ap[R, :], ALU.mult, ALU.add, reads=[kf, ang], writes=[dst])
            self.stt(dst.ap[R, :], kf.ap[R, :], -LO, dst.ap[R, :], ALU.mult, ALU.add, reads=[kf, dst], writes=[dst])
            self.ts(dst.ap[R, :], dst.ap[R, :], -math.pi, math.pi, ALU.max, ALU.min, reads=[dst], writes=[dst])
            self.act(dst.ap[R, :], dst.ap[R, :], AF.Sin, reads=[dst], writes=[dst])
        self.ts(ssin.ap[R, :], ssin.ap[R, :], self.ropec.ap[R, 1:2], None, ALU.mult, None, reads=[ssin, self.ropec], writes=[ssin])
        P.release(ang, ki, kf)
        return cosT, ssin

    def mla_kv_expand(self, seq, latb, kr, pos_rel, ntok):
        P, S, nc = self.P, self.S, self.nc
        sc = self.scr[seq]
        R = slice(64, 96)
        kst = P.alloc([NH_MLA, ntok], BF16, "kst")
        def kgen(h):
            ps = self.next_ps(hold=True)
            self.mm(ps, ps.t[0:64, 0:ntok], [(self.wukv.ap[:, 128 * h:128 * h + 64], latb.ap)], reads=[self.wukv, latb])
            yield
            kf = P.alloc([ntok], F32, "kf")
            self.copy("act", kf.ap[0:64, :], ps.t[0:64, 0:ntok], reads=[ps], writes=[kf])
            self.copy("dve", kf.ap[R, :], kr.ap[R, :], reads=[kr], writes=[kf])
            self.ps_release(ps)
            yield
            yield from self.rms_g([View(kf.ap[0:96, :], kf.bufs)], 96, self.cmb.ap[0:96, 3, 0:96],
                                  [self.gv.ap[0:96, self.gcol["ev_g_k"]:self.gcol["ev_g_k"] + 1]],
                                  [View(kst.ap[0:96, h, :], kst.c(h).bufs)], ntok)
            P.release(kf)
        Pipe([kgen(h) for h in range(NH_MLA)], width=4).drain()
        bk = self.scr_buf(seq, "mk", pos_rel)
        S.dma("sp", sc["mk"][:, :, pos_rel:pos_rel + ntok].rearrange("h r t -> r h t"), kst.ap[0:96, :, :], reads=[kst], writes=[bk])
        P.release(kst)
        nsub = -(-ntok // 128)
        vst = P.alloc([nsub, NH_MLA * 65], BF16, "vst")
        S.op("dve", lambda: nc.vector.memset(vst.ap, 1.0), writes=[vst])
        wv = self.wukv.ap.rearrange("p (h c) -> p h c", h=NH_MLA)[:, :, 64:128]
        for s in range(nsub):
            n = min(128, ntok - 128 * s)
            ps = self.next_ps()
            self.mm(ps, ps.t[0:n, 0:512], [(latb.ap[:, 128 * s:128 * s + n], wv)], reads=[self.wukv, latb])
            self.copy(self.alt(), vst.ap[0:n, s, :].rearrange("p (h c) -> p h c", h=NH_MLA)[:, :, 0:64],
                      ps.t[0:n, 0:512].rearrange("p (h c) -> p h c", h=NH_MLA), reads=[ps], writes=[vst])
        bv = self.scr_buf(seq, "mv", pos_rel)
        kt0 = pos_rel // 128
        S.dma("sp", sc["mv"][kt0:kt0 + nsub].rearrange("k p c -> p k c"), vst.ap, reads=[vst], writes=[bv])
        P.release(vst)

    def even_mixer(self, x, seq, st, pos0, pos_rel, ntok, outs):
        P, S, nc = self.P, self.S, self.nc
        g = self.gcol
        w_in = self.inp["ev_w_in"][0].rearrange("(k p) c -> p k c", p=128)
        R = slice(64, 96)
        hn = P.alloc([8, ntok], BF16, "hn")
        self.rms_x(x, "norm_mix", 0, [hn.c(k) for k in range(8)], ntok)
        hk = lambda k: hn.ap[:, k, :]
        v8 = lambda a, w: a[:, 0:8 * w].rearrange("p (k c) -> p k c", k=8)
        slotA = self.wload([
            (lambda a: v8(a, 448)[:, :, 0:416], w_in[:, :, 0:416]),
            (lambda a: v8(a, 448)[:, :, 416:432], w_in[:, :, 400:416]),
            (lambda a: v8(a, 448)[:, :, 432:448], w_in[:, :, 384:400]),
        ])
        sA = v8(slotA.ap, 448)
        cosT, ssin = self.rope_tables(pos0, ntok)
        cqn = P.alloc([2, ntok], BF16, "cqn")
        lat = P.alloc([ntok], F32, "lat")
        latb = P.alloc([ntok], BF16, "latb")
        kr = P.alloc([ntok], F32, "kr")

        def lat_chain():
            ps = self.next_ps(hold=True)
            self.mm(ps, ps.t[:, 0:ntok], [(sA[:, k, 256:384], hk(k)) for k in range(8)], reads=[slotA, hn])
            yield
            yield from self.rms_g([View(ps.t[:, 0:ntok], ps.bufs)], 128, self.cmb.ap[:, 2, :], [self.gv.ap[:, g["ev_g_kvlat"]:g["ev_g_kvlat"] + 1]],
                                  [lat.v()], ntok, outs2=[latb.v()])
            self.ps_release(ps)

        def kr_chain():
            ps = self.next_ps()
            self.mm(ps, ps.t[0:96, 0:ntok], [(sA[:, k, 320:416], hk(k)) for k in range(8)], reads=[slotA, hn])
            ps2 = self.next_ps()
            self.mm(ps2, ps2.t[0:96, 0:ntok], [(sA[:, k, 352:448], hk(k)) for k in range(8)], reads=[slotA, hn])
            self.tt(kr.ap[R, :], ps.t[R, 0:ntok], cosT.ap[R, :], ALU.mult, reads=[ps, cosT], writes=[kr])
            tmp = P.alloc([ntok], F32, "tmp")
            self.tt(tmp.ap[R, :], ps2.t[R, 0:ntok], ssin.ap[R, :], ALU.mult, reads=[ps2, ssin], writes=[tmp])
            self.tt(kr.ap[R, :], kr.ap[R, :], tmp.ap[R, :], ALU.add, reads=[kr, tmp], writes=[kr])
            P.release(tmp)
            yield

        def cq_chain():
            cq = P.alloc([2, ntok], F32, "cq")
            for c in range(2):
                ps = self.next_ps()
                self.mm(ps, ps.t[:, 0:ntok], [(sA[:, k, 128 * c:128 * c + 128], hk(k)) for k in range(8)], reads=[slotA, hn])
                self.copy(self.alt(), cq.ap[:, c, :], ps.t[:, 0:ntok], reads=[ps], writes=[cq.c(c)])
            yield
            yield from self.rms_g([cq.c(0), cq.c(1)], 128, self.cmb.ap[:, 1, :], [self.gv.ap[:, g["ev_g_qlat"] + c:g["ev_g_qlat"] + c + 1] for c in range(2)],
                                  [cqn.c(0), cqn.c(1)], ntok)
            P.release(cq)
        Pipe([lat_chain(), kr_chain(), cq_chain()], width=3).drain()
        self.mla_kv_expand(seq, latb, kr, pos_rel, ntok)
        self.store_tokmajor([lat.v()], [128], ntok, outs["lat"])
        self.store_tokmajor([View(kr.ap[R, :], kr.bufs)], [32], ntok, outs["kr"])
        P.release(lat, latb, kr)
        qn = P.alloc([NH_MLA, ntok], BF16, "qn")

        def qgen(h):
            ps = self.next_ps(hold=True)
            self.mm(ps, ps.t[0:96, 0:ntok], [(self.wuq.ap[:, k, 96 * h:96 * h + 96], cqn.ap[:, k, :]) for k in range(2)], reads=[self.wuq, cqn])
            ps2 = self.next_ps(hold=True)
            self.mm(ps2, ps2.t[0:96, 0:ntok], [(self.wuqs.ap[:, k, 96 * h:96 * h + 96], cqn.ap[:, k, :]) for k in range(2)], reads=[self.wuqs, cqn])
            yield
            qf = P.alloc([ntok], F32, "qf")
            self.copy("act", qf.ap[0:64, :], ps.t[0:64, 0:ntok], reads=[ps], writes=[qf])
            self.tt(qf.ap[R, :], ps.t[R, 0:ntok], cosT.ap[R, :], ALU.mult, reads=[ps, cosT], writes=[qf])
            tmp = P.alloc([ntok], F32, "tmp")
            self.tt(tmp.ap[R, :], ps2.t[R, 0:ntok], ssin.ap[R, :], ALU.mult, reads=[ps2, ssin], writes=[tmp])
            self.tt(qf.ap[R, :], qf.ap[R, :], tmp.ap[R, :], ALU.add, reads=[qf, tmp], writes=[qf])
            P.release(tmp)
            self.ps_release(ps, ps2)
            yield
            yield from self.rms_g([View(qf.ap[0:96, :], qf.bufs)], 96, self.cmb.ap[0:96, 3, 0:96], [self.gv.ap[0:96, g["ev_g_q"]:g["ev_g_q"] + 1]],
                                  [View(qn.ap[0:96, h, :], qn.c(h).bufs)], ntok)
            P.release(qf)
        mixin = P.alloc([8, ntok], BF16, "mixin")
        rg = self.rglru(hn, w_in, st, ntok, mixin, outs)
        Pipe([qgen(i) for i in range(NH_MLA)], width=2).drain()
        P.release(cqn, cosT, ssin)
        bg = Pipe(rg, width=2)
        nkeys = pos_rel + ntok
        diag = {}
        if st["kind"] == "prompt":
            for i in range(self.TT // 128):
                diag[pos_rel // 128 + i] = self.maskt.ap[:, 1, :]
        self.attention(seq, "mla", lambda h: View(qn.ap[0:96, h, :], qn.c(h).bufs), NH_MLA, DQK, DQK ** -0.5, nkeys, ntok, diag, None,
                       lambda h: (mixin.ap[64 * (h % 2):64 * (h % 2) + 64, h // 2, :], mixin.c(h // 2)), bg=bg,
                       skip=({pos_rel // 128 + i: 128 * i for i in range(self.TT // 128)} if st["kind"] == "prompt" else None))
        P.release(qn, hn)
        w_out = self.inp["ev_w_out"][0].rearrange("(j p) c -> p j c", p=128)
        for ob in range(2):
            slot = self.wload([(lambda a: v8(a, 512), w_out[:, :, 512 * ob:512 * ob + 512])])
            sv = v8(slot.ap, 512)
            for oo in range(4):
                oc = 4 * ob + oo
                ps = self.next_ps()
                self.mm(ps, ps.t[:, 0:ntok], [(sv[:, j, 128 * oo:128 * oo + 128], mixin.ap[:, j, :]) for j in range(8)], reads=[slot, mixin])
                self.tt(x.ap[:, oc, :], ps.t[:, 0:ntok], x.ap[:, oc, :], ALU.add, reads=[ps, x.c(oc)], writes=[x.c(oc)])
                self.xstat_chunk(x, oc, ntok)
        P.release(mixin)

    def rglru(self, hn, w_in, st, ntok, mixin, outs):
        P, S, nc = self.P, self.S, self.nc
        g = self.gcol
        v8 = lambda a, w: a[:, 0:8 * w].rearrange("p (k c) -> p k c", k=8)
        slotB = self.wload([(lambda a: v8(a, 512), w_in[:, :, 416:928])])
        slotC = self.wload([(lambda a: v8(a, 512), w_in[:, :, 928:1440])])
        sB, sC = v8(slotB.ap, 512), v8(slotC.ap, 512)
        U = st["U"]
        hc = st["hcarry"]
        if ntok < 3:
            raise NotImplementedError

        def chain(c):
            ps = self.next_ps(hold=True)
            self.mm(ps, ps.t[:, 0:ntok], [(sB[:, k, 128 * c:128 * c + 128], hn.ap[:, k, :]) for k in range(8)], reads=[slotB, hn])
            psg = self.next_ps(hold=True)
            self.mm(psg, psg.t[:, 0:ntok], [(sC[:, k, 128 * c:128 * c + 128], hn.ap[:, k, :]) for k in range(8)], reads=[slotC, hn])
            yield
            self.copy("act", U.ap[:, c, 3:3 + ntok], ps.t[:, 0:ntok], reads=[ps], writes=[U.c(c)])
            xg = P.alloc([ntok], F32, "xg")
            self.copy("act", xg.ap, psg.t[:, 0:ntok], reads=[psg], writes=[xg])
            self.ps_release(ps, psg)
            yield
            xc = P.alloc([ntok], F32, "xc")
            cw = g["conv_w"]
            self.ts(xc.ap, U.ap[:, c, 0:ntok], self.gv.ap[:, cw + c:cw + c + 1], self.gv.ap[:, g["conv_b"] + c:g["conv_b"] + c + 1], ALU.mult, ALU.add,
                    reads=[U.c(c), self.gv], writes=[xc])
            for j in range(1, 4):
                self.stt(xc.ap, U.ap[:, c, j:j + ntok], self.gv.ap[:, cw + 4 * j + c:cw + 4 * j + c + 1], xc.ap, ALU.mult, ALU.add,
                         reads=[U.c(c), xc, self.gv], writes=[xc])
            self.copy("dve", U.ap[:, c, 0:3], U.ap[:, c, ntok:ntok + 3], reads=[U.c(c)], writes=[U.c(c)])
            u = P.alloc([ntok], F32, "u")
            self.tt(u.ap, xg.ap, xg.ap, ALU.mult, reads=[xg], writes=[u])
            self.ts(u.ap, u.ap, 0.044715, 1.0, ALU.mult, ALU.add, reads=[u], writes=[u])
            self.tt(u.ap, u.ap, xg.ap, ALU.mult, reads=[u, xg], writes=[u])
            yield
            xcb = P.alloc([ntok], BF16, "xcb")
            self.copy("act", xcb.ap, xc.ap, reads=[xc], writes=[xcb])
            self.act(u.ap, u.ap, AF.Sigmoid, reads=[u], writes=[u], scale=1.5957691216057308)
            yield
            psr = self.next_ps(hold=True)
            self.mm(psr, psr.t[:, 0:ntok], [(self.gw.ap[:, c, :], xcb.ap)], reads=[self.gw, xcb])
            psi = self.next_ps(hold=True)
            self.mm(psi, psi.t[:, 0:ntok], [(self.gw.ap[:, 4 + c, :], xcb.ap)], reads=[self.gw, xcb])
            P.release(xcb)
            self.tt(u.ap, u.ap, xg.ap, ALU.mult, reads=[u, xg], writes=[u])
            P.release(xg)
            yield
            a = P.alloc([ntok], F32, "a")
            b = P.alloc([ntok], F32, "b")
            self.act(a.ap, psr.t[:, 0:ntok], AF.Sigmoid, reads=[psr, self.gv], writes=[a], bias=self.gv.ap[:, g["gb_r"] + c:g["gb_r"] + c + 1])
            self.act(a.ap, a.ap, AF.Exp, reads=[a, self.gv], writes=[a], scale=self.gv.ap[:, g["nsp8"] + c:g["nsp8"] + c + 1])
            self.act(b.ap, psi.t[:, 0:ntok], AF.Sigmoid, reads=[psi, self.gv], writes=[b], bias=self.gv.ap[:, g["gb_i"] + c:g["gb_i"] + c + 1])
            self.ps_release(psr, psi)
            yield
            self.tt(b.ap, b.ap, xc.ap, ALU.mult, reads=[b, xc], writes=[b])
            t1 = P.alloc([ntok], F32, "t1")
            self.tt(t1.ap, a.ap, a.ap, ALU.mult, reads=[a], writes=[t1])
            self.ts(t1.ap, t1.ap, -1.0, 1.0, ALU.mult, ALU.add, reads=[t1], writes=[t1])
            P.release(xc)
            yield
            self.act(t1.ap, t1.ap, AF.Sqrt, reads=[t1], writes=[t1])
            yield
            self.tt(b.ap, b.ap, t1.ap, ALU.mult, reads=[b, t1], writes=[b])
            P.release(t1)
            hs = P.alloc([ntok], F32, "hs")
            S.op("dve", lambda: nc.vector.tensor_tensor_scan(out=hs.ap, data0=a.ap, data1=b.ap, initial=hc.ap[:, c:c + 1], op0=ALU.mult, op1=ALU.add),
                 reads=[a, b, hc], writes=[hs])
            self.copy("dve", hc.ap[:, c:c + 1], hs.ap[:, ntok - 1:ntok], reads=[hs], writes=[hc])
            P.release(a, b)
            self.tt(mixin.ap[:, 4 + c, :], u.ap, hs.ap, ALU.mult, reads=[u, hs], writes=[mixin.c(4 + c)])
            P.release(u, hs)
        return [chain(c) for c in range(4)]

    def mem_attend(self, x, li, mst, ntok):
        P, S, nc = self.P, self.S, self.nc
        g = self.gcol
        v8 = lambda a, w: a[:, 0:8 * w].rearrange("p (k c) -> p k c", k=8)
        hn = P.alloc([8, ntok], BF16, "hn")
        self.rms_x(x, "norm_mem", li, [hn.c(k) for k in range(8)], ntok)
        wq = self.inp["mem_w_q"][li].rearrange("(k p) c -> p k c", p=128)
        slot = self.wload([(lambda a: v8(a, 512), wq)])
        sv = v8(slot.ap, 512)
        att = P.alloc([4, ntok], BF16, "att")
        memK, memV = mst["K"], mst["V"]

        def hgen(h):
            ps = self.next_ps(hold=True)
            self.mm(ps, ps.t[:, 0:ntok], [(sv[:, k, 128 * h:128 * h + 128], hn.ap[:, k, :]) for k in range(8)], reads=[slot, hn])
            yield
            qn = P.alloc([ntok], BF16, "mqn")
            yield from self.rms_g([View(ps.t[:, 0:ntok], ps.bufs)], 128, self.cmb.ap[:, 2, :], [self.gv.ap[:, g["mem_g_q"][li]:g["mem_g_q"][li] + 1]], [qn.v()], ntok)
            self.ps_release(ps)
            yield
            pts = []
            for mt in range(2):
                pss = self.next_ps()
                self.mm(pss, pss.t[:, 0:ntok], [(memK.ap[:, li * 4 + h, 128 * mt:128 * mt + 128], qn.ap)], reads=[memK, qn])
                pt = P.alloc([ntok], BF16, "pt")
                self.act(pt.ap, pss.t[:, 0:ntok], AF.Exp, reads=[pss], writes=[pt], scale=MEM_HD ** -0.5)
                pts.append(pt)
            P.release(qn)
            yield
            po = self.next_ps(hold=True)
            pd = self.next_ps(hold=True)
            for mt in range(2):
                self.mm1(po, po.t[:, 0:ntok], memV.ap[:, li * 2 + mt, 128 * h:128 * h + 128], pts[mt].ap, mt == 0, mt == 1, reads=[memV, pts[mt]])
            for mt in range(2):
                self.mm1(pd, pd.t[:, 0:ntok], self.cmb.ap[:, 5, :], pts[mt].ap, mt == 0, mt == 1, reads=[self.cm, pts[mt]])
            P.release(*pts)
            yield
            rc = P.alloc([ntok], F32, "rc")
            self.act(rc.ap, pd.t[:, 0:ntok], AF.Ln, reads=[pd], writes=[rc])
            self.act(rc.ap, rc.ap, AF.Exp, reads=[rc], writes=[rc], scale=-1.0)
            yield
            self.tt(att.ap[:, h, :], po.t[:, 0:ntok], rc.ap, ALU.mult, reads=[po, rc], writes=[att.c(h)])
            P.release(rc)
            self.ps_release(po, pd)
        Pipe([hgen(h) for h in range(MEM_H)], width=2).drain()
        P.release(hn)
        wo = self.inp["mem_w_o"][li].rearrange("(j p) c -> p j c", p=128)
        v4 = lambda a: a[:, 0:4096].rearrange("p (j c) -> p j c", j=4)
        slot = self.wload([(lambda a: v4(a), wo)])
        sv = v4(slot.ap)
        for oc in range(8):
            ps = self.next_ps()
            self.mm(ps, ps.t[:, 0:ntok], [(sv[:, j, 128 * oc:128 * oc + 128], att.ap[:, j, :]) for j in range(4)], reads=[slot, att])
            self.tt(x.ap[:, oc, :], ps.t[:, 0:ntok], x.ap[:, oc, :], ALU.add, reads=[ps, x.c(oc)], writes=[x.c(oc)])
            self.xstat_chunk(x, oc, ntok)
        P.release(att)

    def odd_mixer(self, x, seq, st, pos0, pos_rel, ntok, outs):
        P, S, nc = self.P, self.S, self.nc
        g = self.gcol
        sc = self.scr[seq]
        v8 = lambda a, w: a[:, 0:8 * w].rearrange("p (k c) -> p k c", k=8)
        w_in = self.inp["od_w_in"][0].rearrange("(k p) c -> p k c", p=128)
        hn = P.alloc([8, ntok], BF16, "hn")
        self.rms_x(x, "norm_mix", 1, [hn.c(k) for k in range(8)], ntok)
        nsub = -(-ntok // 128)
        bv = self.scr_buf(seq, "fv", pos_rel)
        kt0 = pos_rel // 128
        vst = P.alloc([nsub, NH_FOX * 65], BF16, "fvst")
        S.op("dve", lambda: nc.vector.memset(vst.ap, 1.0), writes=[vst])
        slots = [self.wload([(lambda a: v8(a, 512), w_in[:, :, 2048 + 512 * blk:2048 + 512 * blk + 512])]) for blk in range(2)]
        for s in range(nsub):
            n = min(128, ntok - 128 * s)
            stg = P.alloc([1024], F32, "vstg")
            for blk in range(2):
                sv = v8(slots[blk].ap, 512)
                ps = self.next_ps()
                self.mm(ps, ps.t[0:n, 0:512], [(hn.ap[:, k, 128 * s:128 * s + n], sv[:, k, :]) for k in range(8)], reads=[slots[blk], hn])
                self.copy("act", stg.ap[0:n, 512 * blk:512 * blk + 512], ps.t[0:n, 0:512], reads=[ps], writes=[stg])
                self.copy("dve", vst.ap[0:n, s, :].rearrange("p (h c) -> p h c", h=NH_FOX)[:, 8 * blk:8 * blk + 8, 0:64],
                          ps.t[0:n, 0:512].rearrange("p (h c) -> p h c", h=8), reads=[ps], writes=[vst])
            S.dma("sp", outs["fv"](s, n), stg.ap[0:n, :], reads=[stg])
            P.release(stg)
        S.dma("sp", sc["fv"][kt0:kt0 + nsub].rearrange("k p c -> p k c"), vst.ap, reads=[vst], writes=[bv])
        P.release(vst)
        qn = P.alloc([NH_FOX, ntok], BF16, "fqz")
        S.op("dve", lambda: nc.vector.memset(qn.ap, 0.0), writes=[qn])
        knf = P.alloc([8, ntok], F32, "knf")
        kst = P.alloc([8, ntok], BF16, "fkst")

        def cgen(part, c, slot):
            sv = v8(slot.ap, 512)
            cc = c % 4
            ps = self.next_ps(hold=True)
            self.mm(ps, ps.t[:, 0:ntok], [(sv[:, k, 128 * cc:128 * cc + 128], hn.ap[:, k, :]) for k in range(8)], reads=[slot, hn])
            yield
            if part == 0:
                def qout(rt, c=c, ps=ps):
                    for hh in range(2):
                        rr = slice(64 * hh, 64 * hh + 64)
                        self.stt(qn.ap[rr, 2 * c + hh, :], ps.t[rr, 0:ntok], self.gv.ap[rr, g["od_g_q"]:g["od_g_q"] + 1], rt.ap[rr, :], ALU.mult, ALU.mult,
                                 reads=[ps, rt, self.gv], writes=[qn.c(2 * c + hh)])
                yield from self.rms_g([View(ps.t[:, 0:ntok], ps.bufs)], 128, self.cmb.ap[:, 4, :], None, None, ntok, custom=qout)
            else:
                yield from self.rms_g([View(ps.t[:, 0:ntok], ps.bufs)], 128, self.cmb.ap[:, 4, :], [self.gv.ap[:, g["od_g_k"]:g["od_g_k"] + 1]], [knf.c(c)], ntok,
                                      outs2=[kst.c(c)])
            self.ps_release(ps)
        kslots = [self.wload([(lambda a: v8(a, 512), w_in[:, :, 1024 + 512 * i:1024 + 512 * i + 512])]) for i in range(2)]
        Pipe([cgen(1, c, kslots[c // 4]) for c in range(8)], width=3).drain()
        bk = self.scr_buf(seq, "fk", pos_rel)
        S.dma("sp", sc["fk"][:, :, pos_rel:pos_rel + ntok].rearrange("c r t -> r c t"), kst.ap, reads=[kst], writes=[bk])
        P.release(kst)
        qslots = [self.wload([(lambda a: v8(a, 512), w_in[:, :, 512 * i:512 * i + 512])]) for i in range(2)]
        Pipe([cgen(0, c, qslots[c // 4]) for c in range(8)], width=3).drain()
        self.store_tokmajor([knf.c(c) for c in range(8)], [128] * 8, ntok, outs["fk"])
        P.release(knf)
        slot = self.wload([(lambda a: v8(a, 16), w_in[:, :, 3072:3088])])
        sv = v8(slot.ap, 16)
        ps = self.next_ps()
        self.mm(ps, ps.t[0:16, 0:ntok], [(sv[:, k, :], hn.ap[:, k, :]) for k in range(8)], reads=[slot, hn])
        P.release(hn)
        lf = P.alloc([ntok], F32, "lf")
        self.act(lf.ap[0:16, :], ps.t[0:16, 0:ntok], AF.Exp, reads=[ps, self.gv], writes=[lf], bias=self.gv.ap[0:16, g["nbf"]:g["nbf"] + 1], scale=-1.0)
        self.act(lf.ap[0:16, :], lf.ap[0:16, :], AF.Ln, reads=[lf, self.onec], writes=[lf], bias=self.onec.ap[0:16, 0:1])
        self.ts(lf.ap[0:16, :], lf.ap[0:16, :], -1.0, None, ALU.mult, None, reads=[lf], writes=[lf])
        self.store_tokmajor([View(lf.ap[0:16, :], lf.bufs)], [16], ntok, outs["fl"])
        self.fox_cumsum(st, lf, pos_rel, ntok)
        P.release(lf)
        nkeys = pos_rel + ntok
        nkt = -(-nkeys // 128)
        cK = st["cK"]
        nb = self.nbt
        ref_kt = pos_rel // 128 + (2 if ntok >= 384 else 0)
        pr = self.next_ps()
        self.mm(pr, pr.t[:, 0:16], [(self.sel0.ap, cK.ap[:, ref_kt, :])], reads=[self.sel0, cK])
        cref = P.alloc([16], F32, "cref")
        self.copy("dve", cref.ap, pr.t[:, 0:16], reads=[pr], writes=[cref])
        for kt in range(nkt):
            self.tt(nb.ap[:, kt, :], cref.ap, cK.ap[:, kt, :], ALU.subtract, reads=[cref, cK], writes=[nb])
        P.release(cref)
        diag = {}
        if st["kind"] == "prompt":
            for i in range(self.TT // 128):
                diag[pos_rel // 128 + i] = self.maskt.ap[:, 0, :]
        else:
            diag[pos_rel // 128] = self.maskt.ap[0:ntok, 0, :]
        attn = P.alloc([8, ntok], BF16, "fattn")
        skip = {pos_rel // 128 + i: 128 * i for i in range(self.TT // 128)} if st["kind"] == "prompt" else None
        self.attention(seq, "fox", lambda h: View(qn.ap[:, h, :], qn.c(h).bufs), NH_FOX, 128, HD_FOX ** -0.5,
                       nkeys, ntok, diag, lambda kt, h, ksz: nb.ap[0:ksz, kt, h:h + 1],
                       lambda h: (attn.ap[64 * (h % 2):64 * (h % 2) + 64, h // 2, :], attn.c(h // 2)), skip=skip)
        P.release(qn)
        w_out = self.inp["od_w_out"][0].rearrange("(j p) c -> p j c", p=128)
        for ob in range(2):
            slot = self.wload([(lambda a: v8(a, 512), w_out[:, :, 512 * ob:512 * ob + 512])])
            sv = v8(slot.ap, 512)
            for oo in range(4):
                oc = 4 * ob + oo
                ps = self.next_ps()
                self.mm(ps, ps.t[:, 0:ntok], [(sv[:, j, 128 * oo:128 * oo + 128], attn.ap[:, j, :]) for j in range(8)], reads=[slot, attn])
                self.tt(x.ap[:, oc, :], ps.t[:, 0:ntok], x.ap[:, oc, :], ALU.add, reads=[ps, x.c(oc)], writes=[x.c(oc)])
                self.xstat_chunk(x, oc, ntok)
        P.release(attn)

    def fox_cumsum(self, st, lf, pos_rel, ntok):
        P, S, nc = self.P, self.S, self.nc
        cc = st["ccarry"]
        cT = P.alloc([ntok], F32, "cT")
        S.op("dve", lambda: nc.vector.tensor_tensor_scan(out=cT.ap[0:16, :], data0=self.onesf.ap[0:16, 0:ntok], data1=lf.ap[0:16, :],
                                                         initial=cc.ap[0:16, 0:1], op0=ALU.mult, op1=ALU.add),
             reads=[lf, cc, self.onesf], writes=[cT])
        self.copy("dve", cc.ap[0:16, 0:1], cT.ap[0:16, ntok - 1:ntok], reads=[cT], writes=[cc])
        cK = st["cK"]
        nsub = -(-ntok // 128)
        for s in range(nsub):
            n = min(128, ntok - 128 * s)
            ps = self.next_ps()
            self.tr(ps, ps.t[0:n, 0:16], cT.ap[0:16, 128 * s:128 * s + n], 16, reads=[cT])
            self.copy(self.alt(), cK.ap[0:n, pos_rel // 128 + s, :], ps.t[0:n, 0:16], reads=[ps], writes=[cK])
        P.release(cT)

    def load_featmajor(self, src_fn, ntok, F, dst_fn, q="sp"):
        P = self.P
        nsub = -(-ntok // 128)
        nch = -(-F // 128)
        for s in range(nsub):
            n = min(128, ntok - 128 * s)
            stg = P.alloc([F], F32, "istg")
            self.S.dma(q, stg.ap[0:n, 0:F], src_fn(s, n), writes=[stg])
            for c in range(nch):
                rows = min(128, F - 128 * c)
                ps = self.next_ps()
                self.tr(ps, ps.t[0:rows, 0:n], stg.ap[0:n, 128 * c:128 * c + rows], n, reads=[stg])
                dst, dv = dst_fn(c, rows, s, n)
                self.copy(self.alt(), dst, ps.t[0:rows, 0:n], reads=[ps], writes=[dv])
            P.release(stg)

    def mem_kv_prompt(self, si, mst):
        P, S, nc = self.P, self.S, self.nc
        g = self.gcol
        v8 = lambda a, w: a[:, 0:8 * w].rearrange("p (k c) -> p k c", k=8)
        memT = P.alloc([8, MEM_T], F32, "memT")
        self.load_featmajor(lambda s, n: self.inp["memp"][si, 128 * s:128 * s + n, :], MEM_T, D,
                            lambda c, rows, s, n: (memT.ap[:, c, 128 * s:128 * s + n], memT.c(c)))
        self.stage("memT")
        for li in range(2):
            memn = P.alloc([8, MEM_T], BF16, "memn")
            self.rms([memT.c(k) for k in range(8)], 128, self.cmb.ap[:, 0, :], [self.gv.ap[:, g["norm_mem_src"][li] + k:g["norm_mem_src"][li] + k + 1] for k in range(8)],
                     [memn.c(k) for k in range(8)], MEM_T)
            wkv = self.inp["mem_w_kv"][li].rearrange("(k p) c -> p k c", p=128)
            slots = [self.wload([(lambda a: v8(a, 512), wkv[:, :, 512 * b:512 * b + 512])]) for b in range(2)]
            self.stage("memn")
            kf = P.alloc([4, MEM_T], F32, "mkf")
            for h in range(MEM_H):
                sv = v8(slots[h // 2].ap, 512)
                ps = self.next_ps()
                self.mm(ps, ps.t[:, 0:MEM_T], [(sv[:, k, 256 * (h % 2):256 * (h % 2) + 128], memn.ap[:, k, :]) for k in range(8)], reads=[slots[h // 2], memn])
                self.rms([View(ps.t[:, 0:MEM_T], ps.bufs)], 128, self.cmb.ap[:, 2, :], [self.gv.ap[:, g["mem_g_k"][li]:g["mem_g_k"][li] + 1]],
                         [kf.c(h)], MEM_T, outs2=[View(mst["K"].ap[:, li * 4 + h, :], mst["K"].bufs)])
            self.store_tokmajor([kf.c(h) for h in range(4)], [128] * 4, MEM_T, lambda s, n: self.out["o_pmk"][li, si, 128 * s:128 * s + n, :])
            P.release(kf)
            self.stage("memk")
            for s in range(2):
                ps = self.next_ps()
                for hh in range(2):
                    for k in range(8):
                        sv = v8(slots[hh].ap, 512)
                        rhs = sv[:, k, :].rearrange("p (h c) -> p h c", h=2)[:, :, 128:256]
                        self.mm1(ps, ps.t[:, 256 * hh:256 * hh + 256], memn.ap[:, k, 128 * s:128 * s + 128], rhs,
                                 k == 0, k == 7, reads=[slots[hh], memn])
                stg = P.alloc([512], F32, "mvstg")
                self.copy("act", stg.ap, ps.t[:, 0:512], reads=[ps], writes=[stg])
                self.copy("dve", mst["V"].ap[:, li * 2 + s, :], ps.t[:, 0:512], reads=[ps], writes=[mst["V"]])
                self.stage(f"memv{li}{s}a")
                S.dma("sp", self.out["o_pmv"][li, si, 128 * s:128 * s + 128, :], stg.ap, reads=[stg])
                P.release(stg)
                self.stage(f"memv{li}{s}")
            P.release(memn)
        P.release(memT)

    def mem_kv_sample(self, mst):
        P, S = self.P, self.S
        for li in range(2):
            self.load_featmajor(lambda s, n: self.inp["c_mk"][li, 128 * s:128 * s + n, :], MEM_T, 512,
                                lambda c, rows, s, n: (mst["K"].ap[:, li * 4 + c, 128 * s:128 * s + n], mst["K"]))
            S.dma("pool", mst["V"].ap[:, li * 2:li * 2 + 2, :], self.inp["c_mv"][li].rearrange("(s p) c -> p s c", p=128), writes=[mst["V"]])

    def issue_xload(self, x_src_fn, ntok):
        nsub = -(-ntok // 128)
        stg = self.P.alloc([nsub, D], F32, "xstg")
        for s in range(nsub):
            n = min(128, ntok - 128 * s)
            self.S.dma("sp", stg.ap[0:n, s, :], x_src_fn(s, n), writes=[stg.c(s)])
        return stg

    def run_tiles(self, ctxs, prefetch=None, deferred=None, defer_store=False):
        P = self.P
        for c in ctxs:
            ntok, xstg = c["ntok"], c["xstg"]
            x = P.alloc([8, ntok], F32, "x")
            nsub = -(-ntok // 128)
            for ch in range(8):
                ps = self.next_ps()
                for s in range(nsub):
                    n = min(128, ntok - 128 * s)
                    self.tr(ps, ps.t[:, 128 * s:128 * s + n], xstg.ap[0:n, s, 128 * ch:128 * ch + 128], n, reads=[xstg.c(s)])
                self.copy(self.alt(), x.ap[:, ch, :], ps.t[:, 0:ntok], reads=[ps], writes=[x.c(ch)])
                self.xstat_chunk(x, ch, ntok)
            P.release(xstg)
            c["x"] = x
        blocks = [(c["x"], c["ntok"]) for c in ctxs]
        a = lambda c: (c["x"], c["seq"], c["st"], c["pos0"], c["pos_rel"], c["ntok"], c["outs"])
        self.stage("xload")
        self.ffn(blocks, 0, 1, mid=deferred)
        self.stage("ffn1")
        for c in ctxs:
            self.even_mixer(*a(c))
        self.stage("even")
        for c in ctxs:
            if not c["mst"]:
                c["mst"].update({"K": P.alloc([8, MEM_T], BF16, "memK"), "V": P.alloc([4, 512], BF16, "memV")})
                self.mem_kv_sample(c["mst"])
            self.mem_attend(c["x"], 0, c["mst"], c["ntok"])
        self.stage("mem0")
        self.ffn(blocks, 0, 2)
        self.stage("l0")
        self.ffn(blocks, 1, 1)
        for c in ctxs:
            self.odd_mixer(*a(c))
        self.stage("odd")
        for c in ctxs:
            self.mem_attend(c["x"], 1, c["mst"], c["ntok"])
        nxt = prefetch() if prefetch is not None else None
        self.ffn(blocks, 1, 2)

        def store():
            for c in ctxs:
                self.store_tokmajor([c["x"].c(k) for k in range(8)], [128] * 8, c["ntok"], c["outs"]["y"])
                P.release(c["x"])
        if defer_store:
            return nxt, store
        store()
        return nxt, None

    def setup(self):
        nc, S, P = self.nc, self.S, self.P
        inp = self.inp
        self._alt = 0
        self.xstat = {}
        self.ps_rr = 0
        self.w_rr = 0
        self.cmf = P.alloc([8, 128], F32, "cmf")
        S.dma("sp", self.cmf.ap, inp["cmat"].rearrange("m p c -> p m c"), writes=[self.cmf])
        self.cmb = P.alloc([8, 128], BF16, "cmb")
        self.cm = self.cmb
        S.dma("pool", self.cmb.ap, inp["cmat"].rearrange("m p c -> p m c"), writes=[self.cmb])
        self.ident = View(self.cmf.ap[:, 6, :], self.cmf.bufs)
        self.identb = View(self.cmb.ap[:, 6, :], self.cmb.bufs)
        self.sel0 = View(self.cmf.ap[:, 7, :], self.cmf.bufs)
        self.maskt = P.alloc([2, 512], BF16, "maskt")
        S.dma("pool", self.maskt.ap, inp["cmask"].rearrange("m p c -> p m c"), writes=[self.maskt])
        self.zerob = P.alloc([128], BF16, "zerob")
        S.op("dve", lambda: nc.vector.memset(self.zerob.ap, 0.0), writes=[self.zerob])
        self.tix = P.alloc([512], F32, "tix")
        S.dma("sp", self.tix.ap, inp["ctix"], writes=[self.tix])
        self.onesf = P.alloc([512], F32, "onesf")
        S.op("dve", lambda: nc.vector.memset(self.onesf.ap, 1.0), writes=[self.onesf])
        self.ropec = P.alloc([2], F32, "ropec")
        S.dma("sp", self.ropec.ap, inp["crope"], writes=[self.ropec])
        self.epsc = P.alloc([1], F32, "epsc")
        S.op("dve", lambda: nc.vector.memset(self.epsc.ap, EPS), writes=[self.epsc])
        self.npic = P.alloc([1], F32, "npic")
        S.op("dve", lambda: nc.vector.memset(self.npic.ap, -math.pi), writes=[self.npic])
        self.onec = P.alloc([1], F32, "onec")
        S.op("dve", lambda: nc.vector.memset(self.onec.ap, 1.0), writes=[self.onec])
        cols = {}
        ncol = 0

        def take(n):
            nonlocal ncol
            c0 = ncol
            ncol += n
            return c0
        for nm in ("norm_ffn1", "norm_mix", "norm_mem", "norm_mem_src", "norm_ffn2"):
            cols[nm] = [take(8), take(8)]
        cols["ev_g_qlat"] = take(2)
        cols["ev_g_kvlat"] = take(1)
        cols["mem_g_q"] = [take(1), take(1)]
        cols["mem_g_k"] = [take(1), take(1)]
        cols["ev_g_q"] = take(1)
        cols["ev_g_k"] = take(1)
        cols["od_g_q"] = take(1)
        cols["od_g_k"] = take(1)
        cols["conv_w"] = take(16)
        cols["conv_b"] = take(4)
        cols["gb_r"] = take(4)
        cols["gb_i"] = take(4)
        cols["nsp8"] = take(4)
        cols["nbf"] = take(1)
        self.gcol = cols
        self.gv = P.alloc([ncol], F32, "gv")
        gv = self.gv
        S.op("dve", lambda: nc.vector.memset(gv.ap, 0.0), writes=[gv])
        nq = {"allow_slow_non_contiguous": True}

        def col_load(c0, src_1d, n):
            S.dma("sp", gv.ap[:, c0:c0 + n], src_1d.rearrange("(c p) -> p c", p=128), writes=[gv], **nq)

        def rows_load(r0, r1, c0, src_1d):
            S.dma("sp", gv.ap[r0:r1, c0:c0 + 1], src_1d.rearrange("(p o) -> p o", o=1), writes=[gv], **nq)
        for nm in ("norm_ffn1", "norm_mix", "norm_mem", "norm_mem_src", "norm_ffn2"):
            for li in range(2):
                col_load(cols[nm][li], inp[nm][li], 8)
        col_load(cols["ev_g_qlat"], inp["ev_g_qlat"][0], 2)
        col_load(cols["ev_g_kvlat"], inp["ev_g_kvlat"][0], 1)
        for li in range(2):
            col_load(cols["mem_g_q"][li], inp["mem_g_q"][li], 1)
            col_load(cols["mem_g_k"][li], inp["mem_g_k"][li], 1)
        rows_load(0, 96, cols["ev_g_q"], inp["ev_g_q"][0])
        rows_load(0, 96, cols["ev_g_k"], inp["ev_g_k"][0])
        for half in range(2):
            rows_load(64 * half, 64 * half + 64, cols["od_g_q"], inp["od_g_q"][0])
            rows_load(64 * half, 64 * half + 64, cols["od_g_k"], inp["od_g_k"][0])
        for j in range(4):
            col_load(cols["conv_w"] + 4 * j, inp["ev_conv_w"][0, j], 4)
        col_load(cols["conv_b"], inp["ev_conv_b"][0], 4)
        for c in range(4):
            for half in range(2):
                rows_load(64 * half, 64 * half + 64, cols["gb_r"] + c, inp["ev_gate_b"][0, 2 * c + half, 0:64])
                rows_load(64 * half, 64 * half + 64, cols["gb_i"] + c, inp["ev_gate_b"][0, 2 * c + half, 64:128])
        col_load(cols["nsp8"], inp["ev_lambda"][0], 4)
        rows_load(0, 16, cols["nbf"], inp["od_b_f"][0])
        c0 = cols["nsp8"]
        self.act(gv.ap[:, c0:c0 + 4], gv.ap[:, c0:c0 + 4], AF.Exp, reads=[gv], writes=[gv], scale=-1.0)
        self.act(gv.ap[:, c0:c0 + 4], gv.ap[:, c0:c0 + 4], AF.Ln, reads=[gv], writes=[gv], bias=self.onec.ap[:, 0:1])
        self.ts(gv.ap[:, c0:c0 + 4], gv.ap[:, c0:c0 + 4], -8.0, None, ALU.mult, None, reads=[gv], writes=[gv])
        c0 = cols["nbf"]
        self.ts(gv.ap[0:16, c0:c0 + 1], gv.ap[0:16, c0:c0 + 1], -1.0, None, ALU.mult, None, reads=[gv], writes=[gv])
        self.wuq = P.alloc([2, 768], BF16, "wuq")
        self.wuqs = P.alloc([2, 768], BF16, "wuqs")
        wuq_src = inp["ev_w_uq"][0].rearrange("(k p) c -> p k c", p=128)
        S.dma("pool", self.wuq.ap, wuq_src, writes=[self.wuq])
        src4 = inp["ev_w_uq"][0].rearrange("(k p) (h d) -> p k h d", p=128, d=DQK)
        dst4 = self.wuqs.ap.rearrange("p k (h d) -> p k h d", d=DQK)
        for k in range(2):
            S.dma("pool", dst4[:, k, :, 0:64], src4[:, k, :, 0:64], writes=[self.wuqs])
            S.dma("pool", dst4[:, k, :, 64:80], src4[:, k, :, 80:96], writes=[self.wuqs])
            S.dma("pool", dst4[:, k, :, 80:96], src4[:, k, :, 64:80], writes=[self.wuqs])
        self.wukv = P.alloc([1024], BF16, "wukv")
        S.dma("pool", self.wukv.ap, inp["ev_w_ukv"][0], writes=[self.wukv])
        self.gw = P.alloc([8, 128], BF16, "gw")
        S.op("dve", lambda: nc.vector.memset(self.gw.ap, 0.0), writes=[self.gw])
        for c in range(4):
            for half in range(2):
                n = 2 * c + half
                S.dma("pool", self.gw.ap[64 * half:64 * half + 64, c, 64 * half:64 * half + 64], inp["ev_gate_w"][0, n, :, 0:64], writes=[self.gw])
                S.dma("pool", self.gw.ap[64 * half:64 * half + 64, 4 + c, 64 * half:64 * half + 64], inp["ev_gate_w"][0, n, :, 64:128], writes=[self.gw])
        self.wslots = [P.alloc([4096], BF16, f"wslot{i}") for i in range(self.cfg.get("NWSLOT", 6))]
        nktmax = -(-max(self.SEQ, self.PAST + self.DEC) // 128)
        self.nbt = P.alloc([nktmax, 16], F32, "nb")

    def new_seq_state(self, kind, with_mem=True):
        nc, S, P = self.nc, self.S, self.P
        nktmax = -(-max(self.SEQ, self.PAST + self.DEC) // 128)
        st = {"kind": kind}
        st["U"] = P.alloc([4, (self.TT if kind == "prompt" else self.DEC) + 3], F32, "U")
        st["hcarry"] = P.alloc([4], F32, "hcarry")
        st["ccarry"] = P.alloc([1], F32, "ccarry")
        st["cK"] = P.alloc([nktmax, 16], F32, "cK")
        S.op("dve", lambda: nc.vector.memset(st["U"].ap, 0.0), writes=[st["U"]])
        S.op("dve", lambda: nc.vector.memset(st["hcarry"].ap, 0.0), writes=[st["hcarry"]])
        S.op("dve", lambda: nc.vector.memset(st["ccarry"].ap, 0.0), writes=[st["ccarry"]])
        S.op("dve", lambda: nc.vector.memset(st["cK"].ap, 0.0), writes=[st["cK"]])
        mst = {"K": P.alloc([8, MEM_T], BF16, "memK"), "V": P.alloc([4, 512], BF16, "memV")} if with_mem else {}
        return st, mst

    def free_seq_state(self, st, mst):
        self.P.release(st["U"], st["hcarry"], st["ccarry"], st["cK"], mst["K"], mst["V"])

    def build(self):
        self.nc = nc = bass.Bass("TRN2", target_bir_lowering=False)
        self.declare()
        nq = {"allow_slow_non_contiguous": True}
        with contextlib.ExitStack() as es:
            self.S = S = Sched(nc, es)
            self.P = P = Pool(nc, es, self.cfg.get("NPAGES", 206))
            pst = [es.enter_context(nc.psum_tensor(f"ps{i}", [128, 512], F32)) for i in range(8)]
            self.psg = [PS(pst[i], f"ps{i}") for i in range(6)]
            self.pso = [PS(pst[6], "pso0"), PS(pst[7], "pso1")]
            try:
                self.setup()
                self.stage("setup")
                self.body()
            except _Stop:
                pass
            S.finish()
            self.stats = dict(ops=S.n_ops, waits=S.n_waits, ticks=dict(S.tick), peak_pages=P.peak)
        return nc

    def body(self):
        if True:
            nc, S, P = self.nc, self.S, self.P
            nq = {"allow_slow_non_contiguous": True}
            TT = self.TT
            out = self.out
            for si in range(self.NSEQ):
                st, mst = self.new_seq_state("prompt")
                self.mem_kv_prompt(si, mst)
                self.stage("memkv")
                ntile = self.SEQ // TT
                xsrc = lambda p0: (lambda s, n: self.inp["xp"][si, p0 + 128 * s:p0 + 128 * s + n, :])
                xstg = self.issue_xload(xsrc(0), TT)
                dstore = None
                for ti in range(ntile):
                    p0 = ti * TT
                    outs = {
                        "y": lambda s, n, p0=p0: out["o_yp"][si, p0 + 128 * s:p0 + 128 * s + n, :],
                        "lat": lambda s, n, p0=p0: out["o_plat"][si, p0 + 128 * s:p0 + 128 * s + n, :],
                        "kr": lambda s, n, p0=p0: out["o_pkr"][si, p0 + 128 * s:p0 + 128 * s + n, :],
                        "fk": lambda s, n, p0=p0: out["o_pfk"][si, p0 + 128 * s:p0 + 128 * s + n, :],
                        "fv": lambda s, n, p0=p0: out["o_pfv"][si, p0 + 128 * s:p0 + 128 * s + n, :],
                        "fl": lambda s, n, p0=p0: out["o_pfl"][si, p0 + 128 * s:p0 + 128 * s + n, :],
                    }
                    pf = (lambda p1=p0 + TT: self.issue_xload(xsrc(p1), TT)) if ti + 1 < ntile else None
                    ctxs = [dict(seq=si, st=st, mst=mst, xstg=xstg, pos0=p0, pos_rel=p0, ntok=TT, outs=outs)]
                    last = (si == self.NSEQ - 1 and ti == ntile - 1 and self.DEC > 0)
                    if last:
                        sctx = self.sample_prepare()
                        ctxs.append(sctx)
                    xstg, dstore = self.run_tiles(ctxs, prefetch=pf, deferred=(dstore if ti > 0 else None), defer_store=(ti + 1 < ntile))
                    if last:
                        self.sample_finish(sctx)
                S.dma("sp", out["o_ph"][si].rearrange("(c p) -> p c", p=128), st["hcarry"].ap, reads=[st["hcarry"]], **nq)
                for c in range(4):
                    S.dma("sp", out["o_pconv"][si][:, 128 * c:128 * c + 128].rearrange("j p -> p j"), st["U"].ap[:, c, 0:3], reads=[st["U"]], **nq)
                self.free_seq_state(st, mst)

    def sample_prepare(self):
        nc, S, P = self.nc, self.S, self.P
        inp, out = self.inp, self.out
        nq = {"allow_slow_non_contiguous": True}
        seq = self.NSEQ
        PAST, DEC = self.PAST, self.DEC
        st, mst = self.new_seq_state("sample", with_mem=False)
        S.dma("sp", st["hcarry"].ap, inp["s_h"].rearrange("(c p) -> p c", p=128), writes=[st["hcarry"]], **nq)
        for c in range(4):
            S.dma("sp", st["U"].ap[:, c, 0:3], inp["s_conv"][:, 128 * c:128 * c + 128].rearrange("j p -> p j"), writes=[st["U"]], **nq)
        sc = self.scr[seq]
        CH = 512
        for p0 in range(0, PAST, CH):
            n = min(CH, PAST - p0)
            latb = P.alloc([n], BF16, "platb")
            kr = P.alloc([n], F32, "pkr")
            self.load_featmajor(lambda s, m: inp["c_lat"][p0 + 128 * s:p0 + 128 * s + m, :], n, KVL,
                                lambda c, rows, s, m: (latb.ap[:, 128 * s:128 * s + m], latb.v()))
            self.load_featmajor(lambda s, m: inp["c_kr"][p0 + 128 * s:p0 + 128 * s + m, :], n, ROPE,
                                lambda c, rows, s, m: (kr.ap[64:96, 128 * s:128 * s + m], kr.v()))
            self.mla_kv_expand(seq, latb, kr, p0, n)
            P.release(latb, kr)
            kst = P.alloc([8, n], BF16, "pfk")
            self.load_featmajor(lambda s, m: inp["c_fk"][p0 + 128 * s:p0 + 128 * s + m, :], n, 1024,
                                lambda c, rows, s, m: (kst.ap[:, c, 128 * s:128 * s + m], kst.c(c)))
            S.dma("sp", sc["fk"][:, :, p0:p0 + n].rearrange("c r t -> r c t"), kst.ap, reads=[kst], writes=[self.scr_buf(seq, "fk", p0)])
            P.release(kst)
            nsub = n // 128
            vst = P.alloc([nsub, NH_FOX * 65], BF16, "pfv")
            S.op("dve", lambda: nc.vector.memset(vst.ap, 1.0), writes=[vst])
            for s in range(nsub):
                stg = P.alloc([1024], F32, "pvstg")
                S.dma("sp", stg.ap, inp["c_fv"][p0 + 128 * s:p0 + 128 * s + 128, :], writes=[stg])
                self.copy(self.alt(), vst.ap[:, s, :].rearrange("p (h c) -> p h c", h=NH_FOX)[:, :, 0:64], stg.ap.rearrange("p (h c) -> p h c", h=NH_FOX),
                          reads=[stg], writes=[vst])
                P.release(stg)
            S.dma("sp", sc["fv"][p0 // 128:p0 // 128 + nsub].rearrange("k p c -> p k c"), vst.ap, reads=[vst], writes=[self.scr_buf(seq, "fv", p0)])
            P.release(vst)
            lf = P.alloc([n], F32, "plf")
            self.load_featmajor(lambda s, m: inp["c_fl"][p0 + 128 * s:p0 + 128 * s + m, :], n, NH_FOX,
                                lambda c, rows, s, m: (lf.ap[0:16, 128 * s:128 * s + m], lf.v()))
            self.fox_cumsum(st, lf, p0, n)
            P.release(lf)
        outs = {
            "y": lambda s, n: out["o_ys"][0:n, :],
            "lat": lambda s, n: out["o_slat"][0:n, :],
            "kr": lambda s, n: out["o_skr"][0:n, :],
            "fk": lambda s, n: out["o_sfk"][0:n, :],
            "fv": lambda s, n: out["o_sfv"][0:n, :],
            "fl": lambda s, n: out["o_sfl"][0:n, :],
        }
        xstg = self.issue_xload(lambda s, n: inp["xs"][0:n, :], DEC)
        return dict(seq=seq, st=st, mst=mst, xstg=xstg, pos0=PAST, pos_rel=PAST, ntok=DEC, outs=outs)

    def sample_finish(self, ctx):
        nc, S, P = self.nc, self.S, self.P
        out = self.out
        nq = {"allow_slow_non_contiguous": True}
        st, mst = ctx["st"], ctx["mst"]
        S.dma("sp", out["o_sh"].rearrange("(c p) -> p c", p=128), st["hcarry"].ap, reads=[st["hcarry"]], **nq)
        for c in range(4):
            S.dma("sp", out["o_sconv"][:, 128 * c:128 * c + 128].rearrange("j p -> p j"), st["U"].ap[:, c, 0:3], reads=[st["U"]], **nq)
        self.free_seq_state(st, mst)


def make_consts(TT):
    cm = np.zeros((8, 128, 128), np.float32)
    cm[0] = 1.0 / 1024
    cm[1] = 1.0 / 256
    cm[2] = 1.0 / 128
    cm[3, :96, :96] = 1.0 / 96
    cm[4, :64, :64] = 1.0 / 64
    cm[4, 64:, 64:] = 1.0 / 64
    cm[5] = 1.0
    cm[6] = np.eye(128, dtype=np.float32)
    cm[7, 0, :] = 1.0
    mask = np.zeros((2, 128, 512), np.float32)
    kk = np.arange(128)[:, None]
    qq = np.arange(512)[None, :]
    mask[0] = np.where(kk <= qq, 0.0, NEG)
    mask[1] = np.where(kk // 64 <= qq // 64, 0.0, NEG)
    tix = np.broadcast_to(np.arange(512, dtype=np.float32)[None, :], (128, 512)).copy()
    rope = np.zeros((128, 2), np.float32)
    half = ROPE // 2
    invf = (10000.0 ** (-np.arange(half, dtype=np.float32) / half)).astype(np.float32)
    rope[64:80, 0] = invf
    rope[80:96, 0] = invf
    rope[64:80, 1] = -1.0
    rope[80:96, 1] = 1.0
    return {"cmat": cm, "cmask": mask, "ctix": tix, "crope": rope}


FULL_CFG = dict(SEQ=4096, NSEQ=2, TT=512, DEC=16, PAST=2048)
_W_NAMES = ["norm_ffn1", "ffn1_w_in", "ffn1_w_out", "norm_mix", "norm_mem", "norm_mem_src", "mem_w_q", "mem_w_kv", "mem_w_o",
            "mem_g_q", "mem_g_k", "norm_ffn2", "ffn2_w_in", "ffn2_w_out", "ev_w_in", "ev_g_qlat", "ev_g_kvlat", "ev_w_uq", "ev_w_ukv",
            "ev_g_q", "ev_g_k", "ev_conv_w", "ev_conv_b", "ev_gate_w", "ev_gate_b", "ev_lambda", "ev_w_out", "od_w_in", "od_b_f",
            "od_g_q", "od_g_k", "od_w_out"]


def run(cfg, inputs, n_cores=8):
    f = lambda a: np.ascontiguousarray(np.asarray(a, dtype=np.float32))
    mk = MK(cfg)
    nc = mk.build()
    NSEQ, DEC = cfg["NSEQ"], cfg["DEC"]
    consts = make_consts(cfg["TT"])
    shared = {n: f(inputs[n]) for n in _W_NAMES}
    shared.update(consts)
    in_maps = []
    for i in range(n_cores):
        m = dict(shared)
        m["xp"] = f(inputs["x_prompt"][NSEQ * i:NSEQ * (i + 1)])
        m["memp"] = f(inputs["mem_prompt"][NSEQ * i:NSEQ * (i + 1)])
        m["xs"] = f(inputs["x_sample"][i])
        m["c_lat"] = f(inputs["cache_mla_latent"][0, i])
        m["c_kr"] = f(inputs["cache_mla_krope"][0, i])
        m["s_h"] = f(inputs["state_lru_h"][0, i])
        m["s_conv"] = f(inputs["state_lru_conv"][0, i])
        m["c_fk"] = f(np.asarray(inputs["cache_fox_k"])[0, i].reshape(-1, 1024))
        m["c_fv"] = f(np.asarray(inputs["cache_fox_v"])[0, i].reshape(-1, 1024))
        m["c_fl"] = f(inputs["cache_fox_logf"][0, i])
        m["c_mk"] = f(np.asarray(inputs["cache_mem_k"])[:, i].reshape(2, MEM_T, 512))
        m["c_mv"] = f(np.asarray(inputs["cache_mem_v"])[:, i].reshape(2, MEM_T, 512))
        in_maps.append(m)
    res = run_bass_kernel_spmd(nc, in_maps, core_ids=list(range(n_cores)))
    R = res.results
    cat = lambda k: np.concatenate([r[k] for r in R], axis=0)
    stk = lambda k: np.stack([r[k] for r in R], axis=0)
    B = NSEQ * n_cores
    SEQ = cfg["SEQ"]
    outs = (
        cat("o_yp"),
        stk("o_ys"),
        cat("o_plat")[None],
        cat("o_pkr")[None],
        cat("o_ph")[None],
        cat("o_pconv")[None],
        cat("o_pfk").reshape(1, B, SEQ, NH_FOX, HD_FOX),
        cat("o_pfv").reshape(1, B, SEQ, NH_FOX, HD_FOX),
        cat("o_pfl")[None],
        np.concatenate([r["o_pmk"] for r in R], axis=1).reshape(2, B, MEM_T, MEM_H, MEM_HD),
        np.concatenate([r["o_pmv"] for r in R], axis=1).reshape(2, B, MEM_T, MEM_H, MEM_HD),
        stk("o_slat")[None],
        stk("o_skr")[None],
        stk("o_sh")[None],
        stk("o_sconv")[None],
        stk("o_sfk").reshape(1, n_cores, DEC, NH_FOX, HD_FOX),
        stk("o_sfv").reshape(1, n_cores, DEC, NH_FOX, HD_FOX),
        stk("o_sfl")[None],
    )
    return tuple(np.ascontiguousarray(o, dtype=np.float32) for o in outs), mk


def kernel(**inputs):
    outs, _ = run(FULL_CFG, inputs)
    return outs
```
